# Optimizing a Trainium2 kernel written in Bass

```python
import math
import jax, jax.numpy as jnp
from jax import lax
import numpy as np

D_MODEL = 1024
BATCH = 8
SEQ = 8192
DEPTH = 4

GRID_W = 64
CTX_LEN = 256
HEAD_DIM = 64
N_GROUPS = 4
GROUP_W = D_MODEL // N_GROUPS
HEADS_A = GROUP_W // HEAD_DIM
KV_A = HEADS_A // 2
HEADS_B = GROUP_W // HEAD_DIM
KV_B = HEADS_B // 2
D_HYENA = GROUP_W
D_FNET = GROUP_W
FNET_GROUP_DIM = 64
FNET_GROUPS = D_FNET // FNET_GROUP_DIM
WINDOW = 128
BLOCK = 128
ROPE_THETA = 10000.0
HYENA_BANDS = 8
HYENA_EMB = 2 * HYENA_BANDS + 1
HYENA_HIDDEN = 64
HYENA_FAST_DECAY = 0.3
HYENA_SLOW_DECAY = 1.5
HYENA_TARGET = 1e-2
D_FF = ((8 * D_MODEL // 3 + 63) // 64) * 64
PROJ_SIZES = (HEADS_A * HEAD_DIM, KV_A * HEAD_DIM, KV_A * HEAD_DIM,
              HEADS_B * HEAD_DIM, KV_B * HEAD_DIM, KV_B * HEAD_DIM,
              3 * D_HYENA, D_FNET)
D_PROJ = sum(PROJ_SIZES)
DEEPNORM_ALPHA = (2 * DEPTH) ** 0.25
DEEPNORM_BETA = (8 * DEPTH) ** -0.25
LN_EPS = 1e-6
NEG_INF = -1e30

kernel_name = "hybrid_dit_parallel_groups"


def layer_norm(x, g=None, b=None):
    xf = x.astype(jnp.float32)
    mu = jnp.mean(xf, axis=-1, keepdims=True)
    var = jnp.mean(jnp.square(xf - mu), axis=-1, keepdims=True)
    y = (xf - mu) * lax.rsqrt(var + LN_EPS)
    if g is not None:
        y = y * g.astype(jnp.float32) + b.astype(jnp.float32)
    return y.astype(x.dtype)


def rms_norm(x, g):
    xf = x.astype(jnp.float32)
    y = xf * lax.rsqrt(jnp.mean(jnp.square(xf), axis=-1, keepdims=True) + LN_EPS)
    return (y * g.astype(jnp.float32)).astype(x.dtype)


def modulate(h, shift, scale):
    return h * (1.0 + scale) + shift


def axial_rope_tables(L):
    rows = L // GRID_W
    row = jnp.broadcast_to(jnp.arange(rows)[:, None], (rows, GRID_W)).reshape(-1).astype(jnp.float32)
    col = jnp.broadcast_to(jnp.arange(GRID_W)[None, :], (rows, GRID_W)).reshape(-1).astype(jnp.float32)
    half = HEAD_DIM // 2
    inv = ROPE_THETA ** (-jnp.arange(0, half, 2, dtype=jnp.float32) / half)
    ang = jnp.concatenate([row[:, None] * inv, col[:, None] * inv], axis=-1)
    return jnp.cos(ang), jnp.sin(ang)


def apply_axial_rope(x, cos, sin):
    B, L, H, _ = x.shape
    q = HEAD_DIM // 4
    xs = x.astype(jnp.float32).reshape(B, L, H, 2, 2, q)
    c = cos.reshape(L, 1, 2, q)
    s = sin.reshape(L, 1, 2, q)
    x1, x2 = xs[..., 0, :], xs[..., 1, :]
    out = jnp.stack([x1 * c - x2 * s, x2 * c + x1 * s], axis=-2)
    return out.reshape(B, L, H, HEAD_DIM).astype(x.dtype)


def split_projection(z):
    B, L, _ = z.shape
    idx = np.cumsum(PROJ_SIZES)[:-1].tolist()
    qa, ka, va, qb, kb, vb, zh, zf = jnp.split(z, idx, axis=-1)
    hd = lambda t, n: t.reshape(B, L, n, HEAD_DIM)
    return (hd(qa, HEADS_A), hd(ka, KV_A), hd(va, KV_A),
            hd(qb, HEADS_B), hd(kb, KV_B), hd(vb, KV_B), zh, zf)


def dense_attention(q, k, v, sink):
    B, C, H, _ = q.shape
    KV = k.shape[2]
    G = H // KV
    qg = q.reshape(B, C, KV, G, HEAD_DIM)
    s = jnp.einsum('bqhgd,bkhd->bhgqk', qg, k).astype(jnp.float32) * (HEAD_DIM ** -0.5)
    if sink is not None:
        s_sink = jnp.broadcast_to(sink.astype(jnp.float32).reshape(1, KV, G, 1, 1), s.shape[:-1] + (1,))
        s = jnp.concatenate([s, s_sink], axis=-1)
    p = jax.nn.softmax(s, axis=-1)[..., :k.shape[1]].astype(v.dtype)
    o = jnp.einsum('bhgqk,bkhd->bqhgd', p, v)
    return o.reshape(B, C, H * HEAD_DIM)


def banded_window_attention(q, k, v, kc, vc, sink):
    B, L, H, _ = q.shape
    KV = k.shape[2]
    G = H // KV
    C = kc.shape[1]
    nb = L // BLOCK
    qb = q.reshape(B, nb, BLOCK, KV, G, HEAD_DIM)

    def band(t):
        tb = t.reshape(B, nb, BLOCK, KV, HEAD_DIM)
        tp = jnp.pad(tb, ((0, 0), (1, 1), (0, 0), (0, 0), (0, 0)))
        return jnp.concatenate([tp[:, :-2], tp[:, 1:-1], tp[:, 2:]], axis=2)

    kb, vb = band(k), band(v)
    scale = HEAD_DIM ** -0.5
    s_loc = jnp.einsum('bnqhgd,bnkhd->bnhgqk', qb, kb).astype(jnp.float32) * scale
    qi = jnp.arange(BLOCK)[:, None]
    kj = jnp.arange(3 * BLOCK)[None, :] - BLOCK
    in_window = jnp.abs(kj - qi) <= WINDOW
    kabs = jnp.arange(nb)[:, None] * BLOCK + kj
    in_range = (kabs >= 0) & (kabs < L)
    mask = in_window[None] & in_range[:, None, :]
    s_loc = jnp.where(mask[None, :, None, None], s_loc, NEG_INF)
    s_ctx = jnp.einsum('bnqhgd,bchd->bnhgqc', qb, kc).astype(jnp.float32) * scale
    s_sink = jnp.broadcast_to(sink.astype(jnp.float32).reshape(1, 1, KV, G, 1, 1), s_loc.shape[:-1] + (1,))
    p = jax.nn.softmax(jnp.concatenate([s_loc, s_ctx, s_sink], axis=-1), axis=-1).astype(v.dtype)
    o = (jnp.einsum('bnhgqk,bnkhd->bnqhgd', p[..., :3 * BLOCK], vb)
         + jnp.einsum('bnhgqc,bchd->bnqhgd', p[..., 3 * BLOCK:3 * BLOCK + C], vc))
    return o.reshape(B, L, H * HEAD_DIM)


def blockwise_full_attention(q, k, v, kc, vc):
    B, L, H, _ = q.shape
    KV = k.shape[2]
    G = H // KV
    nb = L // BLOCK
    qb = q.reshape(B, nb, BLOCK, KV, G, HEAD_DIM).swapaxes(0, 1)
    k_all = jnp.concatenate([k, kc], axis=1)
    v_all = jnp.concatenate([v, vc], axis=1)

    def one_block(q_blk):
        s = jnp.einsum('bqhgd,bkhd->bhgqk', q_blk, k_all).astype(jnp.float32) * (HEAD_DIM ** -0.5)
        p = jax.nn.softmax(s, axis=-1).astype(v_all.dtype)
        return jnp.einsum('bhgqk,bkhd->bqhgd', p, v_all)

    o = lax.map(one_block, qb)
    return o.swapaxes(0, 1).reshape(B, L, H * HEAD_DIM)


def dwconv3(u, w, b):
    up = jnp.pad(u, ((0, 0), (1, 1), (0, 0)))
    return up[:, :-2] * w[0] + up[:, 1:-1] * w[1] + up[:, 2:] * w[2] + b


def hyena_filters(L, w1, b1, freq, w2, b2, w3):
    f32 = jnp.float32
    t = jnp.arange(L, dtype=f32)
    t_norm = t / max(L - 1, 1)
    bands = jnp.linspace(1e-4, HYENA_BANDS - 1, HYENA_BANDS, dtype=f32)
    ang = 2.0 * math.pi * t[:, None] * bands[None, :] / L
    z = jnp.concatenate([t_norm[:, None], jnp.cos(ang), -jnp.sin(ang)], axis=-1)
    fr = freq.astype(f32)
    h = jnp.sin(fr * (z @ w1.astype(f32) + b1.astype(f32)))
    h = jnp.sin(fr * (h @ w2.astype(f32) + b2.astype(f32)))
    h = (h @ w3.astype(f32)).reshape(L, 2, D_HYENA)
    min_decay = math.log(HYENA_TARGET) / HYENA_SLOW_DECAY
    max_decay = math.log(HYENA_TARGET) / HYENA_FAST_DECAY
    deltas = jnp.abs(jnp.linspace(min_decay, max_decay, D_HYENA, dtype=f32))
    decay = jnp.exp(-t_norm[:, None] * deltas[None, :])
    h = h * decay[:, None, :]
    return h[:, 0], h[:, 1]


def bidir_long_conv(u, h_fwd, h_bwd):
    B, L, D = u.shape
    k = jnp.concatenate([h_fwd.at[0].add(h_bwd[0]), jnp.zeros((1, D), jnp.float32), h_bwd[1:][::-1]], axis=0)
    U = jnp.fft.rfft(u.astype(jnp.float32), n=2 * L, axis=1)
    K = jnp.fft.rfft(k, axis=0)
    y = jnp.fft.irfft(U * K[None], n=2 * L, axis=1)[:, :L]
    return y.astype(u.dtype)


def hyena_mixer(z, conv_w, conv_b, f_w1, f_b1, f_freq, f_w2, f_b2, f_w3, skip):
    z = dwconv3(z, conv_w, conv_b)
    x0, x1, v = jnp.split(z, 3, axis=-1)
    h_fwd, h_bwd = hyena_filters(z.shape[1], f_w1, f_b1, f_freq, f_w2, f_b2, f_w3)
    v = v * x1
    v = bidir_long_conv(v, h_fwd, h_bwd) + skip * v
    return v * x0


def fnet_mixer(u, w, b):
    B, L, _ = u.shape
    ug = u.astype(jnp.float32).reshape(B, L, FNET_GROUPS, FNET_GROUP_DIM)
    f = jnp.real(jnp.fft.fftn(ug, axes=(1, 3), norm='ortho'))
    return f.reshape(B, L, D_FNET).astype(u.dtype) @ w + b


def merge_groups(oa, ob, oc, od, g):
    o = jnp.stack([oa, ob, oc, od], axis=-2)
    of = o.astype(jnp.float32)
    of = of * lax.rsqrt(jnp.mean(jnp.square(of), axis=-1, keepdims=True) + LN_EPS)
    o = of.reshape(o.shape[:-2] + (N_GROUPS * GROUP_W,)) * g.astype(jnp.float32)
    return o.astype(oa.dtype)


def conv_ffn(h, w_up, b_up, conv_w, conv_b, w_down, b_down):
    u = dwconv3(h @ w_up + b_up, conv_w, conv_b)
    a, g = jnp.split(u, 2, axis=-1)
    return (jax.nn.silu(a) * g) @ w_down + b_down


def setup_inputs(seed: int = 0) -> dict:
    key = jax.random.key(seed)
    ks = iter(jax.random.split(key, 64))
    D = D_MODEL

    def nrm(shape, scale):
        return jax.random.normal(next(ks), shape, jnp.float32) * scale

    return {
        'x': nrm((BATCH, SEQ, D), 1.0),
        'c': nrm((BATCH, D), 1.0),
        'ctx': nrm((BATCH, CTX_LEN, D), 1.0),
        'c_ctx': nrm((D,), 1.0),
        'w_ada': nrm((DEPTH, D, 6 * D), 0.5 * D ** -0.5),
        'b_ada': nrm((DEPTH, 6 * D), 0.02),
        'w_in': nrm((DEPTH, D, D_PROJ), D ** -0.5),
        'sink_a': nrm((DEPTH, HEADS_A), 0.5),
        'q_norm_g': 1.0 + nrm((DEPTH, HEAD_DIM), 0.02),
        'k_norm_g': 1.0 + nrm((DEPTH, HEAD_DIM), 0.02),
        'hy_conv_w': nrm((DEPTH, 3, 3 * D_HYENA), 3 ** -0.5),
        'hy_conv_b': nrm((DEPTH, 3 * D_HYENA), 0.02),
        'hy_f_w1': nrm((DEPTH, HYENA_EMB, HYENA_HIDDEN), HYENA_EMB ** -0.5),
        'hy_f_b1': nrm((DEPTH, HYENA_HIDDEN), 0.02),
        'hy_f_freq': 1.0 + nrm((DEPTH, HYENA_HIDDEN), 0.02),
        'hy_f_w2': nrm((DEPTH, HYENA_HIDDEN, HYENA_HIDDEN), HYENA_HIDDEN ** -0.5),
        'hy_f_b2': nrm((DEPTH, HYENA_HIDDEN), 0.02),
        'hy_f_w3': nrm((DEPTH, HYENA_HIDDEN, 2 * D_HYENA), HYENA_HIDDEN ** -0.5),
        'hy_skip': nrm((DEPTH, D_HYENA), 1.0),
        'fnet_w': nrm((DEPTH, D_FNET, D_FNET), D_FNET ** -0.5),
        'fnet_b': nrm((DEPTH, D_FNET), 0.02),
        'out_norm_g': 1.0 + nrm((DEPTH, D), 0.02),
        'w_out': nrm((DEPTH, D, D), DEEPNORM_BETA * D ** -0.5),
        'b_out': nrm((DEPTH, D), 0.02),
        'ln1_g': 1.0 + nrm((DEPTH, D), 0.02),
        'ln1_b': nrm((DEPTH, D), 0.02),
        'ffn_w_up': nrm((DEPTH, D, 2 * D_FF), D ** -0.5),
        'ffn_b_up': nrm((DEPTH, 2 * D_FF), 0.02),
        'ffn_conv_w': nrm((DEPTH, 3, 2 * D_FF), 3 ** -0.5),
        'ffn_conv_b': nrm((DEPTH, 2 * D_FF), 0.02),
        'ffn_w_down': nrm((DEPTH, D_FF, D), DEEPNORM_BETA * D_FF ** -0.5),
        'ffn_b_down': nrm((DEPTH, D), 0.02),
        'ln2_g': 1.0 + nrm((DEPTH, D), 0.02),
        'ln2_b': nrm((DEPTH, D), 0.02),
    }


def reference(x, c, ctx, c_ctx, w_ada, b_ada, w_in, sink_a, q_norm_g, k_norm_g,
              hy_conv_w, hy_conv_b, hy_f_w1, hy_f_b1, hy_f_freq, hy_f_w2, hy_f_b2, hy_f_w3, hy_skip,
              fnet_w, fnet_b, out_norm_g, w_out, b_out, ln1_g, ln1_b,
              ffn_w_up, ffn_b_up, ffn_conv_w, ffn_conv_b, ffn_w_down, ffn_b_down, ln2_g, ln2_b):
    L = x.shape[1]
    cos, sin = axial_rope_tables(L)
    for l in range(DEPTH):
        mod = jax.nn.silu(c) @ w_ada[l] + b_ada[l]
        mod_c = jax.nn.silu(c_ctx) @ w_ada[l] + b_ada[l]
        sh1, sc1, g1, sh2, sc2, g2 = jnp.split(mod[:, None, :], 6, axis=-1)
        csh1, csc1, cg1, csh2, csc2, cg2 = jnp.split(mod_c, 6, axis=-1)
        hy_p = (hy_conv_w[l], hy_conv_b[l], hy_f_w1[l], hy_f_b1[l], hy_f_freq[l],
                hy_f_w2[l], hy_f_b2[l], hy_f_w3[l], hy_skip[l])
        ffn_p = (ffn_w_up[l], ffn_b_up[l], ffn_conv_w[l], ffn_conv_b[l], ffn_w_down[l], ffn_b_down[l])

        hc = modulate(layer_norm(ctx), csh1, csc1)
        cqa, cka, cva, cqb, ckb, cvb, czh, czf = split_projection(hc @ w_in[l])
        ckb = rms_norm(ckb, k_norm_g[l])

        h = modulate(layer_norm(x), sh1, sc1)
        qa, ka, va, qb, kb, vb, zh, zf = split_projection(h @ w_in[l])
        oa = banded_window_attention(apply_axial_rope(qa, cos, sin), apply_axial_rope(ka, cos, sin),
                                     va, cka, cva, sink_a[l])
        qb = apply_axial_rope(rms_norm(qb, q_norm_g[l]), cos, sin)
        kb = apply_axial_rope(rms_norm(kb, k_norm_g[l]), cos, sin)
        ob = blockwise_full_attention(qb, kb, vb, ckb, cvb)
        oc = hyena_mixer(zh, *hy_p)
        od = fnet_mixer(zf, fnet_w[l], fnet_b[l])
        y = merge_groups(oa, ob, oc, od, out_norm_g[l]) @ w_out[l] + b_out[l]
        x = layer_norm(DEEPNORM_ALPHA * x + g1 * y, ln1_g[l], ln1_b[l])

        h2 = modulate(layer_norm(x), sh2, sc2)
        x = layer_norm(DEEPNORM_ALPHA * x + g2 * conv_ffn(h2, *ffn_p), ln2_g[l], ln2_b[l])

        if l < DEPTH - 1:
            oac = dense_attention(cqa, cka, cva, sink_a[l])
            obc = dense_attention(rms_norm(cqb, q_norm_g[l]), ckb, cvb, None)
            occ = hyena_mixer(czh, *hy_p)
            odc = fnet_mixer(czf, fnet_w[l], fnet_b[l])
            yc = merge_groups(oac, obc, occ, odc, out_norm_g[l]) @ w_out[l] + b_out[l]
            ctx = layer_norm(DEEPNORM_ALPHA * ctx + cg1 * yc, ln1_g[l], ln1_b[l])
            hc2 = modulate(layer_norm(ctx), csh2, csc2)
            ctx = layer_norm(DEEPNORM_ALPHA * ctx + cg2 * conv_ffn(hc2, *ffn_p), ln2_g[l], ln2_b[l])
    return x
```

```python
import math
from contextlib import ExitStack
import numpy as np
import ml_dtypes
import concourse.bass as bass
import concourse.mybir as mybir
from concourse.bass_utils import run_bass_kernel_spmd

F32 = mybir.dt.float32
BF16 = mybir.dt.bfloat16
AF = mybir.ActivationFunctionType
ALU = mybir.AluOpType
AX = mybir.AxisListType

D = 1024
NCH = 8
CTX = 256
GRID_W = 64
HD = 64
DFF = 2752
DPROJ = 2048
LN_EPS = 1e-6
ROPE_THETA = 10000.0
HY_BANDS = 8
HY_EMB = 17
HY_HID = 64


class Sem:
    __slots__ = ("h", "val")

    def __init__(self, h):
        self.h = h
        self.val = 0


class Eng:
    def __init__(self, name, eng, sem):
        self.name = name
        self.eng = eng
        self.sem = sem
        self.known = {}


class Buf:
    __slots__ = ("name", "w", "r", "dsem", "dram", "ssem")

    def __init__(self, name, dsem=None, dram=False):
        self.ssem = None
        self.name = name
        self.w = {}
        self.r = {}
        self.dsem = dsem
        self.dram = dram


class KB:
    def __init__(self, nc, es, n_dma_sems=94):
        self.nc = nc
        self.E = {}
        for name, eng in (("pe", nc.tensor), ("act", nc.scalar), ("dve", nc.vector),
                          ("pool", nc.gpsimd), ("sp", nc.sync)):
            s = Sem(es.enter_context(nc.semaphore("s_" + name)))
            self.E[name] = Eng(name, eng, s)
        self.free_sems = [Sem(es.enter_context(nc.semaphore("d%d" % i))) for i in range(n_dma_sems - 12)]
        self.free_sw = [Sem(es.enter_context(nc.semaphore("w%d" % i))) for i in range(12)]
        self.all_sems = [e.sem for e in self.E.values()] + list(self.free_sems) + list(self.free_sw)
        self.phase_bufs = []
        self.n_inst = 0

    def buf(self, name, dma=False, persistent=False):
        b = Buf(name, None, dram=persistent and dma and name != "modT")
        if dma and not b.dram:
            b.dsem = self.free_sems.pop()
        if not persistent:
            self.phase_bufs.append(b)
        return b

    def _waits(self, E, reads, writes, skip=None, skip_waw=False):
        deps = {}
        for b in reads:
            for s, v in b.w.items():
                if deps.get(s, 0) < v:
                    deps[s] = v
        for b in writes:
            if not skip_waw:
                for s, v in b.w.items():
                    if s is skip:
                        continue
                    if deps.get(s, 0) < v:
                        deps[s] = v
            for s, v in b.r.items():
                if deps.get(s, 0) < v:
                    deps[s] = v
        for s, v in deps.items():
            if E.name == "pe" and s is E.sem:
                continue
            if E.known.get(s, 0) < v:
                E.eng.wait_ge(s.h, v)
                E.known[s] = v
                self.n_inst += 1

    def _record(self, ev, reads, writes):
        s, v = ev
        for b in reads:
            if b.r.get(s, 0) < v:
                b.r[s] = v
        for b in writes:
            b.w = {s: v}
            b.r = {}

    def op(self, en, fn, reads=(), writes=()):
        E = self.E[en]
        self._waits(E, reads, writes)
        ins = fn(E.eng)
        E.sem.val += 1
        ins.then_inc(E.sem.h, 1)
        self.n_inst += 1
        self._record((E.sem, E.sem.val), reads, writes)

    def mm(self, out_ap, pairs, reads=(), writes=(), transpose=False):
        E = self.E["pe"]
        self._waits(E, reads, writes)
        n = len(pairs)
        ins = None
        for i, (l, r) in enumerate(pairs):
            if transpose:
                ins = E.eng.transpose(out_ap, l, r)
            else:
                ins = E.eng.matmul(out_ap, l, r, start=(i == 0), stop=(i == n - 1))
            self.n_inst += 1
        E.sem.val += 1
        ins.then_inc(E.sem.h, 1)
        self._record((E.sem, E.sem.val), reads, writes)

    def mm1(self, out_ap, l, r, start, stop, reads=(), writes=()):
        E = self.E["pe"]
        self._waits(E, reads, writes)
        ins = E.eng.matmul(out_ap, l, r, start=start, stop=stop)
        self.n_inst += 1
        E.sem.val += 1
        ins.then_inc(E.sem.h, 1)
        self._record((E.sem, E.sem.val), reads, writes)

    def dma(self, qn, out_ap, in_ap, src, dst, **kw):
        E = self.E[qn]
        if dst.dram:
            assert src is not None and not src.dram, (dst.name,)
            if src.dsem is None:
                src.dsem = self.free_sems.pop()
            ds = src.dsem
            self._waits(E, [src], [dst], skip_waw=True)
        elif qn == "pool":
            if dst.ssem is None:
                dst.ssem = self.free_sw.pop()
            ds = dst.ssem
            self._waits(E, [src] if src is not None else [], [dst], skip=ds)
        else:
            ds = dst.dsem
            assert ds is not None, dst.name
            self._waits(E, [src] if src is not None else [], [dst], skip=ds)
        ins = E.eng.dma_start(out=out_ap, in_=in_ap, **kw)
        ds.val += 16
        ins.then_inc(ds.h, 16)
        self.n_inst += 1
        if src is not None:
            if src.r.get(ds, 0) < ds.val:
                src.r[ds] = ds.val
        if dst.dram:
            dst.w[ds] = ds.val
            dst.r = {}
        else:
            dst.w = {ds: ds.val}
            dst.r = {}

    def barrier(self):
        for E in self.E.values():
            for s in self.all_sems:
                if s.val > 0 and E.known.get(s, 0) < s.val:
                    if E.name == "pe" and s is E.sem:
                        continue
                    E.eng.wait_ge(s.h, s.val)
                    E.known[s] = s.val
                    self.n_inst += 1

    def end_phase(self):
        self.barrier()
        for b in self.phase_bufs:
            if b.dsem is not None:
                self.free_sems.append(b.dsem)
            if b.ssem is not None:
                self.free_sw.append(b.ssem)
        self.phase_bufs = []


class Prog:
    def __init__(self, L, depth, dbg=()):
        self.L = L
        self.T = L + CTX
        self.depth = depth
        self.dbg = set(dbg)
        self.nc = bass.Bass("TRN2", target_bir_lowering=False)
        self.consts = {}

    def din(self, name, shape, dt=F32):
        return self.nc.dram_tensor(name, list(shape), dt, kind="ExternalInput").ap()

    def dscr(self, name, shape, dt=F32):
        kind = "ExternalOutput" if name in self.dbg else "Internal"
        return self.nc.dram_tensor(name, list(shape), dt, kind=kind).ap()

    def tiles(self):
        out = [(t0, 512, False) for t0 in range(0, self.L, 512)]
        out.append((self.L, CTX, True))
        return out

    def sb(self, es, name, shape, dt):
        self._uid = getattr(self, "_uid", 0) + 1
        return es.enter_context(self.nc.sbuf_tensor("%s_u%d" % (name, self._uid), list(shape), dt))

    def ps(self, es, name, shape, dt=F32):
        self._uid = getattr(self, "_uid", 0) + 1
        return es.enter_context(self.nc.psum_tensor("%s_u%d" % (name, self._uid), list(shape), dt))

    def phase_tin(self, x_in, ctx_in, xT, xT_buf):
        kb, nc = self.kb, self.nc
        with ExitStack() as es:
            ident = self.sb(es, "ti_ident", [128, 128], F32)
            xin = [self.sb(es, "ti_x%d" % i, [128, 4, D], F32) for i in range(2)]
            xo = [self.sb(es, "ti_o%d" % i, [128, NCH, 512], F32) for i in range(2)]
            pst = [self.ps(es, "ti_ps%d" % i, [128, 4, 512]) for i in range(2)]
            b_id = kb.buf("ident", dma=True)
            b_xin = [kb.buf("xin%d" % i, dma=True) for i in range(2)]
            b_xo = [kb.buf("xo%d" % i) for i in range(2)]
            b_ps = [kb.buf("ps%d" % i) for i in range(2)]
            kb.dma("sp", ident[:], self.c_ident_f, None, b_id)
            xT_v = xT.rearrange("(c p) t -> p c t", p=128)
            for it, (t0, n, isc) in enumerate(self.tiles()):
                nb = n // 128
                src = ctx_in if isc else x_in
                s0 = 0 if isc else t0
                xi, bxi = xin[it % 2], b_xin[it % 2]
                kb.dma("sp", xi[:, 0:nb, :], src[s0:s0 + n, :].rearrange("(b p) d -> p b d", p=128), None, bxi)
                xo_t, bxo = xo[it % 2], b_xo[it % 2]
                for half in range(2):
                    pidx = (2 * it + half) % 2
                    p_t, bp = pst[pidx], b_ps[pidx]
                    for c in range(4):
                        for blk in range(nb):
                            kb.mm(p_t[:, c, blk * 128:(blk + 1) * 128],
                                  [(xi[:, blk, (4 * half + c) * 128:(4 * half + c + 1) * 128], ident[:])],
                                  reads=[bxi, b_id], writes=[bp], transpose=True)
                    eng = "act" if half == 0 else "dve"
                    if eng == "act":
                        kb.op("act", lambda e: e.copy(out=xo_t[:, 4 * half:4 * half + 4, 0:n], in_=p_t[:, :, 0:n]),
                              reads=[bp], writes=[bxo])
                    else:
                        kb.op("dve", lambda e: e.tensor_copy(out=xo_t[:, 4 * half:4 * half + 4, 0:n], in_=p_t[:, :, 0:n]),
                              reads=[bp], writes=[bxo])
                kb.dma("sp", xT_v[:, :, t0:t0 + n], xo_t[:, :, 0:n], bxo, xT_buf)
            kb.end_phase()

    def phase_tout(self, xT, xT_buf, out, out_buf):
        kb, nc = self.kb, self.nc
        with ExitStack() as es:
            ident = self.sb(es, "to_ident", [128, 128], F32)
            xin = [self.sb(es, "to_x%d" % i, [128, NCH, 512], F32) for i in range(2)]
            xo = [self.sb(es, "to_o%d" % i, [128, 4, D], F32) for i in range(2)]
            pst = [self.ps(es, "to_ps%d" % i, [128, 4, 512]) for i in range(2)]
            b_id = kb.buf("ident", dma=True)
            b_xin = [kb.buf("xin%d" % i, dma=True) for i in range(2)]
            b_xo = [kb.buf("xo%d" % i) for i in range(2)]
            b_ps = [kb.buf("ps%d" % i) for i in range(2)]
            kb.dma("sp", ident[:], self.c_ident_f, None, b_id)
            xT_v = xT.rearrange("(c p) t -> p c t", p=128)
            for it, (t0, n, isc) in enumerate(self.tiles()):
                if isc:
                    continue
                xi, bxi = xin[it % 2], b_xin[it % 2]
                kb.dma("sp", xi[:, :, 0:n], xT_v[:, :, t0:t0 + n], xT_buf, bxi)
                xo_t, bxo = xo[it % 2], b_xo[it % 2]
                for half in range(2):
                    pidx = (2 * it + half) % 2
                    p_t, bp = pst[pidx], b_ps[pidx]
                    for blk in range(4):
                        for c in range(4):
                            kb.mm(p_t[:, blk, c * 128:(c + 1) * 128],
                                  [(xi[:, 4 * half + c, blk * 128:(blk + 1) * 128], ident[:])],
                                  reads=[bxi, b_id], writes=[bp], transpose=True)
                    if half == 0:
                        kb.op("act", lambda e: e.copy(out=xo_t[:, :, 512 * half:512 * half + 512], in_=p_t[:, :, :]),
                              reads=[bp], writes=[bxo])
                    else:
                        kb.op("dve", lambda e: e.tensor_copy(out=xo_t[:, :, 512 * half:512 * half + 512], in_=p_t[:, :, :]),
                              reads=[bp], writes=[bxo])
                kb.dma("sp", out[t0:t0 + n, :].rearrange("(b p) d -> p b d", p=128), xo_t[:], bxo, out_buf)
            kb.end_phase()

    def load_consts_common(self, es, pfx):
        kb = self.kb
        C = {}
        C["ones_f"] = self.sb(es, pfx + "ones_f", [128, 128], F32)
        C["ones_b"] = self.sb(es, pfx + "ones_b", [128, 128], BF16)
        C["eps"] = self.sb(es, pfx + "eps", [128, 1], F32)
        C["b"] = kb.buf(pfx + "consts")
        kb.op("dve", lambda e: e.memset(C["ones_f"][:], 1.0 / D), writes=[C["b"]])
        kb.op("dve", lambda e: e.memset(C["ones_b"][:], 1.0 / D), writes=[C["b"]])
        kb.op("dve", lambda e: e.memset(C["eps"][:], LN_EPS), writes=[C["b"]])
        return C

    def ln_alloc(self, es, pfx, W):
        kb = self.kb
        S = {}
        S["psm"] = self.ps(es, pfx + "psm", [128, 512])
        S["psv"] = self.ps(es, pfx + "psv", [128, 512])
        S["xc"] = self.sb(es, pfx + "xc", [128, NCH, W], F32)
        S["sq"] = self.sb(es, pfx + "sq", [128, NCH, W], BF16)
        S["rstd"] = self.sb(es, pfx + "rstd", [128, W], F32)
        for k in ("psm", "psv", "xc", "sq", "rstd"):
            S["b_" + k] = kb.buf(pfx + k)
        return S

    def ln_feat(self, X, bX, n, S, C):
        kb = self.kb
        xc, sq, rstd, psm, psv = S["xc"], S["sq"], S["rstd"], S["psm"], S["psv"]
        kb.mm(psm[:, 0:n], [(C["ones_f"][:], X[:, k, 0:n]) for k in range(NCH)],
              reads=[bX, C["b"]], writes=[S["b_psm"]])
        kb.op("dve", lambda e: e.tensor_tensor(out=xc[:, :, 0:n], in0=X[:, :, 0:n],
                                               in1=psm[:, 0:n].unsqueeze(1).broadcast_to([128, NCH, n]),
                                               op=ALU.subtract),
              reads=[bX, S["b_psm"]], writes=[S["b_xc"]])
        kb.op("act", lambda e: e.activation(out=sq[:, :, 0:n], in_=xc[:, :, 0:n], func=AF.Square),
              reads=[S["b_xc"]], writes=[S["b_sq"]])
        kb.mm(psv[:, 0:n], [(C["ones_b"][:], sq[:, k, 0:n]) for k in range(NCH)],
              reads=[S["b_sq"], C["b"]], writes=[S["b_psv"]])
        kb.op("act", lambda e: e.activation(out=rstd[:, 0:n], in_=psv[:, 0:n], func=AF.Ln, bias=C["eps"][:, 0:1]),
              reads=[S["b_psv"], C["b"]], writes=[S["b_rstd"]])
        kb.op("act", lambda e: e.activation(out=rstd[:, 0:n], in_=rstd[:, 0:n], func=AF.Exp, scale=-0.5),
              reads=[S["b_rstd"]], writes=[S["b_rstd"]])
        kb.op("dve", lambda e: e.tensor_tensor(out=xc[:, :, 0:n], in0=xc[:, :, 0:n],
                                               in1=rstd[:, 0:n].unsqueeze(1).broadcast_to([128, NCH, n]),
                                               op=ALU.mult),
              reads=[S["b_xc"], S["b_rstd"]], writes=[S["b_xc"]])

    def phase_mod(self):
        kb, nc = self.kb, self.nc
        with ExitStack() as es:
            cs = self.sb(es, "pm_cs", [128, NCH, 2], F32)
            wt = [self.sb(es, "pm_w%d" % i, [128, NCH, 512], F32) for i in range(2)]
            bada = self.sb(es, "pm_bada", [2, 6 * D], F32)
            mod = self.sb(es, "pm_mod", [2, 6 * D], F32)
            pst = [self.ps(es, "pm_ps%d" % i, [128, 512]) for i in range(2)]
            b_cs = kb.buf("cs", dma=True)
            b_wt = [kb.buf("wt%d" % i, dma=True) for i in range(2)]
            b_bada = kb.buf("bada", dma=True)
            b_mod = kb.buf("mod")
            b_ps = [kb.buf("ps%d" % i) for i in range(2)]
            kb.dma("sp", cs[:, :, 0:1], self.c_in.rearrange("o (k p) -> p k o", p=128), None, b_cs,
                   allow_slow_non_contiguous=True)
            kb.dma("sp", cs[:, :, 1:2], self.cctx_in.rearrange("o (k p) -> p k o", p=128), None, b_cs,
                   allow_slow_non_contiguous=True)
            kb.op("act", lambda e: e.activation(out=cs[:], in_=cs[:], func=AF.Silu), reads=[b_cs], writes=[b_cs])
            it = 0
            for l in range(self.depth):
                kb.dma("sp", bada[:], self.w["b_ada"][l:l + 1, :].broadcast_to([2, 6 * D]), None, b_bada)
                for cb in range(12):
                    w_t, bw = wt[it % 2], b_wt[it % 2]
                    p_t, bp = pst[it % 2], b_ps[it % 2]
                    it += 1
                    kb.dma("sp", w_t[:], self.w["w_ada"][l, :, cb * 512:(cb + 1) * 512].rearrange("(k p) n -> p k n", p=128),
                           None, bw)
                    kb.mm(p_t[0:2, :], [(cs[:, k, :], w_t[:, k, :]) for k in range(NCH)], reads=[b_cs, bw], writes=[bp])
                    kb.op("dve", lambda e: e.tensor_tensor(out=mod[:, cb * 512:(cb + 1) * 512], in0=p_t[0:2, :],
                                                           in1=bada[:, cb * 512:(cb + 1) * 512], op=ALU.add),
                          reads=[bp, b_bada], writes=[b_mod])
                kb.dma("sp", self.modv[l], mod[:], b_mod, self.b_modv)
            for l in range(self.depth):
                for j in range(2):
                    kb.dma("sp", self.modT[:, l, :, j:j + 1], self.modv[l, j:j + 1, :].rearrange("o (c p) -> p c o", p=128),
                           self.b_modv, self.b_modT, allow_slow_non_contiguous=True)
            for (c0, c1) in ((8, 16), (32, 40)):
                kb.op("dve", lambda e: e.tensor_scalar(out=self.modT[:, :, c0:c1, :], in0=self.modT[:, :, c0:c1, :],
                                                       scalar1=1.0, scalar2=None, op0=ALU.add),
                      reads=[self.b_modT], writes=[self.b_modT])
            kb.end_phase()

    def phase_proj(self, l, xT, b_xT):
        kb, nc = self.kb, self.nc
        T = self.T
        with ExitStack() as es:
            C = self.load_consts_common(es, "p1_")
            S = self.ln_alloc(es, "p1_", 512)
            w_sb = self.sb(es, "p1_w", [128, NCH, DPROJ], BF16)
            b_w = kb.buf("w_in", dma=True)
            for k in range(NCH):
                kb.dma("pool", w_sb[:, k, :], self.w["w_in"][l, k * 128:(k + 1) * 128, :], None, b_w)
            identb = self.sb(es, "p1_idb", [128, 128], BF16)
            b_id = kb.buf("idb", dma=True)
            kb.dma("sp", identb[:], self.c_ident_b, None, b_id)
            bdc = self.sb(es, "p1_bdc", [128, 2, 128], BF16)
            b_bdc = kb.buf("bdc", dma=True)
            kb.dma("sp", bdc[:], self.c_bdcs, None, b_bdc)
            gtab = self.sb(es, "p1_gtab", [128, 6, HD], F32)
            b_g = kb.buf("gtab", dma=True)
            for h in range(6):
                src = self.w["q_norm_g"] if h < 4 else self.w["k_norm_g"]
                kb.dma("sp", gtab[:, h, :], src[l:l + 1, :].broadcast_to([128, HD]), None, b_g)
            xt = [self.sb(es, "p1_x%d" % i, [128, NCH, 512], F32) for i in range(2)]
            b_xt = [kb.buf("xt%d" % i, dma=True) for i in range(2)]
            cs_t = [self.sb(es, "p1_cs%d" % i, [128, 4, 2, 32], F32) for i in range(2)]
            b_cs = [kb.buf("cs%d" % i, dma=True) for i in range(2)]
            hT = self.sb(es, "p1_hT", [128, NCH, 512], BF16)
            b_hT = kb.buf("hT")
            psz = self.ps(es, "p1_psz", [128, 1024])
            b_psz = kb.buf("psz")
            zs_l = [self.sb(es, "p1_zs%d" % i, [128, 1024], F32) for i in range(2)]
            b_zs_l = [kb.buf("zs%d" % i) for i in range(2)]
            sq6_l = [self.sb(es, "p1_sq6%d" % i, [128, 384], F32) for i in range(2)]
            ss6_l = [self.sb(es, "p1_ss6%d" % i, [128, 6], F32) for i in range(2)]
            b_sq6_l, b_ss6_l = [kb.buf("sq6%d" % i) for i in range(2)], [kb.buf("ss6%d" % i) for i in range(2)]
            tmp_l = [[self.sb(es, "p1_tmp%d_%d" % (s_, i), [128, 192], F32) for i in range(4)] for s_ in range(2)]
            b_tmp_l = [[kb.buf("tmp%d_%d" % (s_, i)) for i in range(4)] for s_ in range(2)]
            rope_o_l = [[self.sb(es, "p1_ro%d_%d" % (s_, i), [128, 384], BF16) for i in range(2)] for s_ in range(2)]
            b_ro_l = [[kb.buf("ro%d_%d" % (s_, i)) for i in range(2)] for s_ in range(2)]
            pst_l = [self.ps(es, "p1_pst%d" % i, [128, 6, 128], BF16) for i in range(2)]
            b_pst_l = [kb.buf("pst%d" % i) for i in range(2)]
            bi = 0
            qkT = [self.sb(es, "p1_qkT%d" % i, [128, 6, 512], BF16) for i in range(2)]
            b_qkT = [kb.buf("qkT%d" % i) for i in range(2)]
            vt = [self.sb(es, "p1_vt%d" % i, [128, 4, 2, 128], BF16) for i in range(2)]
            b_vt = [kb.buf("vt%d" % i) for i in range(2)]
            psf = [self.ps(es, "p1_psf%d" % i, [128, 512]) for i in range(2)]
            b_psf = [kb.buf("psf%d" % i) for i in range(2)]
            zfm = [self.sb(es, "p1_zfm%d" % i, [128, 8, 512], BF16) for i in range(2)]
            b_zfm = [kb.buf("zfm%d" % i) for i in range(2)]
            fab = [self.sb(es, "p1_fab%d" % i, [128, 2, 2, 512], BF16) for i in range(2)]
            b_fab = [kb.buf("fab%d" % i) for i in range(2)]
            xT_v = xT.rearrange("(c p) t -> p c t", p=128)
            eps_hd = self.sb(es, "p1_epshd", [128, 1], F32)
            kb.op("dve", lambda e: e.memset(eps_hd[:], LN_EPS), writes=[C["b"]])

            tiles = self.tiles()
            hTs = [hT, self.sb(es, "p1_hT2", [128, NCH, 512], BF16)]
            b_hTs = [b_hT, kb.buf("hT2")]

            def tile_loads(it):
                t0, n, isc = tiles[it]
                nb = n // 128
                x_t, bx = xt[it % 2], b_xt[it % 2]
                kb.dma("sp", x_t[:, :, 0:n], xT_v[:, :, t0:t0 + n], b_xT, bx)
                c_t, bc = cs_t[it % 2], b_cs[it % 2]
                kb.dma("sp", c_t[:, 0:nb, :, :], self.c_rope[t0:t0 + n].rearrange("(b p) a d -> p b a d", p=128), None, bc)

            def tile_ln(it):
                t0, n, isc = tiles[it]
                j = 1 if isc else 0
                x_t, bx = xt[it % 2], b_xt[it % 2]
                hT, b_hT = hTs[it % 2], b_hTs[it % 2]
                self.ln_stage(0, x_t, bx, n, S, C)
                yield
                self.ln_stage(1, x_t, bx, n, S, C)
                yield
                xc = S["xc"]
                for k in range(NCH):
                    kb.op("act", lambda e: e.activation(out=hT[:, k, 0:n], in_=xc[:, k, 0:n], func=AF.Identity,
                                                        scale=self.modT[:, l, 8 + k, j:j + 1],
                                                        bias=self.modT[:, l, 0 + k, j:j + 1]),
                          reads=[S["b_xc"], self.b_modT], writes=[b_hT])
                yield

            def tile_work(it):
                nonlocal bi
                t0, n, isc = tiles[it]
                j = 1 if isc else 0
                nb = n // 128
                c_t, bc = cs_t[it % 2], b_cs[it % 2]
                hT, b_hT = hTs[it % 2], b_hTs[it % 2]
                qk_t, bqk = qkT[it % 2], b_qkT[it % 2]
                v_t, bv = vt[it % 2], b_vt[it % 2]
                for b in range(nb):
                    zs, b_zs = zs_l[bi % 2], b_zs_l[bi % 2]
                    sq6, b_sq6, ss6, b_ss6 = sq6_l[bi % 2], b_sq6_l[bi % 2], ss6_l[bi % 2], b_ss6_l[bi % 2]
                    tmp, b_tmp = tmp_l[bi % 2], b_tmp_l[bi % 2]
                    rope_o, b_ro = rope_o_l[bi % 2], b_ro_l[bi % 2]
                    pst, b_pst = pst_l[bi % 2], b_pst_l[bi % 2]
                    bi += 1
                    for half in range(2):
                        kb.mm(psz[:, half * 512:(half + 1) * 512],
                              [(hT[:, k, b * 128:(b + 1) * 128], w_sb[:, k, half * 512:(half + 1) * 512]) for k in range(NCH)],
                              reads=[b_hT, b_w], writes=[b_psz])
                    kb.op("act", lambda e: e.copy(out=zs[:], in_=psz[:]), reads=[b_psz], writes=[b_zs])
                    kb.op("pool", lambda e: e.tensor_copy(out=v_t[:, b, 0, :], in_=zs[:, 384:512]), reads=[b_zs], writes=[bv])
                    kb.op("pool", lambda e: e.tensor_copy(out=v_t[:, b, 1, :], in_=zs[:, 896:1024]), reads=[b_zs], writes=[bv])
                    kb.op("act", lambda e: e.activation(out=sq6[:], in_=zs[:, 512:896], func=AF.Square), reads=[b_zs], writes=[b_sq6])
                    kb.op("dve", lambda e: e.tensor_reduce(out=ss6[:], in_=sq6[:].rearrange("p (h d) -> p h d", h=6), axis=AX.X, op=ALU.add),
                          reads=[b_sq6], writes=[b_ss6])
                    kb.op("act", lambda e: e.activation(out=ss6[:], in_=ss6[:], func=AF.Ln, scale=1.0 / HD, bias=eps_hd[:, 0:1]),
                          reads=[b_ss6, C["b"]], writes=[b_ss6])
                    kb.op("act", lambda e: e.activation(out=ss6[:], in_=ss6[:], func=AF.Exp, scale=-0.5), reads=[b_ss6], writes=[b_ss6])
                    zb = zs[:, 512:896].rearrange("p (h d) -> p h d", h=6)
                    kb.op("dve", lambda e: e.tensor_tensor(out=zb, in0=zb, in1=ss6[:].unsqueeze(2).broadcast_to([128, 6, HD]), op=ALU.mult),
                          reads=[b_zs, b_ss6], writes=[b_zs])
                    kb.op("dve", lambda e: e.tensor_tensor(out=zb, in0=zb, in1=gtab[:], op=ALU.mult), reads=[b_zs, b_g], writes=[b_zs])
                    cosb = c_t[:, b, 0, :].rearrange("p (a d) -> p a d", a=2).unsqueeze(1).broadcast_to([128, 6, 2, 16])
                    sinb = c_t[:, b, 1, :].rearrange("p (a d) -> p a d", a=2).unsqueeze(1).broadcast_to([128, 6, 2, 16])
                    for gi, c0 in enumerate((0, 512)):
                        zz = zs[:, c0:c0 + 384].rearrange("p (h a s d) -> p h a s d", h=6, a=2, s=2)
                        x1, x2 = zz[:, :, :, 0, :], zz[:, :, :, 1, :]
                        ro, bro = rope_o[gi], b_ro[gi]
                        rr = ro[:].rearrange("p (h a s d) -> p h a s d", h=6, a=2, s=2)
                        tv = [t[:].rearrange("p (h a d) -> p h a d", h=6, a=2) for t in tmp]
                        kb.op("dve", lambda e: e.tensor_tensor(out=tv[0], in0=x1, in1=cosb, op=ALU.mult), reads=[b_zs, bc], writes=[b_tmp[0]])
                        kb.op("dve", lambda e: e.tensor_tensor(out=tv[1], in0=x2, in1=sinb, op=ALU.mult), reads=[b_zs, bc], writes=[b_tmp[1]])
                        kb.op("pool", lambda e: e.tensor_tensor(out=tv[2], in0=x2, in1=cosb, op=ALU.mult), reads=[b_zs, bc], writes=[b_tmp[2]])
                        kb.op("pool", lambda e: e.tensor_tensor(out=tv[3], in0=x1, in1=sinb, op=ALU.mult), reads=[b_zs, bc], writes=[b_tmp[3]])
                        kb.op("dve", lambda e: e.tensor_tensor(out=rr[:, :, :, 0, :], in0=tv[0], in1=tv[1], op=ALU.subtract),
                              reads=[b_tmp[0], b_tmp[1]], writes=[bro])
                        kb.op("pool", lambda e: e.tensor_tensor(out=rr[:, :, :, 1, :], in0=tv[2], in1=tv[3], op=ALU.add),
                              reads=[b_tmp[2], b_tmp[3]], writes=[bro])
                        for c3 in range(3):
                            kb.mm(pst[:, gi * 3 + c3, :], [(ro[:, c3 * 128:(c3 + 1) * 128], identb[:])], reads=[bro, b_id], writes=[b_pst],
                                  transpose=True)
                    kb.op("act", lambda e: e.copy(out=qk_t[:, :, b * 128:(b + 1) * 128], in_=pst[:]), reads=[b_pst], writes=[bqk])
                    yield
                kb.dma("sp", self.QKa.rearrange("(c p) t -> p c t", p=128)[:, :, t0:t0 + n], qk_t[:, 0:3, 0:n], bqk, self.b_QKa)
                kb.dma("sp", self.QKb.rearrange("(c p) t -> p c t", p=128)[:, :, t0:t0 + n], qk_t[:, 3:6, 0:n], bqk, self.b_QKb)
                kb.dma("sp", self.Va[t0:t0 + n, :].rearrange("(b p) d -> p b d", p=128), v_t[:, 0:nb, 0, :], bv, self.b_Va)
                kb.dma("sp", self.Vb[t0:t0 + n, :].rearrange("(b p) d -> p b d", p=128), v_t[:, 0:nb, 1, :], bv, self.b_Vb)
                zf_t, bzf = zfm[it % 2], b_zfm[it % 2]
                for cc in range(8):
                    p_t, bp = psf[cc % 2], b_psf[cc % 2]
                    kb.mm(p_t[:, 0:n], [(w_sb[:, k, 1024 + cc * 128:1024 + (cc + 1) * 128], hT[:, k, 0:n]) for k in range(NCH)],
                          reads=[b_hT, b_w], writes=[bp])
                    if cc % 2 == 0:
                        kb.op("act", lambda e: e.copy(out=zf_t[:, cc, 0:n], in_=p_t[:, 0:n]), reads=[bp], writes=[bzf])
                    else:
                        kb.op("dve", lambda e: e.tensor_copy(out=zf_t[:, cc, 0:n], in_=p_t[:, 0:n]), reads=[bp], writes=[bzf])
                c0 = self.zcol(t0, isc)
                kb.dma("sp", self.zhT.rearrange("(c p) t -> p c t", p=128)[:, :, c0:c0 + n], zf_t[:, 0:6, 0:n], bzf, self.b_zhT)
                yield
                fa_t, bfa = fab[it % 2], b_fab[it % 2]
                for ab in range(2):
                    for ch in range(2):
                        p_t, bp = psf[(ab * 2 + ch) % 2], b_psf[(ab * 2 + ch) % 2]
                        kb.mm(p_t[:, 0:n], [(bdc[:, ab, :], zf_t[:, 6 + ch, 0:n])], reads=[bzf, b_bdc], writes=[bp])
                        kb.op("act", lambda e: e.copy(out=fa_t[:, ab, ch, 0:n], in_=p_t[:, 0:n]), reads=[bp], writes=[bfa])
                kb.dma("sp", self.faT.rearrange("(c p) t -> p c t", p=128)[:, :, t0:t0 + n], fa_t[:, 0, :, 0:n], bfa, self.b_faT)
                kb.dma("sp", self.fbT.rearrange("(c p) t -> p c t", p=128)[:, :, t0:t0 + n], fa_t[:, 1, :, 0:n], bfa, self.b_fbT)

                yield

            def drain(g):
                for _ in g:
                    pass

            nt = len(tiles)
            tile_loads(0)
            drain(tile_ln(0))
            for it in range(nt):
                gl = None
                if it + 1 < nt:
                    tile_loads(it + 1)
                    gl = tile_ln(it + 1)
                gw = tile_work(it)
                while gw is not None or gl is not None:
                    if gw is not None:
                        try:
                            next(gw)
                        except StopIteration:
                            gw = None
                    if gl is not None:
                        try:
                            next(gl)
                        except StopIteration:
                            gl = None
            kb.end_phase()

    def zcol(self, t0, isc):
        return (t0 + 3) if isc else (t0 + 1)
    def phase_attn(self, l, which):
        kb_, nc = self.kb, self.nc
        kb = kb_
        T, L = self.T, self.L
        nblk, nlat = T // 128, L // 128
        QK, V, bQK, bV = (self.QKa, self.Va, self.b_QKa, self.b_Va) if which == "a" else (self.QKb, self.Vb, self.b_QKb, self.b_Vb)
        rb = 0 if which == "a" else 256
        with ExitStack() as es:
            ones_f = self.sb(es, "at_ones", [128, 128], F32)
            b_ones = kb.buf("ones")
            kb.op("dve", lambda e: e.memset(ones_f[:], 1.0), writes=[b_ones])
            if which == "a":
                masks = self.sb(es, "at_mask", [128, 6, 512], BF16)
                b_mask = kb.buf("mask", dma=True)
                kb.dma("sp", masks[:], self.c_wmask, None, b_mask)
                sk = self.sb(es, "at_sink", [128, 4], F32)
                b_sk = kb.buf("sink", dma=True)
                kb.dma("sp", sk[:], self.w["sink_a"][l:l + 1, :].broadcast_to([128, 4]), None, b_sk)
                kb.op("act", lambda e: e.activation(out=sk[:], in_=sk[:], func=AF.Exp), reads=[b_sk], writes=[b_sk])
            K2 = self.sb(es, "at_K2", [128, T], BF16)
            b_K2 = kb.buf("K2", dma=True)
            V1 = self.sb(es, "at_V1", [128, nblk, 65], BF16)
            b_V1 = kb.buf("V1", dma=True)
            Q2 = [self.sb(es, "at_Q%d" % i, [128, 512], BF16) for i in range(2)]
            b_Q2 = [kb.buf("Q%d" % i, dma=True) for i in range(2)]
            psS = [self.ps(es, "at_psS%d" % i, [128, 2, 512]) for i in range(2)]
            b_psS = [kb.buf("psS%d" % i) for i in range(2)]
            NPT = 5
            LAG = 2
            pT = [self.sb(es, "at_pT%d" % i, [128, 2, 512], BF16) for i in range(NPT)]
            b_pT = [kb.buf("pT%d" % i) for i in range(NPT)]
            psO2 = [self.ps(es, "at_psO%d" % i, [128, 2, 512]) for i in range(2)]
            b_psO2 = [[kb.buf("psO%d_%d" % (i, h)) for h in range(2)] for i in range(2)]
            den = self.sb(es, "at_den", [128, 512], F32)
            b_den = kb.buf("den")
            osb = [self.sb(es, "at_o%d" % i, [64, 2, 512], F32) for i in range(2)]
            b_osb = [kb.buf("o%d" % i) for i in range(2)]
            qi = 0
            ei = 0
            pi = 0
            for j in range(2):
                for hh in range(2):
                    kb.dma("sp", K2[64 * hh:64 * hh + 64, :], QK[256 + 64 * j:256 + 64 * j + 64, :], bQK, b_K2)
                kb.op("pool", lambda e: e.memset(V1[:, :, 64:65], 1.0), writes=[b_V1])
                for b0 in range(0, nblk, 16):
                    b1 = min(nblk, b0 + 16)
                    kb.dma("sp", V1[:, b0:b1, 0:64], V[b0 * 128:b1 * 128, 64 * j:64 * j + 64].rearrange("(b p) d -> p b d", p=128), bV, b_V1)
                for (t0, n, isc) in self.tiles():
                    qi += 1
                    q_t, bq = Q2[qi % 2], b_Q2[qi % 2]
                    o_t, bo = osb[qi % 2], b_osb[qi % 2]
                    kb.dma("sp", q_t[:, 0:n], QK[128 * j:128 * j + 128, t0:t0 + n], bQK, bq)
                    ctx_blocks = [(b, None) for b in range(nlat, nblk)]
                    if isc:
                        blocks = ctx_blocks
                    elif which == "b":
                        blocks = [(b, None) for b in range(nblk)]
                    else:
                        qb0 = t0 // 128
                        blocks = [(b, b - qb0 + 1) for b in range(max(0, qb0 - 1), min(nlat - 1, qb0 + 4) + 1)] + ctx_blocks
                    nk = len(blocks)
                    psO, b_psO = psO2[qi % 2], b_psO2[qi % 2]
                    pend = []
                    for ki, (kblk, rel) in enumerate(blocks):
                        ps_t, bps = psS[ei % 2], b_psS[ei % 2]
                        p_t, bp = pT[pi % NPT], b_pT[pi % NPT]
                        pi += 1
                        ei += 1
                        for hh in range(2):
                            kb.mm(ps_t[:, hh, 0:n], [(K2[64 * hh:64 * hh + 64, kblk * 128:(kblk + 1) * 128], q_t[64 * hh:64 * hh + 64, 0:n])],
                                  reads=[b_K2, bq], writes=[bps])
                        kb.op("act", lambda e: e.activation(out=p_t[:, :, 0:n], in_=ps_t[:, :, 0:n], func=AF.Exp, scale=HD ** -0.5),
                              reads=[bps], writes=[bp])
                        if rel is not None:
                            kb.op("dve", lambda e: e.tensor_tensor(out=p_t[:, :, 0:n], in0=p_t[:, :, 0:n],
                                                                    in1=masks[:, rel, 0:n].unsqueeze(1).broadcast_to([128, 2, n]), op=ALU.mult),
                                  reads=[bp, b_mask], writes=[bp])
                        pend.append((ki, p_t, bp, kblk))
                        if len(pend) > LAG:
                            pk, pp_t, pbp, pkblk = pend.pop(0)
                            for hh in range(2):
                                kb.mm1(psO[0:65, hh, 0:n], V1[:, pkblk, :], pp_t[:, hh, 0:n], start=(pk == 0), stop=(pk == nk - 1),
                                       reads=[b_V1, pbp], writes=[b_psO[hh]])
                    while pend:
                        pk, pp_t, pbp, pkblk = pend.pop(0)
                        for hh in range(2):
                            kb.mm1(psO[0:65, hh, 0:n], V1[:, pkblk, :], pp_t[:, hh, 0:n], start=(pk == 0), stop=(pk == nk - 1),
                                   reads=[b_V1, pbp], writes=[b_psO[hh]])
                    psBt, b_psB = psS[ei % 2], b_psS[ei % 2]
                    ei += 1
                    for hh in range(2):
                        if which == "a":
                            kb.op("dve", lambda e: e.tensor_scalar(out=den[64:65, 0:n], in0=psO[64:65, hh, 0:n],
                                                                   scalar1=sk[64:65, 2 * j + hh:2 * j + hh + 1], scalar2=None, op0=ALU.add),
                                  reads=[b_psO[hh], b_sk], writes=[b_den])
                            kb.op("dve", lambda e: e.reciprocal(out=den[64:65, 0:n], in_=den[64:65, 0:n]), reads=[b_den], writes=[b_den])
                        else:
                            kb.op("dve", lambda e: e.reciprocal(out=den[64:65, 0:n], in_=psO[64:65, hh, 0:n]), reads=[b_psO[hh]], writes=[b_den])
                        kb.mm(psBt[0:64, hh, 0:n], [(ones_f[64:65, 0:64], den[64:65, 0:n])], reads=[b_ones, b_den], writes=[b_psB])
                        kb.op("act", lambda e: e.copy(out=o_t[:, hh, 0:n], in_=psO[0:64, hh, 0:n]), reads=[b_psO[hh]], writes=[bo])
                        kb.op("dve", lambda e: e.tensor_tensor(out=o_t[:, hh, 0:n], in0=o_t[:, hh, 0:n], in1=psBt[0:64, hh, 0:n], op=ALU.mult),
                              reads=[bo, b_psB], writes=[bo])
                    r0 = rb + 128 * j
                    kb.dma("sp", self.mixT[r0:r0 + 128, t0:t0 + n].rearrange("(h d) t -> d h t", h=2), o_t[:, :, 0:n], bo, self.b_mixT)
            kb.end_phase()
    def dft_alloc(self, es, pfx):
        kb = self.kb
        Dd = {}
        Dd["psA"] = [self.ps(es, pfx + "psA%d" % i, [128, 1024]) for i in range(2)]
        Dd["psBr"] = [self.ps(es, pfx + "psBr%d" % i, [128, 512]) for i in range(2)]
        Dd["psBi"] = [self.ps(es, pfx + "psBi%d" % i, [128, 512]) for i in range(2)]
        Dd["t"] = [self.sb(es, pfx + "t%d" % i, [128, 512], F32) for i in range(8)]
        Dd["Br"] = [self.sb(es, pfx + "Br%d" % i, [128, 512], BF16) for i in range(2)]
        Dd["Bi"] = [self.sb(es, pfx + "Bi%d" % i, [128, 512], BF16) for i in range(2)]
        for k in ("psA", "psBr", "psBi", "Br", "Bi"):
            Dd["b_" + k] = [kb.buf(pfx + k + str(i)) for i in range(2)]
        Dd["b_t"] = [kb.buf(pfx + "t%d" % i) for i in range(8)]
        Dd["cnt"] = 0
        return Dd

    def dft_tabs(self, es, pfx, N1, inv):
        kb = self.kb
        key = "%d_%s" % (N1, "i" if inv else "f")
        KA, MA, FA = (128, N1, 128) if inv else (N1, 128, N1)
        Tt = {"N1": N1, "MA": MA, "FA": FA}
        Tt["tabA"] = self.sb(es, pfx + "tabA" + key, [KA, 2, 2 * FA], BF16)
        Tt["tw"] = self.sb(es, pfx + "tw" + key, [MA, 2, FA], F32)
        Tt["tabB"] = self.sb(es, pfx + "tabB" + key, [MA, 4, MA], BF16)
        Tt["b"] = kb.buf(pfx + "tabs" + key, dma=True)
        kb.dma("sp", Tt["tabA"][:], self.c_dft["tabA_" + key], None, Tt["b"])
        kb.dma("sp", Tt["tw"][:], self.c_dft["tw_" + key], None, Tt["b"])
        kb.dma("sp", Tt["tabB"][:], self.c_dft["tabB_" + key], None, Tt["b"])
        return Tt

    def dft2(self, Dd, Tt, Xre, Xim, bX, KA, want_im, P_out, out_cb, CB=64):
        kb = self.kb
        MA, FA = Tt["MA"], Tt["FA"]
        G = min(512 // FA, CB)
        W = G * FA
        tabA, tw, tabB, bT = Tt["tabA"], Tt["tw"], Tt["tabB"], Tt["b"]
        bXs = list(bX) if isinstance(bX, (list, tuple)) else [bX]
        for g0 in range(0, CB, G):
            i = Dd["cnt"] % 2
            Dd["cnt"] += 1
            psA, bpsA = Dd["psA"][i], Dd["b_psA"][i]
            Br, bBr = Dd["Br"][i], Dd["b_Br"][i]
            Bi, bBi = Dd["Bi"][i], Dd["b_Bi"][i]
            for ci in range(G):
                c = g0 + ci
                pairs = [(Xre[0:KA, c, 0:MA], tabA[0:KA, 0, :])]
                if Xim is not None:
                    pairs.append((Xim[0:KA, c, 0:MA], tabA[0:KA, 1, :]))
                kb.mm(psA[0:MA, ci * 2 * FA:(ci + 1) * 2 * FA], pairs, reads=bXs + [bT], writes=[bpsA])
            Av = psA[0:MA, 0:2 * W].rearrange("p (g s k) -> p g s k", g=G, s=2)
            Are, Aim = Av[:, :, 0, :], Av[:, :, 1, :]
            cb_ = tw[:, 0, :].unsqueeze(1).broadcast_to([MA, G, FA])
            sb_ = tw[:, 1, :].unsqueeze(1).broadcast_to([MA, G, FA])
            tset = Dd["t"][4 * i:4 * i + 4]
            bt = Dd["b_t"][4 * i:4 * i + 4]
            t = [x[0:MA, 0:W].rearrange("p (g k) -> p g k", g=G) for x in tset]
            kb.op("dve", lambda e: e.tensor_tensor(out=t[0], in0=Are, in1=cb_, op=ALU.mult), reads=[bpsA, bT], writes=[bt[0]])
            kb.op("dve", lambda e: e.tensor_tensor(out=t[1], in0=Aim, in1=sb_, op=ALU.mult), reads=[bpsA, bT], writes=[bt[1]])
            kb.op("pool", lambda e: e.tensor_tensor(out=Br[0:MA, 0:W], in0=tset[0][0:MA, 0:W], in1=tset[1][0:MA, 0:W], op=ALU.add),
                  reads=[bt[0], bt[1]], writes=[bBr])
            kb.op("dve", lambda e: e.tensor_tensor(out=t[2], in0=Aim, in1=cb_, op=ALU.mult), reads=[bpsA, bT], writes=[bt[2]])
            kb.op("dve", lambda e: e.tensor_tensor(out=t[3], in0=Are, in1=sb_, op=ALU.mult), reads=[bpsA, bT], writes=[bt[3]])
            kb.op("pool", lambda e: e.tensor_tensor(out=Bi[0:MA, 0:W], in0=tset[2][0:MA, 0:W], in1=tset[3][0:MA, 0:W], op=ALU.subtract),
                  reads=[bt[2], bt[3]], writes=[bBi])
            psr, bpsr = Dd["psBr"][i], Dd["b_psBr"][i]
            kb.mm(psr[0:P_out, 0:W], [(tabB[0:MA, 0, 0:P_out], Br[0:MA, 0:W]), (tabB[0:MA, 1, 0:P_out], Bi[0:MA, 0:W])],
                  reads=[bBr, bBi, bT], writes=[bpsr])
            psi, bpsi = None, None
            if want_im:
                psi, bpsi = Dd["psBi"][i], Dd["b_psBi"][i]
                kb.mm(psi[0:P_out, 0:W], [(tabB[0:MA, 2, 0:P_out], Br[0:MA, 0:W]), (tabB[0:MA, 3, 0:P_out], Bi[0:MA, 0:W])],
                      reads=[bBr, bBi, bT], writes=[bpsi])
            out_cb(g0, G, W, psr, bpsr, psi, bpsi)

    def phase_hyena_filter(self, l):
        kb = self.kb
        L = self.L
        NT = 2 * L + 512 + 1
        with ExitStack() as es:
            w1 = self.sb(es, "hf_w1", [HY_EMB, HY_HID], F32)
            w2 = self.sb(es, "hf_w2", [HY_HID, HY_HID], F32)
            w3 = self.sb(es, "hf_w3", [HY_HID, 512], F32)
            vec = self.sb(es, "hf_vec", [HY_HID, 6], F32)
            b_w = kb.buf("hf_w", dma=True)
            kb.dma("sp", w1[:], self.w["hy_f_w1"][l], None, b_w)
            kb.dma("sp", w2[:], self.w["hy_f_w2"][l], None, b_w)
            kb.dma("sp", w3[:], self.w["hy_f_w3"][l], None, b_w)
            for i, nm in enumerate(("hy_f_b1", "hy_f_freq", "hy_f_b2")):
                kb.dma("sp", vec[:, i:i + 1], self.w[nm][l:l + 1, :].rearrange("o h -> h o"), None, b_w, allow_slow_non_contiguous=True)
            kb.op("dve", lambda e: e.tensor_tensor(out=vec[:, 3:4], in0=vec[:, 0:1], in1=vec[:, 1:2], op=ALU.mult), reads=[b_w], writes=[b_w])
            kb.op("dve", lambda e: e.tensor_tensor(out=vec[:, 4:5], in0=vec[:, 2:3], in1=vec[:, 1:2], op=ALU.mult), reads=[b_w], writes=[b_w])
            zt = [self.sb(es, "hf_z%d" % i, [HY_EMB, 512], F32) for i in range(2)]
            b_zt = [kb.buf("hf_z%d" % i, dma=True) for i in range(2)]
            dec = [self.sb(es, "hf_dec%d" % i, [128, 2, 512], F32) for i in range(2)]
            b_dec = [kb.buf("hf_dec%d" % i, dma=True) for i in range(2)]
            ps1s = [self.ps(es, "hf_ps1%d" % i, [128, 512]) for i in range(2)]
            ps3 = [self.ps(es, "hf_ps3%d" % i, [128, 512]) for i in range(2)]
            b_ps1s, b_ps3 = [kb.buf("ps1%d" % i) for i in range(2)], [kb.buf("ps3%d" % i) for i in range(2)]
            hs = [self.sb(es, "hf_h%d" % i, [HY_HID, 512], F32) for i in range(2)]
            ms = [self.sb(es, "hf_m%d" % i, [HY_HID, 512], F32) for i in range(2)]
            b_hs, b_ms = [kb.buf("h%d" % i) for i in range(2)], [kb.buf("m%d" % i) for i in range(2)]
            kf = [self.sb(es, "hf_kf%d" % i, [128, 2, 512], F32) for i in range(2)]
            b_kf = [kb.buf("kf%d" % i) for i in range(2)]
            kfb = [self.sb(es, "hf_kfb%d" % i, [128, 2, 512], BF16) for i in range(2)]
            b_kfb = [kb.buf("kfb%d" % i) for i in range(2)]
            hb0 = self.sb(es, "hf_hb0", [128, 2, 1], F32)
            b_hb0 = kb.buf("hb0")
            tl = [(2 * L + 512, 1, True, False)]
            for c0 in range(0, L, 512):
                tl.append((c0, 512, False, c0 == 0))
            for c0 in range(L, 2 * L, 512):
                tl.append((c0, 512, True, False))
            tl.append((2 * L, 256, False, True))
            tl.append((2 * L + 256, 256, True, False))
            kfv = self.kfT.rearrange("(c p) n -> p c n", p=128)
            decv = self.c_hdec.rearrange("(c p) n -> p c n", p=128)
            for it, (c0, n, back, addh) in enumerate(tl):
                z_t, bz = zt[it % 2], b_zt[it % 2]
                d_t, bd = dec[it % 2], b_dec[it % 2]
                ps1, b_ps1 = ps1s[it % 2], b_ps1s[it % 2]
                h, b_h = hs[it % 2], b_hs[it % 2]
                m, b_m = ms[it % 2], b_ms[it % 2]
                kw = dict(allow_slow_non_contiguous=True) if n == 1 else {}
                kb.dma("sp", z_t[:, 0:n], self.c_hz[:, c0:c0 + n], None, bz, **kw)
                kb.dma("sp", d_t[:, :, 0:n], decv[:, :, c0:c0 + n], None, bd, **kw)
                src, bsrc, K = z_t, bz, HY_EMB
                for (wt, bcol) in ((w1, 3), (w2, 4)):
                    kb.mm(ps1[0:HY_HID, 0:n], [(wt[0:K, :], src[0:K, 0:n])], reads=[b_w, bsrc], writes=[b_ps1])
                    kb.op("act", lambda e: e.activation(out=h[:, 0:n], in_=ps1[0:HY_HID, 0:n], func=AF.Identity,
                                                        scale=vec[:, 1:2], bias=vec[:, bcol:bcol + 1]),
                          reads=[b_ps1, b_w], writes=[b_h])
                    for rep in range(1):
                        for (cmp_, thr, add) in ((ALU.is_gt, math.pi, -2 * math.pi), (ALU.is_lt, -math.pi, 2 * math.pi)):
                            kb.op("dve", lambda e: e.tensor_scalar(out=m[:, 0:n], in0=h[:, 0:n], scalar1=thr, scalar2=add, op0=cmp_, op1=ALU.mult),
                                  reads=[b_h], writes=[b_m])
                            kb.op("dve", lambda e: e.tensor_tensor(out=h[:, 0:n], in0=h[:, 0:n], in1=m[:, 0:n], op=ALU.add),
                                  reads=[b_h, b_m], writes=[b_h])
                    kb.op("act", lambda e: e.activation(out=h[:, 0:n], in_=h[:, 0:n], func=AF.Sin), reads=[b_h], writes=[b_h])
                    src, bsrc, K = h, b_h, HY_HID
                kf_t, bkf = kf[it % 2], b_kf[it % 2]
                kfb_t, bkfb = kfb[it % 2], b_kfb[it % 2]
                for ch in range(2):
                    p3, bp3 = ps3[ch], b_ps3[ch]
                    wc0 = (256 if back else 0) + ch * 128
                    kb.mm(p3[:, 0:n], [(w3[:, wc0:wc0 + 128], h[:, 0:n])], reads=[b_w, b_h], writes=[bp3])
                    if n == 1:
                        kb.op("dve", lambda e: e.tensor_copy(out=hb0[:, ch, :], in_=p3[:, 0:1]), reads=[bp3], writes=[b_hb0])
                    else:
                        kb.op("dve", lambda e: e.tensor_tensor(out=kf_t[:, ch, 0:n], in0=p3[:, 0:n], in1=d_t[:, ch, 0:n], op=ALU.mult),
                              reads=[bp3, bd], writes=[bkf])
                if n == 1:
                    continue
                if addh:
                    kb.op("dve", lambda e: e.tensor_tensor(out=kf_t[:, :, 0:1], in0=kf_t[:, :, 0:1], in1=hb0[:], op=ALU.add),
                          reads=[bkf, b_hb0], writes=[bkf])
                kb.op("pool", lambda e: e.tensor_copy(out=kfb_t[:, :, 0:n], in_=kf_t[:, :, 0:n]), reads=[bkf], writes=[bkfb])
                kb.dma("sp", kfv[:, :, c0:c0 + n], kfb_t[:, :, 0:n], bkfb, self.b_kfT)
            kb.end_phase()

    def phase_zinit(self):
        kb = self.kb
        with ExitStack() as es:
            z = self.sb(es, "zi_z", [128, 6, 1], BF16)
            b_z = kb.buf("z")
            kb.op("dve", lambda e: e.memset(z[:], 0.0), writes=[b_z])
            zv = self.zhT.rearrange("(c p) t -> p c t", p=128)
            for col in (0, self.L + 1, self.L + 2, self.T + 3):
                kb.dma("sp", zv[:, :, col:col + 1], z[:], b_z, self.b_zhT, allow_slow_non_contiguous=True)
            kb.end_phase()

    def phase_hyena_prep(self, l):
        kb = self.kb
        with ExitStack() as es:
            cw = self.sb(es, "hp_cw", [128, 6, 4], F32)
            b_cw = kb.buf("cw", dma=True)
            for tp in range(3):
                kb.dma("sp", cw[:, :, tp:tp + 1], self.w["hy_conv_w"][l, tp:tp + 1, :].rearrange("o (c p) -> p c o", p=128), None, b_cw,
                       allow_slow_non_contiguous=True)
            kb.dma("sp", cw[:, :, 3:4], self.w["hy_conv_b"][l:l + 1, :].rearrange("o (c p) -> p c o", p=128), None, b_cw,
                   allow_slow_non_contiguous=True)
            zt = [self.sb(es, "hp_z%d" % i, [128, 6, 514], BF16) for i in range(2)]
            b_zt = [kb.buf("z%d" % i, dma=True) for i in range(2)]
            u = [self.sb(es, "hp_u%d" % i, [128, 6, 512], F32) for i in range(2)]
            b_u = [kb.buf("u%d" % i) for i in range(2)]
            o = [self.sb(es, "hp_o%d" % i, [128, 4, 512], BF16) for i in range(2)]
            b_o = [kb.buf("o%d" % i) for i in range(2)]
            zv = self.zhT.rearrange("(c p) t -> p c t", p=128)
            for it, (t0, n, isc) in enumerate(self.tiles()):
                z_t, bz = zt[it % 2], b_zt[it % 2]
                u_t, bu = u[it % 2], b_u[it % 2]
                o_t, bo = o[it % 2], b_o[it % 2]
                c0 = self.zcol(t0, isc)
                kb.dma("sp", z_t[:, :, 0:n + 2], zv[:, :, c0 - 1:c0 + n + 1], self.b_zhT, bz)
                for c in range(6):
                    kb.op("act", lambda e: e.activation(out=u_t[:, c, 0:n], in_=z_t[:, c, 1:n + 1], func=AF.Identity,
                                                        scale=cw[:, c, 1:2], bias=cw[:, c, 3:4]), reads=[bz, b_cw], writes=[bu])
                    kb.op("dve", lambda e: e.scalar_tensor_tensor(out=u_t[:, c, 0:n], in0=z_t[:, c, 0:n], scalar=cw[:, c, 0:1],
                                                                  in1=u_t[:, c, 0:n], op0=ALU.mult, op1=ALU.add),
                          reads=[bz, b_cw, bu], writes=[bu])
                    kb.op("dve", lambda e: e.scalar_tensor_tensor(out=u_t[:, c, 0:n], in0=z_t[:, c, 2:n + 2], scalar=cw[:, c, 2:3],
                                                                  in1=u_t[:, c, 0:n], op0=ALU.mult, op1=ALU.add),
                          reads=[bz, b_cw, bu], writes=[bu])
                kb.op("pool", lambda e: e.tensor_tensor(out=o_t[:, 0:2, 0:n], in0=u_t[:, 4:6, 0:n], in1=u_t[:, 2:4, 0:n], op=ALU.mult),
                      reads=[bu], writes=[bo])
                kb.op("pool", lambda e: e.tensor_copy(out=o_t[:, 2:4, 0:n], in_=u_t[:, 0:2, 0:n]), reads=[bu], writes=[bo])
                kb.dma("sp", self.vxT.rearrange("(c p) t -> p c t", p=128)[:, :, t0:t0 + n], o_t[:, 0:2, 0:n], bo, self.b_vxT)
                kb.dma("sp", self.x0T.rearrange("(c p) t -> p c t", p=128)[:, :, t0:t0 + n], o_t[:, 2:4, 0:n], bo, self.b_x0T)
            kb.end_phase()

    def phase_hyena_conv(self, l):
        kb = self.kb
        L = self.L
        with ExitStack() as es:
            Dd = self.dft_alloc(es, "hc_")
            skip = self.sb(es, "hc_skip", [128, 256], F32)
            b_skip = kb.buf("skip", dma=True)
            kb.dma("sp", skip[:], self.w["hy_skip"][l:l + 1, :].broadcast_to([128, 256]), None, b_skip)
            X = self.sb(es, "hc_X", [128, 64, 128], BF16)
            X0 = self.sb(es, "hc_X0", [128, 64, 128], BF16)
            Yr = self.sb(es, "hc_Yr", [128, 64, 128], BF16)
            Yi = self.sb(es, "hc_Yi", [128, 64, 128], BF16)
            O = self.sb(es, "hc_O", [128, 64, 128], F32)
            b_X, b_X0 = kb.buf("X", dma=True), kb.buf("X0", dma=True)
            b_Yr, b_Yi, b_O = kb.buf("Yr"), kb.buf("Yi"), kb.buf("O")
            kh = [self.sb(es, "hc_kh%d" % i, [128, 2, 512], F32) for i in range(2)]
            b_kh = [kb.buf("kh%d" % i, dma=True) for i in range(2)]
            tt = [self.sb(es, "hc_tt%d" % i, [128, 512], F32) for i in range(8)]
            b_tt = [kb.buf("tt%d" % i) for i in range(8)]
            ksp = self.sb(es, "hc_ksp", [128, 2, 512], F32)
            b_ksp = kb.buf("ksp")
            cnt = [0]
            for (tok0, Ls, kcol, Kh, bKh) in ((0, L, 0, self.Khat, self.b_Khat), (L, CTX, 2 * L, self.Khatc, self.b_Khatc)):
                N1 = 2 * Ls // 128
                P1 = Ls // 128
                Tf = self.dft_tabs(es, "hc%d_" % tok0, N1, False)
                Ti = self.dft_tabs(es, "hc%d_" % tok0, N1, True)
                G = min(512 // N1, 64)
                W = G * N1
                for c0 in range(0, 256, 64):
                    for h0 in (0, 32):
                        kb.dma("sp", X[0:N1, h0:h0 + 32, :], self.kfT[c0 + h0:c0 + h0 + 32, kcol:kcol + 2 * Ls].rearrange("c (a b) -> a c b", b=128), self.b_kfT, b_X)

                    def cb_k(g0, G_, W_, psr, bpsr, psi, bpsi, c0=c0):
                        kb.op("act", lambda e: e.copy(out=ksp[:, 0, 0:W_], in_=psr[:, 0:W_]), reads=[bpsr], writes=[b_ksp])
                        kb.op("dve", lambda e: e.tensor_copy(out=ksp[:, 1, 0:W_], in_=psi[:, 0:W_]), reads=[bpsi], writes=[b_ksp])
                        col = (c0 + g0) * N1
                        kb.dma("sp", Kh[:, :, col:col + W_].rearrange("s p n -> p s n"), ksp[:, :, 0:W_], b_ksp, bKh)
                    self.dft2(Dd, Tf, X, None, b_X, N1, True, 128, cb_k)
                for c0 in range(0, 256, 64):
                    for h0 in (0, 32):
                        kb.dma("sp", X[0:P1, h0:h0 + 32, :], self.vxT[c0 + h0:c0 + h0 + 32, tok0:tok0 + Ls].rearrange("c (a b) -> a c b", b=128), self.b_vxT, b_X)
                        kb.dma("sp", X0[0:P1, h0:h0 + 32, :], self.x0T[c0 + h0:c0 + h0 + 32, tok0:tok0 + Ls].rearrange("c (a b) -> a c b", b=128), self.b_x0T, b_X0)

                    def cb_f(g0, G_, W_, psr, bpsr, psi, bpsi, c0=c0):
                        i = cnt[0] % 2
                        cnt[0] += 1
                        k_t, bk = kh[i], b_kh[i]
                        col = (c0 + g0) * N1
                        kb.dma("sp", k_t[:, :, 0:W_], Kh[:, :, col:col + W_].rearrange("s p n -> p s n"), bKh, bk)
                        yr = Yr[:, g0:g0 + G_, 0:N1]
                        yi = Yi[:, g0:g0 + G_, 0:N1]
                        tv = [x[:, 0:W_] for x in tt[4 * i:4 * i + 4]]
                        btt = b_tt[4 * i:4 * i + 4]
                        kb.op("dve", lambda e: e.tensor_tensor(out=tv[0], in0=psr[:, 0:W_], in1=k_t[:, 0, 0:W_], op=ALU.mult), reads=[bpsr, bk], writes=[btt[0]])
                        kb.op("dve", lambda e: e.tensor_tensor(out=tv[1], in0=psi[:, 0:W_], in1=k_t[:, 1, 0:W_], op=ALU.mult), reads=[bpsi, bk], writes=[btt[1]])
                        kb.op("pool", lambda e: e.tensor_tensor(out=yr, in0=tv[0].rearrange("p (g k) -> p g k", g=G_),
                                                                in1=tv[1].rearrange("p (g k) -> p g k", g=G_), op=ALU.subtract),
                              reads=[btt[0], btt[1]], writes=[b_Yr])
                        kb.op("dve", lambda e: e.tensor_tensor(out=tv[2], in0=psr[:, 0:W_], in1=k_t[:, 1, 0:W_], op=ALU.mult), reads=[bpsr, bk], writes=[btt[2]])
                        kb.op("dve", lambda e: e.tensor_tensor(out=tv[3], in0=psi[:, 0:W_], in1=k_t[:, 0, 0:W_], op=ALU.mult), reads=[bpsi, bk], writes=[btt[3]])
                        kb.op("pool", lambda e: e.tensor_tensor(out=yi, in0=tv[2].rearrange("p (g k) -> p g k", g=G_),
                                                                in1=tv[3].rearrange("p (g k) -> p g k", g=G_), op=ALU.add),
                              reads=[btt[2], btt[3]], writes=[b_Yi])
                    self.dft2(Dd, Tf, X, None, b_X, P1, True, 128, cb_f)

                    def cb_i(g0, G_, W_, psr, bpsr, psi, bpsi, c0=c0):
                        ov = O[0:P1, g0:g0 + G_, :]
                        sk = skip[0:P1, c0 + g0:c0 + g0 + G_].unsqueeze(2).broadcast_to([P1, G_, 128])
                        kb.op("pool", lambda e: e.tensor_tensor(out=ov, in0=X[0:P1, g0:g0 + G_, :], in1=sk, op=ALU.mult),
                              reads=[b_X, b_skip], writes=[b_O])
                        kb.op("dve", lambda e: e.scalar_tensor_tensor(out=ov, in0=psr[0:P1, 0:W_].rearrange("p (g k) -> p g k", g=G_),
                                                                      scalar=1.0 / (2 * Ls), in1=ov, op0=ALU.mult, op1=ALU.add),
                              reads=[bpsr, b_O], writes=[b_O])
                        kb.op("dve", lambda e: e.tensor_tensor(out=ov, in0=ov, in1=X0[0:P1, g0:g0 + G_, :], op=ALU.mult),
                              reads=[b_O, b_X0], writes=[b_O])
                    self.dft2(Dd, Ti, Yr, Yi, [b_Yr, b_Yi], 128, False, P1, cb_i)
                    for h0 in (0, 32):
                        kb.dma("sp", self.mixT[512 + c0 + h0:512 + c0 + h0 + 32, tok0:tok0 + Ls].rearrange("c (a b) -> a c b", b=128), O[0:P1, h0:h0 + 32, :], b_O, self.b_mixT)
            kb.end_phase()

    def phase_fnet(self, l):
        kb = self.kb
        L = self.L
        with ExitStack() as es:
            Dd = self.dft_alloc(es, "fn_")
            Xr = self.sb(es, "fn_Xr", [128, 64, 128], BF16)
            Xi = self.sb(es, "fn_Xi", [128, 64, 128], BF16)
            b_Xr, b_Xi = kb.buf("Xr", dma=True), kb.buf("Xi", dma=True)
            fo = [self.sb(es, "fn_fo%d" % i, [128, 512], BF16) for i in range(2)]
            b_fo = [kb.buf("fo%d" % i) for i in range(2)]
            cnt = [0]
            for (tok0, Ls) in ((0, L), (L, CTX)):
                N1 = Ls // 128
                Tf = self.dft_tabs(es, "fn%d_" % tok0, N1, False)
                scale = 1.0 / math.sqrt(64.0 * Ls)
                for c0 in range(0, 256, 64):
                    for h0 in (0, 32):
                        kb.dma("sp", Xr[0:N1, h0:h0 + 32, :], self.faT[c0 + h0:c0 + h0 + 32, tok0:tok0 + Ls].rearrange("c (a b) -> a c b", b=128), self.b_faT, b_Xr)
                        kb.dma("sp", Xi[0:N1, h0:h0 + 32, :], self.fbT[c0 + h0:c0 + h0 + 32, tok0:tok0 + Ls].rearrange("c (a b) -> a c b", b=128), self.b_fbT, b_Xi)

                    def cb(g0, G_, W_, psr, bpsr, psi, bpsi, c0=c0, N1=N1, tok0=tok0, Ls=Ls, scale=scale):
                        i = cnt[0] % 2
                        cnt[0] += 1
                        f_t, bf_ = fo[i], b_fo[i]
                        kb.op("act", lambda e: e.activation(out=f_t[:, 0:W_], in_=psr[:, 0:W_], func=AF.Copy, scale=scale), reads=[bpsr], writes=[bf_])
                        kb.dma("sp", self.fT[c0 + g0:c0 + g0 + G_, tok0:tok0 + Ls].rearrange("c (a b) -> a c b", b=N1),
                               f_t[:, 0:W_].rearrange("p (g k) -> p g k", g=G_), bf_, self.b_fT)
                    self.dft2(Dd, Tf, Xr, Xi, [b_Xr, b_Xi], N1, False, 128, cb)
            kb.end_phase()
        with ExitStack() as es:
            wf = self.sb(es, "fl_w", [128, 2, 256], BF16)
            b_wf = kb.buf("wf", dma=True)
            kb.dma("pool", wf[:], self.w["fnet_w"][l].rearrange("(k p) n -> p k n", p=128), None, b_wf)
            bias = self.sb(es, "fl_b", [128, 2], F32)
            b_bias = kb.buf("bias", dma=True)
            kb.dma("sp", bias[:], self.w["fnet_b"][l:l + 1, :].rearrange("o (c p) -> p (o c)", p=128), None, b_bias, allow_slow_non_contiguous=True)
            ft = [self.sb(es, "fl_f%d" % i, [128, 2, 512], BF16) for i in range(2)]
            b_ft = [kb.buf("f%d" % i, dma=True) for i in range(2)]
            ot = [self.sb(es, "fl_o%d" % i, [128, 2, 512], F32) for i in range(2)]
            b_ot = [kb.buf("o%d" % i) for i in range(2)]
            pst = [self.ps(es, "fl_ps%d" % i, [128, 512]) for i in range(2)]
            b_ps = [kb.buf("ps%d" % i) for i in range(2)]
            for it, (t0, n, isc) in enumerate(self.tiles()):
                f_t, bf_ = ft[it % 2], b_ft[it % 2]
                o_t, bo = ot[it % 2], b_ot[it % 2]
                kb.dma("sp", f_t[:, :, 0:n], self.fT.rearrange("(c p) t -> p c t", p=128)[:, :, t0:t0 + n], self.b_fT, bf_)
                for oc in range(2):
                    p_t, bp = pst[oc], b_ps[oc]
                    kb.mm(p_t[:, 0:n], [(wf[:, k, oc * 128:(oc + 1) * 128], f_t[:, k, 0:n]) for k in range(2)], reads=[b_wf, bf_], writes=[bp])
                    kb.op("act", lambda e: e.activation(out=o_t[:, oc, 0:n], in_=p_t[:, 0:n], func=AF.Identity, bias=bias[:, oc:oc + 1]),
                          reads=[bp, b_bias], writes=[bo])
                kb.dma("sp", self.mixT.rearrange("(c p) t -> p c t", p=128)[:, 6:8, t0:t0 + n], o_t[:, :, 0:n], bo, self.b_mixT)
            kb.end_phase()
    def load_vec8(self, es, name, src_row, b):
        t = self.sb(es, name, [128, NCH], F32)
        self.kb.dma("sp", t[:], src_row.rearrange("o (c p) -> p (o c)", p=128), None, b, allow_slow_non_contiguous=True)
        return t

    def phase_merge(self, l, xin, b_xin, xout, b_xout):
        kb = self.kb
        with ExitStack() as es:
            C = self.load_consts_common(es, "p5_")
            S = self.ln_alloc(es, "p5_", 512)
            ones_g = self.sb(es, "p5_onesg", [128, 128], BF16)
            kb.op("dve", lambda e: e.memset(ones_g[:], 1.0 / 256.0), writes=[C["b"]])
            wo = self.sb(es, "p5_wo", [128, NCH, D], BF16)
            b_wo = kb.buf("wo", dma=True)
            for k in range(NCH):
                kb.dma("pool", wo[:, k, :], self.w["w_out"][l, k * 128:(k + 1) * 128, :], None, b_wo)
            b_v = kb.buf("vecs", dma=True)
            gout = self.load_vec8(es, "p5_gout", self.w["out_norm_g"][l:l + 1, :], b_v)
            bout = self.load_vec8(es, "p5_bout", self.w["b_out"][l:l + 1, :], b_v)
            lng = self.load_vec8(es, "p5_lng", self.w["ln1_g"][l:l + 1, :], b_v)
            lnb = self.load_vec8(es, "p5_lnb", self.w["ln1_b"][l:l + 1, :], b_v)
            gb = self.sb(es, "p5_gb", [128, NCH, 2], F32)
            for j in range(2):
                kb.op("dve", lambda e: e.tensor_tensor(out=gb[:, :, j], in0=bout[:], in1=self.modT[:, l, 16:24, j], op=ALU.mult),
                      reads=[b_v, self.b_modT], writes=[b_v])
            mt = [self.sb(es, "p5_m%d" % i, [128, NCH, 512], F32) for i in range(2)]
            b_mt = [kb.buf("m%d" % i, dma=True) for i in range(2)]
            xt = [self.sb(es, "p5_x%d" % i, [128, NCH, 512], F32) for i in range(2)]
            b_xt = [kb.buf("x%d" % i, dma=True) for i in range(2)]
            sqms = [self.sb(es, "p5_sqm%d" % i, [128, NCH, 512], BF16) for i in range(2)]
            b_sqms = [kb.buf("sqm%d" % i) for i in range(2)]
            psg = self.ps(es, "p5_psg", [128, 4, 512])
            b_psg = kb.buf("psg")
            rg = self.sb(es, "p5_rg", [128, 4, 512], F32)
            b_rg = kb.buf("rg")
            mbs = [self.sb(es, "p5_mb%d" % i, [128, NCH, 512], BF16) for i in range(2)]
            b_mbs = [kb.buf("mb%d" % i) for i in range(2)]
            psy = [self.ps(es, "p5_psy%d" % i, [128, 512]) for i in range(2)]
            b_psy = [kb.buf("psy%d" % i) for i in range(2)]
            rs = [self.sb(es, "p5_r%d" % i, [128, NCH, 512], F32) for i in range(2)]
            b_rs = [kb.buf("r%d" % i) for i in range(2)]
            eps_c = C["eps"]
            tiles = self.tiles()

            def front(it, stage):
                t0, n, isc = tiles[it]
                j = 1 if isc else 0
                m_t, bm = mt[it % 2], b_mt[it % 2]
                x_t, bx = xt[it % 2], b_xt[it % 2]
                sqm, b_sqm = sqms[it % 2], b_sqms[it % 2]
                mb, b_mb = mbs[it % 2], b_mbs[it % 2]
                r, b_r = rs[it % 2], b_rs[it % 2]
                if stage == 0:
                    kb.dma("sp", m_t[:, :, 0:n], self.mixT.rearrange("(c p) t -> p c t", p=128)[:, :, t0:t0 + n], self.b_mixT, bm)
                    kb.dma("sp", x_t[:, :, 0:n], xin.rearrange("(c p) t -> p c t", p=128)[:, :, t0:t0 + n], b_xin, bx)
                elif stage == 1:
                    kb.op("act", lambda e: e.activation(out=sqm[:, :, 0:n], in_=m_t[:, :, 0:n], func=AF.Square), reads=[bm], writes=[b_sqm])
                    for g in range(4):
                        kb.mm(psg[:, g, 0:n], [(ones_g[:], sqm[:, 2 * g, 0:n]), (ones_g[:], sqm[:, 2 * g + 1, 0:n])], reads=[b_sqm, C["b"]], writes=[b_psg])
                    kb.op("act", lambda e: e.activation(out=rg[:, :, 0:n], in_=psg[:, :, 0:n], func=AF.Ln, bias=eps_c[:, 0:1]), reads=[b_psg, C["b"]], writes=[b_rg])
                    kb.op("act", lambda e: e.activation(out=rg[:, :, 0:n], in_=rg[:, :, 0:n], func=AF.Exp, scale=-0.5), reads=[b_rg], writes=[b_rg])
                elif stage == 2:
                    for g in range(4):
                        kb.op("dve", lambda e: e.tensor_tensor(out=m_t[:, 2 * g:2 * g + 2, 0:n], in0=m_t[:, 2 * g:2 * g + 2, 0:n],
                                                               in1=rg[:, g, 0:n].unsqueeze(1).broadcast_to([128, 2, n]), op=ALU.mult),
                              reads=[bm, b_rg], writes=[bm])
                    for k in range(NCH):
                        kb.op("act" if k % 2 == 0 else "pool",
                              (lambda e: e.activation(out=mb[:, k, 0:n], in_=m_t[:, k, 0:n], func=AF.Copy, scale=gout[:, k:k + 1])) if k % 2 == 0 else
                              (lambda e: e.tensor_scalar(out=mb[:, k, 0:n], in0=m_t[:, k, 0:n], scalar1=gout[:, k:k + 1], scalar2=None, op0=ALU.mult)),
                              reads=[bm, b_v], writes=[b_mb])
                else:
                    for oc in range(NCH):
                        p_t, bp = psy[oc % 2], b_psy[oc % 2]
                        kb.mm(p_t[:, 0:n], [(wo[:, k, oc * 128:(oc + 1) * 128], mb[:, k, 0:n]) for k in range(NCH)], reads=[b_wo, b_mb], writes=[bp])
                        kb.op("act", lambda e: e.activation(out=r[:, oc, 0:n], in_=p_t[:, 0:n], func=AF.Identity,
                                                            scale=self.modT[:, l, 16 + oc, j:j + 1], bias=gb[:, oc, j:j + 1]),
                              reads=[bp, self.b_modT, b_v], writes=[b_r])
                        kb.op("dve", lambda e: e.scalar_tensor_tensor(out=r[:, oc, 0:n], in0=x_t[:, oc, 0:n], scalar=float(self.alpha),
                                                                      in1=r[:, oc, 0:n], op0=ALU.mult, op1=ALU.add),
                              reads=[bx, b_r], writes=[b_r])

            def back(it, stage):
                t0, n, isc = tiles[it]
                r, b_r = rs[it % 2], b_rs[it % 2]
                if stage < 2:
                    self.ln_stage(stage, r, b_r, n, S, C)
                else:
                    for k in range(NCH):
                        kb.op("act", lambda e: e.activation(out=r[:, k, 0:n], in_=S["xc"][:, k, 0:n], func=AF.Identity,
                                                            scale=lng[:, k:k + 1], bias=lnb[:, k:k + 1]),
                              reads=[S["b_xc"], b_v], writes=[b_r])
                    kb.dma("sp", xout.rearrange("(c p) t -> p c t", p=128)[:, :, t0:t0 + n], r[:, :, 0:n], b_r, b_xout)

            nt = len(tiles)
            front(0, 0)
            for it in range(nt + 1):
                if it + 1 < nt:
                    front(it + 1, 0)
                for st in range(3):
                    if it < nt:
                        front(it, st + 1)
                    if it >= 1:
                        back(it - 1, st)
            kb.end_phase()

    def ffn_tiles(self):
        out = []
        for (s0, s1, isc) in ((0, self.L, False), (self.L, self.T, True)):
            t = s0
            while t < s1:
                nv = min(510, s1 - t)
                out.append((t, nv, s0, s1, isc))
                t += nv
        return out

    def phase_wprep(self):
        kb = self.kb
        with ExitStack() as es:
            st = [self.sb(es, "wp_s%d" % i, [128, NCH, 256], BF16) for i in range(3)]
            b_st = [kb.buf("s%d" % i, dma=True) for i in range(3)]
            it = 0
            for l in range(self.depth):
                wv = self.w["ffn_w_up"][l].rearrange("(k p) n -> p k n", p=128)
                for f in range(22):
                    sz = 128 if f < 21 else 64
                    s_t, bs = st[it % 3], b_st[it % 3]
                    it += 1
                    kb.dma("pool", s_t[:, :, 0:sz], wv[:, :, f * 128:f * 128 + sz], None, bs)
                    kb.dma("pool", s_t[:, :, 128:128 + sz], wv[:, :, DFF + f * 128:DFF + f * 128 + sz], None, bs)
                    kb.dma("sp", self.wup[l, f], s_t[:], bs, self.b_wup)
            kb.end_phase()

    def ln_stage(self, stage, X, bX, n, S, C):
        kb = self.kb
        xc, sq, rstd, psm, psv = S["xc"], S["sq"], S["rstd"], S["psm"], S["psv"]
        if stage == 0:
            kb.mm(psm[:, 0:n], [(C["ones_f"][:], X[:, k, 0:n]) for k in range(NCH)], reads=[bX, C["b"]], writes=[S["b_psm"]])
            kb.op("dve", lambda e: e.tensor_tensor(out=xc[:, :, 0:n], in0=X[:, :, 0:n],
                                                   in1=psm[:, 0:n].unsqueeze(1).broadcast_to([128, NCH, n]), op=ALU.subtract),
                  reads=[bX, S["b_psm"]], writes=[S["b_xc"]])
            kb.op("act", lambda e: e.activation(out=sq[:, :, 0:n], in_=xc[:, :, 0:n], func=AF.Square), reads=[S["b_xc"]], writes=[S["b_sq"]])
        elif stage == 1:
            kb.mm(psv[:, 0:n], [(C["ones_b"][:], sq[:, k, 0:n]) for k in range(NCH)], reads=[S["b_sq"], C["b"]], writes=[S["b_psv"]])
            kb.op("act", lambda e: e.activation(out=rstd[:, 0:n], in_=psv[:, 0:n], func=AF.Ln, bias=C["eps"][:, 0:1]),
                  reads=[S["b_psv"], C["b"]], writes=[S["b_rstd"]])
            kb.op("act", lambda e: e.activation(out=rstd[:, 0:n], in_=rstd[:, 0:n], func=AF.Exp, scale=-0.5), reads=[S["b_rstd"]], writes=[S["b_rstd"]])
            kb.op("dve", lambda e: e.tensor_tensor(out=xc[:, :, 0:n], in0=xc[:, :, 0:n],
                                                   in1=rstd[:, 0:n].unsqueeze(1).broadcast_to([128, NCH, n]), op=ALU.mult),
                  reads=[S["b_xc"], S["b_rstd"]], writes=[S["b_xc"]])

    def phase_ffn(self, l, xin, b_xin, xout, b_xout):
        kb = self.kb
        with ExitStack() as es:
            C = self.load_consts_common(es, "p6_")
            S = self.ln_alloc(es, "p6_", 512)
            wd = self.sb(es, "p6_wd", [128, 22, D], BF16)
            b_wd = kb.buf("wd", dma=True)
            for f in range(22):
                sz = 128 if f < 21 else 64
                kb.dma("pool", wd[0:sz, f, :], self.w["ffn_w_down"][l, f * 128:f * 128 + sz, :], None, b_wd)
            b_v = kb.buf("vecs", dma=True)
            bdn = self.load_vec8(es, "p6_bdn", self.w["ffn_b_down"][l:l + 1, :], b_v)
            lng = self.load_vec8(es, "p6_lng", self.w["ln2_g"][l:l + 1, :], b_v)
            lnb = self.load_vec8(es, "p6_lnb", self.w["ln2_b"][l:l + 1, :], b_v)
            hv = self.sb(es, "p6_hv", [128, 44, 8], F32)
            kb.op("dve", lambda e: e.memset(hv[:], 0.0), writes=[b_v])
            srcs = [self.w["ffn_b_up"][l:l + 1, :], self.w["ffn_conv_w"][l, 0:1, :], self.w["ffn_conv_w"][l, 1:2, :],
                    self.w["ffn_conv_w"][l, 2:3, :], self.w["ffn_conv_b"][l:l + 1, :]]
            for vi, src in enumerate(srcs):
                for half in range(2):
                    base = half * DFF
                    kb.dma("sp", hv[:, 22 * half:22 * half + 21, vi:vi + 1],
                           src[:, base:base + 21 * 128].rearrange("o (c p) -> p c o", p=128), None, b_v, allow_slow_non_contiguous=True)
                    kb.dma("sp", hv[0:64, 22 * half + 21:22 * half + 22, vi:vi + 1],
                           src[:, base + 21 * 128:base + DFF].rearrange("o (c p) -> p c o", p=64), None, b_v, allow_slow_non_contiguous=True)
            kb.op("dve", lambda e: e.tensor_tensor(out=hv[:, :, 5:6], in0=hv[:, :, 1:2], in1=hv[:, :, 2:3], op=ALU.add), reads=[b_v], writes=[b_v])
            kb.op("dve", lambda e: e.tensor_tensor(out=hv[:, :, 5:6], in0=hv[:, :, 5:6], in1=hv[:, :, 3:4], op=ALU.add), reads=[b_v], writes=[b_v])
            kb.op("dve", lambda e: e.tensor_tensor(out=hv[:, :, 5:6], in0=hv[:, :, 5:6], in1=hv[:, :, 0:1], op=ALU.mult), reads=[b_v], writes=[b_v])
            kb.op("dve", lambda e: e.tensor_tensor(out=hv[:, :, 5:6], in0=hv[:, :, 5:6], in1=hv[:, :, 4:5], op=ALU.add), reads=[b_v], writes=[b_v])
            kb.op("dve", lambda e: e.scalar_tensor_tensor(out=hv[:, :, 6:7], in0=hv[:, :, 1:2], scalar=-1.0, in1=hv[:, :, 0:1], op0=ALU.mult, op1=ALU.mult),
                  reads=[b_v], writes=[b_v])
            kb.op("dve", lambda e: e.scalar_tensor_tensor(out=hv[:, :, 7:8], in0=hv[:, :, 3:4], scalar=-1.0, in1=hv[:, :, 0:1], op0=ALU.mult, op1=ALU.mult),
                  reads=[b_v], writes=[b_v])
            gb = self.sb(es, "p6_gb", [128, NCH, 2], F32)
            for j in range(2):
                kb.op("dve", lambda e: e.tensor_tensor(out=gb[:, :, j], in0=bdn[:], in1=self.modT[:, l, 40:48, j], op=ALU.mult),
                      reads=[b_v, self.b_modT], writes=[b_v])
            xt = [self.sb(es, "p6_x%d" % i, [128, NCH, 512], F32) for i in range(2)]
            b_xt = [kb.buf("x%d" % i, dma=True) for i in range(2)]
            hTs = [self.sb(es, "p6_hT%d" % i, [128, NCH, 512], BF16) for i in range(2)]
            b_hTs = [kb.buf("hT%d" % i) for i in range(2)]
            wu = [self.sb(es, "p6_wu%d" % i, [128, NCH, 256], BF16) for i in range(3)]
            b_wu = [kb.buf("wu%d" % i, dma=True) for i in range(3)]
            psu = [self.ps(es, "p6_psu%d" % i, [128, 2, 512]) for i in range(3)]
            b_psu = [kb.buf("psu%d" % i) for i in range(3)]
            cc = [self.sb(es, "p6_c%d" % i, [128, 2, 512], F32) for i in range(3)]
            b_cc = [kb.buf("c%d" % i) for i in range(3)]
            sl = [self.sb(es, "p6_sl%d" % i, [128, 512], F32) for i in range(2)]
            b_sl = [kb.buf("sl%d" % i) for i in range(2)]
            pT = self.sb(es, "p6_pT", [128, 22, 512], BF16)
            b_pT = kb.buf("pT")
            psy = [S["psm"], S["psv"]]
            b_psy = [S["b_psm"], S["b_psv"]]
            r = self.sb(es, "p6_r", [128, NCH, 512], F32)
            b_r = kb.buf("r")
            xv = xin.rearrange("(c p) t -> p c t", p=128)
            tiles = self.ffn_tiles()

            def load_x(it):
                t0, nv, s0, s1, isc = tiles[it]
                x_t, bx = xt[it % 2], b_xt[it % 2]
                lo, hi = max(s0, t0 - 1), min(s1, t0 + nv + 1)
                c_lo = lo - (t0 - 1)
                kb.op("pool", lambda e: e.memset(x_t[:, :, 0:nv + 2], 0.0), writes=[bx])
                kb.dma("sp", x_t[:, :, c_lo:c_lo + hi - lo], xv[:, :, lo:hi], b_xin, bx)

            def prologue(it, stage):
                t0, nv, s0, s1, isc = tiles[it]
                j = 1 if isc else 0
                x_t, bx = xt[it % 2], b_xt[it % 2]
                hT, b_hT = hTs[it % 2], b_hTs[it % 2]
                ncol = nv + 2
                if stage < 2:
                    self.ln_stage(stage, x_t, bx, ncol, S, C)
                    return
                for k in range(NCH):
                    kb.op("act", lambda e: e.activation(out=hT[:, k, 0:ncol], in_=S["xc"][:, k, 0:ncol], func=AF.Identity,
                                                        scale=self.modT[:, l, 32 + k, j:j + 1], bias=self.modT[:, l, 24 + k, j:j + 1]),
                          reads=[S["b_xc"], self.b_modT], writes=[b_hT])
                if t0 == s0:
                    kb.op("pool", lambda e: e.memset(hT[:, :, 0:1], 0.0), writes=[b_hT])
                if t0 + nv == s1:
                    kb.op("pool", lambda e: e.memset(hT[:, :, nv + 1:nv + 2], 0.0), writes=[b_hT])

            def epilogue(ep, stage):
                et0, env = ep
                self.ln_stage(stage, r, b_r, env, S, C)
                if stage == 1:
                    for k in range(NCH):
                        kb.op("act", lambda e: e.activation(out=r[:, k, 0:env], in_=S["xc"][:, k, 0:env], func=AF.Identity,
                                                            scale=lng[:, k:k + 1], bias=lnb[:, k:k + 1]),
                              reads=[S["b_xc"], b_v], writes=[b_r])
                    kb.dma("sp", xout.rearrange("(c p) t -> p c t", p=128)[:, :, et0:et0 + env], r[:, :, 0:env], b_r, b_xout)

            load_x(0)
            for st in range(3):
                prologue(0, st)
            fi = 0
            epi = None
            for it, (t0, nv, s0, s1, isc) in enumerate(tiles):
                j = 1 if isc else 0
                x_t, bx = xt[it % 2], b_xt[it % 2]
                hT, b_hT = hTs[it % 2], b_hTs[it % 2]
                ncol = nv + 2
                has_next = it + 1 < len(tiles)
                if has_next:
                    load_x(it + 1)
                pend = None

                def stage2(pp):
                    pf, psz_, pc_t, pbc = pp
                    s_t, bs = sl[pf % 2], b_sl[pf % 2]
                    kb.op("act", lambda e: e.activation(out=s_t[0:psz_, 0:nv], in_=pc_t[0:psz_, 0, 0:nv], func=AF.Silu), reads=[pbc], writes=[bs])
                    kb.op("pool", lambda e: e.tensor_tensor(out=pT[0:psz_, pf, 0:nv], in0=s_t[0:psz_, 0:nv], in1=pc_t[0:psz_, 1, 0:nv], op=ALU.mult),
                          reads=[bs, pbc], writes=[b_pT])

                for f in range(22):
                    sz = 128 if f < 21 else 64
                    w_t, bw = wu[fi % 3], b_wu[fi % 3]
                    p_t, bp = psu[fi % 3], b_psu[fi % 3]
                    c_t, bc = cc[fi % 3], b_cc[fi % 3]
                    fi += 1
                    kb.dma("sp", w_t[:], self.wup[l, f], self.b_wup, bw)
                    for ag in range(2):
                        kb.mm(p_t[0:sz, ag, 0:ncol], [(w_t[:, k, ag * 128:ag * 128 + sz], hT[:, k, 0:ncol]) for k in range(NCH)],
                              reads=[bw, b_hT], writes=[bp])
                    for ag in range(2):
                        hc = 22 * ag + f
                        kb.op("act", lambda e: e.activation(out=c_t[0:sz, ag, 0:nv], in_=p_t[0:sz, ag, 1:nv + 1], func=AF.Identity,
                                                            scale=hv[0:sz, hc, 2:3], bias=hv[0:sz, hc, 5:6]), reads=[bp, b_v], writes=[bc])
                    for ag in range(2):
                        hc = 22 * ag + f
                        kb.op("dve", lambda e: e.scalar_tensor_tensor(out=c_t[0:sz, ag, 0:nv], in0=p_t[0:sz, ag, 0:nv], scalar=hv[0:sz, hc, 1:2],
                                                                      in1=c_t[0:sz, ag, 0:nv], op0=ALU.mult, op1=ALU.add),
                              reads=[bp, b_v, bc], writes=[bc])
                        kb.op("dve", lambda e: e.scalar_tensor_tensor(out=c_t[0:sz, ag, 0:nv], in0=p_t[0:sz, ag, 2:nv + 2], scalar=hv[0:sz, hc, 3:4],
                                                                      in1=c_t[0:sz, ag, 0:nv], op0=ALU.mult, op1=ALU.add),
                              reads=[bp, b_v, bc], writes=[bc])
                        if t0 == s0:
                            kb.op("pool", lambda e: e.tensor_scalar(out=c_t[0:sz, ag, 0:1], in0=c_t[0:sz, ag, 0:1], scalar1=hv[0:sz, hc, 6:7],
                                                                    scalar2=None, op0=ALU.add), reads=[bc, b_v], writes=[bc])
                        if t0 + nv == s1:
                            kb.op("pool", lambda e: e.tensor_scalar(out=c_t[0:sz, ag, nv - 1:nv], in0=c_t[0:sz, ag, nv - 1:nv], scalar1=hv[0:sz, hc, 7:8],
                                                                    scalar2=None, op0=ALU.add), reads=[bc, b_v], writes=[bc])
                    if pend is not None:
                        stage2(pend)
                    pend = (f, sz, c_t, bc)
                    if has_next and f in (5, 11, 16):
                        prologue(it + 1, (5, 11, 16).index(f))
                    if epi is not None and f in (1, 3):
                        epilogue(epi, 0 if f == 1 else 1)
                        if f == 3:
                            epi = None
                stage2(pend)
                for oc in range(NCH):
                    p_t, bp = psy[oc % 2], b_psy[oc % 2]
                    kb.mm(p_t[:, 0:nv], [(wd[0:(128 if f < 21 else 64), f, oc * 128:(oc + 1) * 128], pT[0:(128 if f < 21 else 64), f, 0:nv])
                                         for f in range(22)], reads=[b_wd, b_pT], writes=[bp])
                    kb.op("act", lambda e: e.activation(out=r[:, oc, 0:nv], in_=p_t[:, 0:nv], func=AF.Identity,
                                                        scale=self.modT[:, l, 40 + oc, j:j + 1], bias=gb[:, oc, j:j + 1]),
                          reads=[bp, self.b_modT, b_v], writes=[b_r])
                    kb.op("dve", lambda e: e.scalar_tensor_tensor(out=r[:, oc, 0:nv], in0=x_t[:, oc, 1:nv + 1], scalar=float(self.alpha),
                                                                  in1=r[:, oc, 0:nv], op0=ALU.mult, op1=ALU.add),
                          reads=[bx, b_r], writes=[b_r])
                epi = (t0, nv)
            if epi is not None:
                epilogue(epi, 0)
                epilogue(epi, 1)
            kb.end_phase()
    def build(self, stop_after=None):
        nc = self.nc
        L, T, depth = self.L, self.T, self.depth
        nc.allow_low_precision("bf16 matmul operands, fp32 accumulation")
        self.x_in = self.din("x", [L, D])
        self.ctx_in = self.din("ctx", [CTX, D])
        self.c_in = self.din("c", [1, D])
        self.cctx_in = self.din("c_ctx", [1, D])
        self.w = {}
        for name, shape in WEIGHT_SHAPES.items():
            self.w[name] = self.din(name, [depth] + list(shape))
        self.c_ident_f = self.din("c_ident_f", [128, 128])
        self.c_ident_b = self.din("c_ident_b", [128, 128], BF16)
        self.c_bdcs = self.din("c_bdcs", [128, 2, 128], BF16)
        self.c_rope = self.din("c_rope", [T, 2, 32])
        self.c_wmask = self.din("c_wmask", [128, 6, 512], BF16)
        NT = 2 * L + 512 + 1
        self.c_hz = self.din("c_hz", [HY_EMB, NT])
        self.c_hdec = self.din("c_hdec", [256, NT])
        self.c_dft = {}
        for (N1, inv) in dft_plans(L):
            key = "%d_%s" % (N1, "i" if inv else "f")
            KA, MA, FA = (128, N1, 128) if inv else (N1, 128, N1)
            self.c_dft["tabA_" + key] = self.din("c_tabA_" + key, [KA, 2, 2 * FA], BF16)
            self.c_dft["tw_" + key] = self.din("c_tw_" + key, [MA, 2, FA])
            self.c_dft["tabB_" + key] = self.din("c_tabB_" + key, [MA, 4, MA], BF16)
        self.out = self.nc.dram_tensor("out", [L, D], F32, kind="ExternalOutput").ap()
        self.xA = self.dscr("xA", [D, T])
        self.xB = self.dscr("xB", [D, T])
        self.modv = self.dscr("modv", [depth, 2, 6 * D])
        self.QKa = self.dscr("QKa", [384, T], BF16)
        self.QKb = self.dscr("QKb", [384, T], BF16)
        self.Va = self.dscr("Va", [T, 128], BF16)
        self.Vb = self.dscr("Vb", [T, 128], BF16)
        self.zhT = self.dscr("zhT", [768, T + 4], BF16)
        self.faT = self.dscr("faT", [256, T], BF16)
        self.fbT = self.dscr("fbT", [256, T], BF16)
        self.mixT = self.dscr("mixT", [D, T])
        self.kfT = self.dscr("kfT", [256, 2 * L + 512], BF16)
        self.vxT = self.dscr("vxT", [256, T], BF16)
        self.x0T = self.dscr("x0T", [256, T], BF16)
        self.fT = self.dscr("fT", [256, T], BF16)
        self.wup = self.dscr("wup", [depth, 22, 128, NCH, 256], BF16)
        self.alpha = 8.0 ** 0.25
        self.Khat = self.dscr("Khat", [2, 128, 256 * (2 * L // 128)])
        self.Khatc = self.dscr("Khatc", [2, 128, 256 * 4])
        with ExitStack() as es:
            self.kb = kb = KB(nc, es)
            for nm in ("xA", "xB", "out", "modv", "QKa", "QKb", "Va", "Vb", "zhT", "faT", "fbT", "mixT",
                       "kfT", "vxT", "x0T", "fT", "Khat", "Khatc", "wup"):
                setattr(self, "b_" + nm, kb.buf(nm, dma=True, persistent=True))
            self.modT = self.sb(es, "modT", [128, depth, 48, 2], F32)
            self.b_modT = kb.buf("modT", dma=True, persistent=True)
            phases = []
            phases.append(("zinit", self.phase_zinit))
            phases.append(("tin", lambda: self.phase_tin(self.x_in, self.ctx_in, self.xA, self.b_xA)))
            phases.append(("mod", self.phase_mod))
            phases.append(("wprep", self.phase_wprep))
            for l in range(depth):
                phases.append(("proj%d" % l, lambda l=l: self.phase_proj(l, self.xA, self.b_xA)))
                phases.append(("attna%d" % l, lambda l=l: self.phase_attn(l, "a")))
                phases.append(("attnb%d" % l, lambda l=l: self.phase_attn(l, "b")))
                phases.append(("hyf%d" % l, lambda l=l: self.phase_hyena_filter(l)))
                phases.append(("hyp%d" % l, lambda l=l: self.phase_hyena_prep(l)))
                phases.append(("hyc%d" % l, lambda l=l: self.phase_hyena_conv(l)))
                phases.append(("fnet%d" % l, lambda l=l: self.phase_fnet(l)))
                phases.append(("merge%d" % l, lambda l=l: self.phase_merge(l, self.xA, self.b_xA, self.xB, self.b_xB)))
                phases.append(("ffn%d" % l, lambda l=l: self.phase_ffn(l, self.xB, self.b_xB, self.xA, self.b_xA)))
            phases.append(("tout", lambda: self.phase_tout(self.xA, self.b_xA, self.out, self.b_out)))
            for name, fn in phases:
                fn()
                if stop_after is not None and name == stop_after:
                    break
            kb.barrier()
        return nc


WEIGHT_SHAPES = {
    "w_ada": (D, 6 * D), "b_ada": (6 * D,), "w_in": (D, DPROJ), "sink_a": (4,), "q_norm_g": (HD,), "k_norm_g": (HD,),
    "hy_conv_w": (3, 768), "hy_conv_b": (768,), "hy_f_w1": (HY_EMB, HY_HID), "hy_f_b1": (HY_HID,), "hy_f_freq": (HY_HID,),
    "hy_f_w2": (HY_HID, HY_HID), "hy_f_b2": (HY_HID,), "hy_f_w3": (HY_HID, 512), "hy_skip": (256,),
    "fnet_w": (256, 256), "fnet_b": (256,), "out_norm_g": (D,), "w_out": (D, D), "b_out": (D,),
    "ln1_g": (D,), "ln1_b": (D,), "ffn_w_up": (D, 2 * DFF), "ffn_b_up": (2 * DFF,), "ffn_conv_w": (3, 2 * DFF),
    "ffn_conv_b": (2 * DFF,), "ffn_w_down": (DFF, D), "ffn_b_down": (D,), "ln2_g": (D,), "ln2_b": (D,),
}


def dft_plans(L):
    s = set()
    for N1 in (2 * L // 128, 2 * CTX // 128):
        s.add((N1, False))
        s.add((N1, True))
    for N1 in (L // 128, CTX // 128):
        s.add((N1, False))
    return sorted(s)


def dft_tables(N1, inv):
    bf = ml_dtypes.bfloat16
    N = N1 * 128
    if not inv:
        n1 = np.arange(N1)[:, None]
        k1 = np.arange(N1)[None, :]
        a = 2 * np.pi * ((n1 * k1) % N1) / N1
        tabA = np.stack([np.concatenate([np.cos(a), -np.sin(a)], 1), np.concatenate([np.sin(a), np.cos(a)], 1)], 1)
        n2 = np.arange(128)[:, None]
        t = 2 * np.pi * ((n2 * k1) % N) / N
        tw = np.stack([np.cos(t), np.sin(t)], 1)
        k2 = np.arange(128)[None, :]
        b = 2 * np.pi * ((n2 * k2) % 128) / 128
        tabB = np.stack([np.cos(b), np.sin(b), -np.sin(b), np.cos(b)], 1)
    else:
        k2 = np.arange(128)[:, None]
        nl = np.arange(128)[None, :]
        a = 2 * np.pi * ((k2 * nl) % 128) / 128
        tabA = np.stack([np.concatenate([np.cos(a), np.sin(a)], 1), np.concatenate([-np.sin(a), np.cos(a)], 1)], 1)
        k1 = np.arange(N1)[:, None]
        t = 2 * np.pi * ((k1 * nl) % N) / N
        tw = np.stack([np.cos(t), -np.sin(t)], 1)
        nh = np.arange(N1)[None, :]
        b = 2 * np.pi * ((k1 * nh) % N1) / N1
        tabB = np.stack([np.cos(b), -np.sin(b), np.sin(b), np.cos(b)], 1)
    return tabA.astype(bf), tw.astype(np.float32), tabB.astype(bf)


def hyena_tables(L):
    def feats(Ls, pos):
        t = pos.astype(np.float64)
        t_norm = t / max(Ls - 1, 1)
        bands = np.linspace(1e-4, HY_BANDS - 1, HY_BANDS)
        ang = 2.0 * np.pi * t[:, None] * bands[None, :] / Ls
        z = np.concatenate([t_norm[:, None], np.cos(ang), -np.sin(ang)], axis=-1)
        min_decay = math.log(1e-2) / 1.5
        max_decay = math.log(1e-2) / 0.3
        deltas = np.abs(np.linspace(min_decay, max_decay, 256))
        dec = np.exp(-t_norm[:, None] * deltas[None, :])
        return z, dec
    zs, ds = [], []
    for Ls in (L, CTX):
        z, d = feats(Ls, np.arange(Ls))
        zs.append(z); ds.append(d)
        pos = Ls - np.arange(Ls)
        pos[0] = 0
        z, d = feats(Ls, pos)
        d[0] = 0.0
        zs.append(z); ds.append(d)
    z, d = feats(L, np.arange(1))
    zs.append(z); ds.append(d)
    order = [0, 1, 2, 3, 4]
    Z = np.concatenate([zs[i] for i in order], 0).T
    Dd = np.concatenate([ds[i] for i in order], 0).T
    return np.ascontiguousarray(Z.astype(np.float32)), np.ascontiguousarray(Dd.astype(np.float32))


def host_consts(L):
    T = L + CTX
    bf = ml_dtypes.bfloat16
    c = {}
    c["c_ident_f"] = np.eye(128, dtype=np.float32)
    c["c_ident_b"] = np.eye(128).astype(bf)
    rows = L // GRID_W
    row = np.broadcast_to(np.arange(rows)[:, None], (rows, GRID_W)).reshape(-1).astype(np.float32)
    col = np.broadcast_to(np.arange(GRID_W)[None, :], (rows, GRID_W)).reshape(-1).astype(np.float32)
    half = HD // 2
    inv = (np.float32(ROPE_THETA) ** (-np.arange(0, half, 2, dtype=np.float32) / np.float32(half))).astype(np.float32)
    ang = np.concatenate([row[:, None] * inv, col[:, None] * inv], axis=-1).astype(np.float32)
    rope = np.zeros((T, 2, 32), np.float32)
    rope[:L, 0] = np.cos(ang)
    rope[:L, 1] = np.sin(ang)
    rope[L:, 0] = 1.0
    c["c_rope"] = rope
    jj = np.arange(64)
    a64 = 2.0 * np.pi * np.outer(jj, jj) / 64.0
    bd = np.zeros((128, 2, 128), np.float64)
    for g in range(2):
        bd[g * 64:(g + 1) * 64, 0, g * 64:(g + 1) * 64] = np.cos(a64)
        bd[g * 64:(g + 1) * 64, 1, g * 64:(g + 1) * 64] = -np.sin(a64)
    c["c_bdcs"] = bd.astype(bf)
    pp = np.arange(128)[:, None, None]
    rr = (np.arange(6) - 1)[None, :, None]
    ff = np.arange(512)[None, None, :]
    c["c_wmask"] = (np.abs(rr * 128 + pp - ff) <= 128).astype(np.float32).astype(bf)
    c["c_hz"], c["c_hdec"] = hyena_tables(L)
    for (N1, inv) in dft_plans(L):
        key = "%d_%s" % (N1, "i" if inv else "f")
        c["c_tabA_" + key], c["c_tw_" + key], c["c_tabB_" + key] = dft_tables(N1, inv)
    return c


def run(inputs, L=8192, depth=4, dbg=(), n_cores=8, trace=False, stop_after=None):
    prog = Prog(L, depth, dbg)
    nc = prog.build(stop_after=stop_after)
    consts = host_consts(L)
    in_maps = []
    for i in range(n_cores):
        m = dict(consts)
        m["x"] = np.ascontiguousarray(inputs["x"][i])
        m["ctx"] = np.ascontiguousarray(inputs["ctx"][i])
        m["c"] = np.ascontiguousarray(inputs["c"][i:i + 1])
        m["c_ctx"] = np.ascontiguousarray(inputs["c_ctx"]).reshape(1, D)
        for name in WEIGHT_SHAPES:
            m[name] = np.ascontiguousarray(inputs[name][:depth])
        in_maps.append(m)
    res = run_bass_kernel_spmd(nc, in_maps, core_ids=list(range(n_cores)), trace=trace)
    return res, prog


def kernel(**inputs):
    res, prog = run(inputs)
    return np.stack([r["out"] for r in res.results], axis=0)
```

```python
import math
from contextlib import ExitStack
import numpy as np
import ml_dtypes
import concourse.bass as bass
import concourse.mybir as mybir
from concourse.bass_utils import run_bass_kernel_spmd

F32 = mybir.dt.float32
BF16 = mybir.dt.bfloat16
AF = mybir.ActivationFunctionType
ALU = mybir.AluOpType
AX = mybir.AxisListType

D = 1024
NCH = 8
CTX = 256
GRID_W = 64
HD = 64
DFF = 2752
DPROJ = 2048
LN_EPS = 1e-6
ROPE_THETA = 10000.0
HY_BANDS = 8
HY_EMB = 17
HY_HID = 64


class Sem:
    __slots__ = ("h", "val")

    def __init__(self, h):
        self.h = h
        self.val = 0


class Eng:
    def __init__(self, name, eng, sem):
        self.name = name
        self.eng = eng
        self.sem = sem
        self.known = {}


class Buf:
    __slots__ = ("name", "w", "r", "dsem", "dram", "ssem")

    def __init__(self, name, dsem=None, dram=False):
        self.ssem = None
        self.name = name
        self.w = {}
        self.r = {}
        self.dsem = dsem
        self.dram = dram


class KB:
    def __init__(self, nc, es, n_dma_sems=94):
        self.nc = nc
        self.E = {}
        for name, eng in (("pe", nc.tensor), ("act", nc.scalar), ("dve", nc.vector),
                          ("pool", nc.gpsimd), ("sp", nc.sync)):
            s = Sem(es.enter_context(nc.semaphore("s_" + name)))
            self.E[name] = Eng(name, eng, s)
        self.free_sems = [Sem(es.enter_context(nc.semaphore("d%d" % i))) for i in range(n_dma_sems - 12)]
        self.free_sw = [Sem(es.enter_context(nc.semaphore("w%d" % i))) for i in range(12)]
        self.all_sems = [e.sem for e in self.E.values()] + list(self.free_sems) + list(self.free_sw)
        self.phase_bufs = []
        self.n_inst = 0

    def buf(self, name, dma=False, persistent=False):
        b = Buf(name, None, dram=persistent and dma and name != "modT")
        if dma and not b.dram:
            b.dsem = self.free_sems.pop()
        if not persistent:
            self.phase_bufs.append(b)
        return b

    def _waits(self, E, reads, writes, skip=None, skip_waw=False):
        deps = {}
        for b in reads:
            for s, v in b.w.items():
                if deps.get(s, 0) < v:
                    deps[s] = v
        for b in writes:
            if not skip_waw:
                for s, v in b.w.items():
                    if s is skip:
                        continue
                    if deps.get(s, 0) < v:
                        deps[s] = v
            for s, v in b.r.items():
                if deps.get(s, 0) < v:
                    deps[s] = v
        for s, v in deps.items():
            if E.name == "pe" and s is E.sem:
                continue
            if E.known.get(s, 0) < v:
                E.eng.wait_ge(s.h, v)
                E.known[s] = v
                self.n_inst += 1

    def _record(self, ev, reads, writes):
        s, v = ev
        for b in reads:
            if b.r.get(s, 0) < v:
                b.r[s] = v
        for b in writes:
            b.w = {s: v}
            b.r = {}

    def op(self, en, fn, reads=(), writes=()):
        E = self.E[en]
        self._waits(E, reads, writes)
        ins = fn(E.eng)
        E.sem.val += 1
        ins.then_inc(E.sem.h, 1)
        self.n_inst += 1
        self._record((E.sem, E.sem.val), reads, writes)

    def mm(self, out_ap, pairs, reads=(), writes=(), transpose=False):
        E = self.E["pe"]
        self._waits(E, reads, writes)
        n = len(pairs)
        ins = None
        for i, (l, r) in enumerate(pairs):
            if transpose:
                ins = E.eng.transpose(out_ap, l, r)
            else:
                ins = E.eng.matmul(out_ap, l, r, start=(i == 0), stop=(i == n - 1))
            self.n_inst += 1
        E.sem.val += 1
        ins.then_inc(E.sem.h, 1)
        self._record((E.sem, E.sem.val), reads, writes)

    def mm1(self, out_ap, l, r, start, stop, reads=(), writes=()):
        E = self.E["pe"]
        self._waits(E, reads, writes)
        ins = E.eng.matmul(out_ap, l, r, start=start, stop=stop)
        self.n_inst += 1
        E.sem.val += 1
        ins.then_inc(E.sem.h, 1)
        self._record((E.sem, E.sem.val), reads, writes)

    def dma(self, qn, out_ap, in_ap, src, dst, **kw):
        E = self.E[qn]
        if dst.dram:
            assert src is not None and not src.dram, (dst.name,)
            if src.dsem is None:
                src.dsem = self.free_sems.pop()
            ds = src.dsem
            self._waits(E, [src], [dst], skip_waw=True)
        elif qn == "pool":
            if dst.ssem is None:
                dst.ssem = self.free_sw.pop()
            ds = dst.ssem
            self._waits(E, [src] if src is not None else [], [dst], skip=ds)
        else:
            ds = dst.dsem
            assert ds is not None, dst.name
            self._waits(E, [src] if src is not None else [], [dst], skip=ds)
        ins = E.eng.dma_start(out=out_ap, in_=in_ap, **kw)
        ds.val += 16
        ins.then_inc(ds.h, 16)
        self.n_inst += 1
        if src is not None:
            if src.r.get(ds, 0) < ds.val:
                src.r[ds] = ds.val
        if dst.dram:
            dst.w[ds] = ds.val
            dst.r = {}
        else:
            dst.w = {ds: ds.val}
            dst.r = {}

    def barrier(self):
        for E in self.E.values():
            for s in self.all_sems:
                if s.val > 0 and E.known.get(s, 0) < s.val:
                    if E.name == "pe" and s is E.sem:
                        continue
                    E.eng.wait_ge(s.h, s.val)
                    E.known[s] = s.val
                    self.n_inst += 1

    def end_phase(self):
        self.barrier()
        for b in self.phase_bufs:
            if b.dsem is not None:
                self.free_sems.append(b.dsem)
            if b.ssem is not None:
                self.free_sw.append(b.ssem)
        self.phase_bufs = []


class Prog:
    def __init__(self, L, depth, dbg=()):
        self.L = L
        self.T = L + CTX
        self.depth = depth
        self.dbg = set(dbg)
        self.nc = bass.Bass("TRN2", target_bir_lowering=False)
        self.consts = {}

    def din(self, name, shape, dt=F32):
        return self.nc.dram_tensor(name, list(shape), dt, kind="ExternalInput").ap()

    def dscr(self, name, shape, dt=F32):
        kind = "ExternalOutput" if name in self.dbg else "Internal"
        return self.nc.dram_tensor(name, list(shape), dt, kind=kind).ap()

    def tiles(self):
        out = [(t0, 512, False) for t0 in range(0, self.L, 512)]
        out.append((self.L, CTX, True))
        return out

    def sb(self, es, name, shape, dt):
        self._uid = getattr(self, "_uid", 0) + 1
        return es.enter_context(self.nc.sbuf_tensor("%s_u%d" % (name, self._uid), list(shape), dt))

    def ps(self, es, name, shape, dt=F32):
        self._uid = getattr(self, "_uid", 0) + 1
        return es.enter_context(self.nc.psum_tensor("%s_u%d" % (name, self._uid), list(shape), dt))

    def phase_tin(self, x_in, ctx_in, xT, xT_buf):
        kb, nc = self.kb, self.nc
        with ExitStack() as es:
            ident = self.sb(es, "ti_ident", [128, 128], F32)
            xin = [self.sb(es, "ti_x%d" % i, [128, 4, D], F32) for i in range(2)]
            xo = [self.sb(es, "ti_o%d" % i, [128, NCH, 512], F32) for i in range(2)]
            pst = [self.ps(es, "ti_ps%d" % i, [128, 4, 512]) for i in range(2)]
            b_id = kb.buf("ident", dma=True)
            b_xin = [kb.buf("xin%d" % i, dma=True) for i in range(2)]
            b_xo = [kb.buf("xo%d" % i) for i in range(2)]
            b_ps = [kb.buf("ps%d" % i) for i in range(2)]
            kb.dma("sp", ident[:], self.c_ident_f, None, b_id)
            xT_v = xT.rearrange("(c p) t -> p c t", p=128)
            for it, (t0, n, isc) in enumerate(self.tiles()):
                nb = n // 128
                src = ctx_in if isc else x_in
                s0 = 0 if isc else t0
                xi, bxi = xin[it % 2], b_xin[it % 2]
                kb.dma("sp", xi[:, 0:nb, :], src[s0:s0 + n, :].rearrange("(b p) d -> p b d", p=128), None, bxi)
                xo_t, bxo = xo[it % 2], b_xo[it % 2]
                for half in range(2):
                    pidx = (2 * it + half) % 2
                    p_t, bp = pst[pidx], b_ps[pidx]
                    for c in range(4):
                        for blk in range(nb):
                            kb.mm(p_t[:, c, blk * 128:(blk + 1) * 128],
                                  [(xi[:, blk, (4 * half + c) * 128:(4 * half + c + 1) * 128], ident[:])],
                                  reads=[bxi, b_id], writes=[bp], transpose=True)
                    eng = "act" if half == 0 else "dve"
                    if eng == "act":
                        kb.op("act", lambda e: e.copy(out=xo_t[:, 4 * half:4 * half + 4, 0:n], in_=p_t[:, :, 0:n]),
                              reads=[bp], writes=[bxo])
                    else:
                        kb.op("dve", lambda e: e.tensor_copy(out=xo_t[:, 4 * half:4 * half + 4, 0:n], in_=p_t[:, :, 0:n]),
                              reads=[bp], writes=[bxo])
                kb.dma("sp", xT_v[:, :, t0:t0 + n], xo_t[:, :, 0:n], bxo, xT_buf)
            kb.end_phase()

    def phase_tout(self, xT, xT_buf, out, out_buf):
        kb, nc = self.kb, self.nc
        with ExitStack() as es:
            ident = self.sb(es, "to_ident", [128, 128], F32)
            xin = [self.sb(es, "to_x%d" % i, [128, NCH, 512], F32) for i in range(2)]
            xo = [self.sb(es, "to_o%d" % i, [128, 4, D], F32) for i in range(2)]
            pst = [self.ps(es, "to_ps%d" % i, [128, 4, 512]) for i in range(2)]
            b_id = kb.buf("ident", dma=True)
            b_xin = [kb.buf("xin%d" % i, dma=True) for i in range(2)]
            b_xo = [kb.buf("xo%d" % i) for i in range(2)]
            b_ps = [kb.buf("ps%d" % i) for i in range(2)]
            kb.dma("sp", ident[:], self.c_ident_f, None, b_id)
            xT_v = xT.rearrange("(c p) t -> p c t", p=128)
            for it, (t0, n, isc) in enumerate(self.tiles()):
                if isc:
                    continue
                xi, bxi = xin[it % 2], b_xin[it % 2]
                kb.dma("sp", xi[:, :, 0:n], xT_v[:, :, t0:t0 + n], xT_buf, bxi)
                xo_t, bxo = xo[it % 2], b_xo[it % 2]
                for half in range(2):
                    pidx = (2 * it + half) % 2
                    p_t, bp = pst[pidx], b_ps[pidx]
                    for blk in range(4):
                        for c in range(4):
                            kb.mm(p_t[:, blk, c * 128:(c + 1) * 128],
                                  [(xi[:, 4 * half + c, blk * 128:(blk + 1) * 128], ident[:])],
                                  reads=[bxi, b_id], writes=[bp], transpose=True)
                    if half == 0:
                        kb.op("act", lambda e: e.copy(out=xo_t[:, :, 512 * half:512 * half + 512], in_=p_t[:, :, :]),
                              reads=[bp], writes=[bxo])
                    else:
                        kb.op("dve", lambda e: e.tensor_copy(out=xo_t[:, :, 512 * half:512 * half + 512], in_=p_t[:, :, :]),
                              reads=[bp], writes=[bxo])
                kb.dma("sp", out[t0:t0 + n, :].rearrange("(b p) d -> p b d", p=128), xo_t[:], bxo, out_buf)
            kb.end_phase()

    def load_consts_common(self, es, pfx):
        kb = self.kb
        C = {}
        C["ones_f"] = self.sb(es, pfx + "ones_f", [128, 128], F32)
        C["ones_b"] = self.sb(es, pfx + "ones_b", [128, 128], BF16)
        C["eps"] = self.sb(es, pfx + "eps", [128, 1], F32)
        C["b"] = kb.buf(pfx + "consts")
        kb.op("dve", lambda e: e.memset(C["ones_f"][:], 1.0 / D), writes=[C["b"]])
        kb.op("dve", lambda e: e.memset(C["ones_b"][:], 1.0 / D), writes=[C["b"]])
        kb.op("dve", lambda e: e.memset(C["eps"][:], LN_EPS), writes=[C["b"]])
        return C

    def ln_alloc(self, es, pfx, W):
        kb = self.kb
        S = {}
        S["psm"] = self.ps(es, pfx + "psm", [128, 512])
        S["psv"] = self.ps(es, pfx + "psv", [128, 512])
        S["xc"] = self.sb(es, pfx + "xc", [128, NCH, W], F32)
        S["sq"] = self.sb(es, pfx + "sq", [128, NCH, W], BF16)
        S["rstd"] = self.sb(es, pfx + "rstd", [128, W], F32)
        for k in ("psm", "psv", "xc", "sq", "rstd"):
            S["b_" + k] = kb.buf(pfx + k)
        return S

    def ln_feat(self, X, bX, n, S, C):
        kb = self.kb
        xc, sq, rstd, psm, psv = S["xc"], S["sq"], S["rstd"], S["psm"], S["psv"]
        kb.mm(psm[:, 0:n], [(C["ones_f"][:], X[:, k, 0:n]) for k in range(NCH)],
              reads=[bX, C["b"]], writes=[S["b_psm"]])
        kb.op("dve", lambda e: e.tensor_tensor(out=xc[:, :, 0:n], in0=X[:, :, 0:n],
                                               in1=psm[:, 0:n].unsqueeze(1).broadcast_to([128, NCH, n]),
                                               op=ALU.subtract),
              reads=[bX, S["b_psm"]], writes=[S["b_xc"]])
        kb.op("act", lambda e: e.activation(out=sq[:, :, 0:n], in_=xc[:, :, 0:n], func=AF.Square),
              reads=[S["b_xc"]], writes=[S["b_sq"]])
        kb.mm(psv[:, 0:n], [(C["ones_b"][:], sq[:, k, 0:n]) for k in range(NCH)],
              reads=[S["b_sq"], C["b"]], writes=[S["b_psv"]])
        kb.op("act", lambda e: e.activation(out=rstd[:, 0:n], in_=psv[:, 0:n], func=AF.Ln, bias=C["eps"][:, 0:1]),
              reads=[S["b_psv"], C["b"]], writes=[S["b_rstd"]])
        kb.op("act", lambda e: e.activation(out=rstd[:, 0:n], in_=rstd[:, 0:n], func=AF.Exp, scale=-0.5),
              reads=[S["b_rstd"]], writes=[S["b_rstd"]])
        kb.op("dve", lambda e: e.tensor_tensor(out=xc[:, :, 0:n], in0=xc[:, :, 0:n],
                                               in1=rstd[:, 0:n].unsqueeze(1).broadcast_to([128, NCH, n]),
                                               op=ALU.mult),
              reads=[S["b_xc"], S["b_rstd"]], writes=[S["b_xc"]])

    def phase_mod(self):
        kb, nc = self.kb, self.nc
        with ExitStack() as es:
            cs = self.sb(es, "pm_cs", [128, NCH, 2], F32)
            wt = [self.sb(es, "pm_w%d" % i, [128, NCH, 512], F32) for i in range(2)]
            bada = self.sb(es, "pm_bada", [2, 6 * D], F32)
            mod = self.sb(es, "pm_mod", [2, 6 * D], F32)
            pst = [self.ps(es, "pm_ps%d" % i, [128, 512]) for i in range(2)]
            b_cs = kb.buf("cs", dma=True)
            b_wt = [kb.buf("wt%d" % i, dma=True) for i in range(2)]
            b_bada = kb.buf("bada", dma=True)
            b_mod = kb.buf("mod")
            b_ps = [kb.buf("ps%d" % i) for i in range(2)]
            kb.dma("sp", cs[:, :, 0:1], self.c_in.rearrange("o (k p) -> p k o", p=128), None, b_cs,
                   allow_slow_non_contiguous=True)
            kb.dma("sp", cs[:, :, 1:2], self.cctx_in.rearrange("o (k p) -> p k o", p=128), None, b_cs,
                   allow_slow_non_contiguous=True)
            kb.op("act", lambda e: e.activation(out=cs[:], in_=cs[:], func=AF.Silu), reads=[b_cs], writes=[b_cs])
            it = 0
            for l in range(self.depth):
                kb.dma("sp", bada[:], self.w["b_ada"][l:l + 1, :].broadcast_to([2, 6 * D]), None, b_bada)
                for cb in range(12):
                    w_t, bw = wt[it % 2], b_wt[it % 2]
                    p_t, bp = pst[it % 2], b_ps[it % 2]
                    it += 1
                    kb.dma("sp", w_t[:], self.w["w_ada"][l, :, cb * 512:(cb + 1) * 512].rearrange("(k p) n -> p k n", p=128),
                           None, bw)
                    kb.mm(p_t[0:2, :], [(cs[:, k, :], w_t[:, k, :]) for k in range(NCH)], reads=[b_cs, bw], writes=[bp])
                    kb.op("dve", lambda e: e.tensor_tensor(out=mod[:, cb * 512:(cb + 1) * 512], in0=p_t[0:2, :],
                                                           in1=bada[:, cb * 512:(cb + 1) * 512], op=ALU.add),
                          reads=[bp, b_bada], writes=[b_mod])
                kb.dma("sp", self.modv[l], mod[:], b_mod, self.b_modv)
            for l in range(self.depth):
                for j in range(2):
                    kb.dma("sp", self.modT[:, l, :, j:j + 1], self.modv[l, j:j + 1, :].rearrange("o (c p) -> p c o", p=128),
                           self.b_modv, self.b_modT, allow_slow_non_contiguous=True)
            for (c0, c1) in ((8, 16), (32, 40)):
                kb.op("dve", lambda e: e.tensor_scalar(out=self.modT[:, :, c0:c1, :], in0=self.modT[:, :, c0:c1, :],
                                                       scalar1=1.0, scalar2=None, op0=ALU.add),
                      reads=[self.b_modT], writes=[self.b_modT])
            kb.end_phase()

    def phase_proj(self, l, xT, b_xT):
        kb, nc = self.kb, self.nc
        T = self.T
        with ExitStack() as es:
            C = self.load_consts_common(es, "p1_")
            S = self.ln_alloc(es, "p1_", 512)
            w_sb = self.sb(es, "p1_w", [128, NCH, DPROJ], BF16)
            b_w = kb.buf("w_in", dma=True)
            for k in range(NCH):
                kb.dma("pool", w_sb[:, k, :], self.w["w_in"][l, k * 128:(k + 1) * 128, :], None, b_w)
            identb = self.sb(es, "p1_idb", [128, 128], BF16)
            b_id = kb.buf("idb", dma=True)
            kb.dma("sp", identb[:], self.c_ident_b, None, b_id)
            bdc = self.sb(es, "p1_bdc", [128, 2, 128], BF16)
            b_bdc = kb.buf("bdc", dma=True)
            kb.dma("sp", bdc[:], self.c_bdcs, None, b_bdc)
            gtab = self.sb(es, "p1_gtab", [128, 6, HD], F32)
            b_g = kb.buf("gtab", dma=True)
            for h in range(6):
                src = self.w["q_norm_g"] if h < 4 else self.w["k_norm_g"]
                kb.dma("sp", gtab[:, h, :], src[l:l + 1, :].broadcast_to([128, HD]), None, b_g)
            xt = [self.sb(es, "p1_x%d" % i, [128, NCH, 512], F32) for i in range(2)]
            b_xt = [kb.buf("xt%d" % i, dma=True) for i in range(2)]
            cs_t = [self.sb(es, "p1_cs%d" % i, [128, 4, 2, 32], F32) for i in range(2)]
            b_cs = [kb.buf("cs%d" % i, dma=True) for i in range(2)]
            hT = self.sb(es, "p1_hT", [128, NCH, 512], BF16)
            b_hT = kb.buf("hT")
            psz = self.ps(es, "p1_psz", [128, 1024])
            b_psz = kb.buf("psz")
            zs_l = [self.sb(es, "p1_zs%d" % i, [128, 1024], F32) for i in range(2)]
            b_zs_l = [kb.buf("zs%d" % i) for i in range(2)]
            sq6_l = [self.sb(es, "p1_sq6%d" % i, [128, 384], F32) for i in range(2)]
            ss6_l = [self.sb(es, "p1_ss6%d" % i, [128, 6], F32) for i in range(2)]
            b_sq6_l, b_ss6_l = [kb.buf("sq6%d" % i) for i in range(2)], [kb.buf("ss6%d" % i) for i in range(2)]
            tmp_l = [[self.sb(es, "p1_tmp%d_%d" % (s_, i), [128, 192], F32) for i in range(4)] for s_ in range(2)]
            b_tmp_l = [[kb.buf("tmp%d_%d" % (s_, i)) for i in range(4)] for s_ in range(2)]
            rope_o_l = [[self.sb(es, "p1_ro%d_%d" % (s_, i), [128, 384], BF16) for i in range(2)] for s_ in range(2)]
            b_ro_l = [[kb.buf("ro%d_%d" % (s_, i)) for i in range(2)] for s_ in range(2)]
            pst_l = [self.ps(es, "p1_pst%d" % i, [128, 6, 128], BF16) for i in range(2)]
            b_pst_l = [kb.buf("pst%d" % i) for i in range(2)]
            bi = 0
            qkT = [self.sb(es, "p1_qkT%d" % i, [128, 6, 512], BF16) for i in range(2)]
            b_qkT = [kb.buf("qkT%d" % i) for i in range(2)]
            vt = [self.sb(es, "p1_vt%d" % i, [128, 4, 2, 128], BF16) for i in range(2)]
            b_vt = [kb.buf("vt%d" % i) for i in range(2)]
            psf = [self.ps(es, "p1_psf%d" % i, [128, 512]) for i in range(2)]
            b_psf = [kb.buf("psf%d" % i) for i in range(2)]
            zfm = [self.sb(es, "p1_zfm%d" % i, [128, 8, 512], BF16) for i in range(2)]
            b_zfm = [kb.buf("zfm%d" % i) for i in range(2)]
            fab = [self.sb(es, "p1_fab%d" % i, [128, 2, 2, 512], BF16) for i in range(2)]
            b_fab = [kb.buf("fab%d" % i) for i in range(2)]
            xT_v = xT.rearrange("(c p) t -> p c t", p=128)
            eps_hd = self.sb(es, "p1_epshd", [128, 1], F32)
            kb.op("dve", lambda e: e.memset(eps_hd[:], LN_EPS), writes=[C["b"]])

            tiles = self.tiles()
            hTs = [hT, self.sb(es, "p1_hT2", [128, NCH, 512], BF16)]
            b_hTs = [b_hT, kb.buf("hT2")]

            def tile_loads(it):
                t0, n, isc = tiles[it]
                nb = n // 128
                x_t, bx = xt[it % 2], b_xt[it % 2]
                kb.dma("sp", x_t[:, :, 0:n], xT_v[:, :, t0:t0 + n], b_xT, bx)
                c_t, bc = cs_t[it % 2], b_cs[it % 2]
                kb.dma("sp", c_t[:, 0:nb, :, :], self.c_rope[t0:t0 + n].rearrange("(b p) a d -> p b a d", p=128), None, bc)

            def tile_ln(it):
                t0, n, isc = tiles[it]
                j = 1 if isc else 0
                x_t, bx = xt[it % 2], b_xt[it % 2]
                hT, b_hT = hTs[it % 2], b_hTs[it % 2]
                self.ln_stage(0, x_t, bx, n, S, C)
                yield
                self.ln_stage(1, x_t, bx, n, S, C)
                yield
                xc = S["xc"]
                for k in range(NCH):
                    kb.op("act", lambda e: e.activation(out=hT[:, k, 0:n], in_=xc[:, k, 0:n], func=AF.Identity,
                                                        scale=self.modT[:, l, 8 + k, j:j + 1],
                                                        bias=self.modT[:, l, 0 + k, j:j + 1]),
                          reads=[S["b_xc"], self.b_modT], writes=[b_hT])
                yield

            def tile_work(it):
                nonlocal bi
                t0, n, isc = tiles[it]
                j = 1 if isc else 0
                nb = n // 128
                c_t, bc = cs_t[it % 2], b_cs[it % 2]
                hT, b_hT = hTs[it % 2], b_hTs[it % 2]
                qk_t, bqk = qkT[it % 2], b_qkT[it % 2]
                v_t, bv = vt[it % 2], b_vt[it % 2]
                for b in range(nb):
                    zs, b_zs = zs_l[bi % 2], b_zs_l[bi % 2]
                    sq6, b_sq6, ss6, b_ss6 = sq6_l[bi % 2], b_sq6_l[bi % 2], ss6_l[bi % 2], b_ss6_l[bi % 2]
                    tmp, b_tmp = tmp_l[bi % 2], b_tmp_l[bi % 2]
                    rope_o, b_ro = rope_o_l[bi % 2], b_ro_l[bi % 2]
                    pst, b_pst = pst_l[bi % 2], b_pst_l[bi % 2]
                    bi += 1
                    for half in range(2):
                        kb.mm(psz[:, half * 512:(half + 1) * 512],
                              [(hT[:, k, b * 128:(b + 1) * 128], w_sb[:, k, half * 512:(half + 1) * 512]) for k in range(NCH)],
                              reads=[b_hT, b_w], writes=[b_psz])
                    kb.op("act", lambda e: e.copy(out=zs[:], in_=psz[:]), reads=[b_psz], writes=[b_zs])
                    kb.op("pool", lambda e: e.tensor_copy(out=v_t[:, b, 0, :], in_=zs[:, 384:512]), reads=[b_zs], writes=[bv])
                    kb.op("pool", lambda e: e.tensor_copy(out=v_t[:, b, 1, :], in_=zs[:, 896:1024]), reads=[b_zs], writes=[bv])
                    kb.op("act", lambda e: e.activation(out=sq6[:], in_=zs[:, 512:896], func=AF.Square), reads=[b_zs], writes=[b_sq6])
                    kb.op("dve", lambda e: e.tensor_reduce(out=ss6[:], in_=sq6[:].rearrange("p (h d) -> p h d", h=6), axis=AX.X, op=ALU.add),
                          reads=[b_sq6], writes=[b_ss6])
                    kb.op("act", lambda e: e.activation(out=ss6[:], in_=ss6[:], func=AF.Ln, scale=1.0 / HD, bias=eps_hd[:, 0:1]),
                          reads=[b_ss6, C["b"]], writes=[b_ss6])
                    kb.op("act", lambda e: e.activation(out=ss6[:], in_=ss6[:], func=AF.Exp, scale=-0.5), reads=[b_ss6], writes=[b_ss6])
                    zb = zs[:, 512:896].rearrange("p (h d) -> p h d", h=6)
                    kb.op("dve", lambda e: e.tensor_tensor(out=zb, in0=zb, in1=ss6[:].unsqueeze(2).broadcast_to([128, 6, HD]), op=ALU.mult),
                          reads=[b_zs, b_ss6], writes=[b_zs])
                    kb.op("dve", lambda e: e.tensor_tensor(out=zb, in0=zb, in1=gtab[:], op=ALU.mult), reads=[b_zs, b_g], writes=[b_zs])
                    cosb = c_t[:, b, 0, :].rearrange("p (a d) -> p a d", a=2).unsqueeze(1).broadcast_to([128, 6, 2, 16])
                    sinb = c_t[:, b, 1, :].rearrange("p (a d) -> p a d", a=2).unsqueeze(1).broadcast_to([128, 6, 2, 16])
                    for gi, c0 in enumerate((0, 512)):
                        zz = zs[:, c0:c0 + 384].rearrange("p (h a s d) -> p h a s d", h=6, a=2, s=2)
                        x1, x2 = zz[:, :, :, 0, :], zz[:, :, :, 1, :]
                        ro, bro = rope_o[gi], b_ro[gi]
                        rr = ro[:].rearrange("p (h a s d) -> p h a s d", h=6, a=2, s=2)
                        tv = [t[:].rearrange("p (h a d) -> p h a d", h=6, a=2) for t in tmp]
                        kb.op("dve", lambda e: e.tensor_tensor(out=tv[0], in0=x1, in1=cosb, op=ALU.mult), reads=[b_zs, bc], writes=[b_tmp[0]])
                        kb.op("dve", lambda e: e.tensor_tensor(out=tv[1], in0=x2, in1=sinb, op=ALU.mult), reads=[b_zs, bc], writes=[b_tmp[1]])
                        kb.op("pool", lambda e: e.tensor_tensor(out=tv[2], in0=x2, in1=cosb, op=ALU.mult), reads=[b_zs, bc], writes=[b_tmp[2]])
                        kb.op("pool", lambda e: e.tensor_tensor(out=tv[3], in0=x1, in1=sinb, op=ALU.mult), reads=[b_zs, bc], writes=[b_tmp[3]])
                        kb.op("dve", lambda e: e.tensor_tensor(out=rr[:, :, :, 0, :], in0=tv[0], in1=tv[1], op=ALU.subtract),
                              reads=[b_tmp[0], b_tmp[1]], writes=[bro])
                        kb.op("pool", lambda e: e.tensor_tensor(out=rr[:, :, :, 1, :], in0=tv[2], in1=tv[3], op=ALU.add),
                              reads=[b_tmp[2], b_tmp[3]], writes=[bro])
                        for c3 in range(3):
                            kb.mm(pst[:, gi * 3 + c3, :], [(ro[:, c3 * 128:(c3 + 1) * 128], identb[:])], reads=[bro, b_id], writes=[b_pst],
                                  transpose=True)
                    kb.op("act", lambda e: e.copy(out=qk_t[:, :, b * 128:(b + 1) * 128], in_=pst[:]), reads=[b_pst], writes=[bqk])
                    yield
                kb.dma("sp", self.QKa.rearrange("(c p) t -> p c t", p=128)[:, :, t0:t0 + n], qk_t[:, 0:3, 0:n], bqk, self.b_QKa)
                kb.dma("sp", self.QKb.rearrange("(c p) t -> p c t", p=128)[:, :, t0:t0 + n], qk_t[:, 3:6, 0:n], bqk, self.b_QKb)
                kb.dma("sp", self.Va[t0:t0 + n, :].rearrange("(b p) d -> p b d", p=128), v_t[:, 0:nb, 0, :], bv, self.b_Va)
                kb.dma("sp", self.Vb[t0:t0 + n, :].rearrange("(b p) d -> p b d", p=128), v_t[:, 0:nb, 1, :], bv, self.b_Vb)
                zf_t, bzf = zfm[it % 2], b_zfm[it % 2]
                for cc in range(8):
                    p_t, bp = psf[cc % 2], b_psf[cc % 2]
                    kb.mm(p_t[:, 0:n], [(w_sb[:, k, 1024 + cc * 128:1024 + (cc + 1) * 128], hT[:, k, 0:n]) for k in range(NCH)],
                          reads=[b_hT, b_w], writes=[bp])
                    if cc % 2 == 0:
                        kb.op("act", lambda e: e.copy(out=zf_t[:, cc, 0:n], in_=p_t[:, 0:n]), reads=[bp], writes=[bzf])
                    else:
                        kb.op("dve", lambda e: e.tensor_copy(out=zf_t[:, cc, 0:n], in_=p_t[:, 0:n]), reads=[bp], writes=[bzf])
                c0 = self.zcol(t0, isc)
                kb.dma("sp", self.zhT.rearrange("(c p) t -> p c t", p=128)[:, :, c0:c0 + n], zf_t[:, 0:6, 0:n], bzf, self.b_zhT)
                yield
                fa_t, bfa = fab[it % 2], b_fab[it % 2]
                for ab in range(2):
                    for ch in range(2):
                        p_t, bp = psf[(ab * 2 + ch) % 2], b_psf[(ab * 2 + ch) % 2]
                        kb.mm(p_t[:, 0:n], [(bdc[:, ab, :], zf_t[:, 6 + ch, 0:n])], reads=[bzf, b_bdc], writes=[bp])
                        kb.op("act", lambda e: e.copy(out=fa_t[:, ab, ch, 0:n], in_=p_t[:, 0:n]), reads=[bp], writes=[bfa])
                kb.dma("sp", self.faT.rearrange("(c p) t -> p c t", p=128)[:, :, t0:t0 + n], fa_t[:, 0, :, 0:n], bfa, self.b_faT)
                kb.dma("sp", self.fbT.rearrange("(c p) t -> p c t", p=128)[:, :, t0:t0 + n], fa_t[:, 1, :, 0:n], bfa, self.b_fbT)

                yield

            def drain(g):
                for _ in g:
                    pass

            nt = len(tiles)
            tile_loads(0)
            drain(tile_ln(0))
            for it in range(nt):
                gl = None
                if it + 1 < nt:
                    tile_loads(it + 1)
                    gl = tile_ln(it + 1)
                gw = tile_work(it)
                while gw is not None or gl is not None:
                    if gw is not None:
                        try:
                            next(gw)
                        except StopIteration:
                            gw = None
                    if gl is not None:
                        try:
                            next(gl)
                        except StopIteration:
                            gl = None
            kb.end_phase()

    def zcol(self, t0, isc):
        return (t0 + 3) if isc else (t0 + 1)
    def phase_attn(self, l, which):
        kb_, nc = self.kb, self.nc
        kb = kb_
        T, L = self.T, self.L
        nblk, nlat = T // 128, L // 128
        QK, V, bQK, bV = (self.QKa, self.Va, self.b_QKa, self.b_Va) if which == "a" else (self.QKb, self.Vb, self.b_QKb, self.b_Vb)
        rb = 0 if which == "a" else 256
        with ExitStack() as es:
            ones_f = self.sb(es, "at_ones", [128, 128], F32)
            b_ones = kb.buf("ones")
            kb.op("dve", lambda e: e.memset(ones_f[:], 1.0), writes=[b_ones])
            if which == "a":
                masks = self.sb(es, "at_mask", [128, 6, 512], BF16)
                b_mask = kb.buf("mask", dma=True)
                kb.dma("sp", masks[:], self.c_wmask, None, b_mask)
                sk = self.sb(es, "at_sink", [128, 4], F32)
                b_sk = kb.buf("sink", dma=True)
                kb.dma("sp", sk[:], self.w["sink_a"][l:l + 1, :].broadcast_to([128, 4]), None, b_sk)
                kb.op("act", lambda e: e.activation(out=sk[:], in_=sk[:], func=AF.Exp), reads=[b_sk], writes=[b_sk])
            K2 = self.sb(es, "at_K2", [128, T], BF16)
            b_K2 = kb.buf("K2", dma=True)
            V1 = self.sb(es, "at_V1", [128, nblk, 65], BF16)
            b_V1 = kb.buf("V1", dma=True)
            Q2 = [self.sb(es, "at_Q%d" % i, [128, 512], BF16) for i in range(2)]
            b_Q2 = [kb.buf("Q%d" % i, dma=True) for i in range(2)]
            psS = [self.ps(es, "at_psS%d" % i, [128, 2, 512]) for i in range(2)]
            b_psS = [kb.buf("psS%d" % i) for i in range(2)]
            NPT = 5
            LAG = 2
            pT = [self.sb(es, "at_pT%d" % i, [128, 2, 512], BF16) for i in range(NPT)]
            b_pT = [kb.buf("pT%d" % i) for i in range(NPT)]
            psO2 = [self.ps(es, "at_psO%d" % i, [128, 2, 512]) for i in range(2)]
            b_psO2 = [[kb.buf("psO%d_%d" % (i, h)) for h in range(2)] for i in range(2)]
            den = self.sb(es, "at_den", [128, 512], F32)
            b_den = kb.buf("den")
            osb = [self.sb(es, "at_o%d" % i, [64, 2, 512], F32) for i in range(2)]
            b_osb = [kb.buf("o%d" % i) for i in range(2)]
            qi = 0
            ei = 0
            pi = 0
            for j in range(2):
                for hh in range(2):
                    kb.dma("sp", K2[64 * hh:64 * hh + 64, :], QK[256 + 64 * j:256 + 64 * j + 64, :], bQK, b_K2)
                kb.op("pool", lambda e: e.memset(V1[:, :, 64:65], 1.0), writes=[b_V1])
                for b0 in range(0, nblk, 16):
                    b1 = min(nblk, b0 + 16)
                    kb.dma("sp", V1[:, b0:b1, 0:64], V[b0 * 128:b1 * 128, 64 * j:64 * j + 64].rearrange("(b p) d -> p b d", p=128), bV, b_V1)
                for (t0, n, isc) in self.tiles():
                    qi += 1
                    q_t, bq = Q2[qi % 2], b_Q2[qi % 2]
                    o_t, bo = osb[qi % 2], b_osb[qi % 2]
                    kb.dma("sp", q_t[:, 0:n], QK[128 * j:128 * j + 128, t0:t0 + n], bQK, bq)
                    ctx_blocks = [(b, None) for b in range(nlat, nblk)]
                    if isc:
                        blocks = ctx_blocks
                    elif which == "b":
                        blocks = [(b, None) for b in range(nblk)]
                    else:
                        qb0 = t0 // 128
                        blocks = [(b, b - qb0 + 1) for b in range(max(0, qb0 - 1), min(nlat - 1, qb0 + 4) + 1)] + ctx_blocks
                    nk = len(blocks)
                    psO, b_psO = psO2[qi % 2], b_psO2[qi % 2]
                    pend = []
                    for ki, (kblk, rel) in enumerate(blocks):
                        ps_t, bps = psS[ei % 2], b_psS[ei % 2]
                        p_t, bp = pT[pi % NPT], b_pT[pi % NPT]
                        pi += 1
                        ei += 1
                        for hh in range(2):
                            kb.mm(ps_t[:, hh, 0:n], [(K2[64 * hh:64 * hh + 64, kblk * 128:(kblk + 1) * 128], q_t[64 * hh:64 * hh + 64, 0:n])],
                                  reads=[b_K2, bq], writes=[bps])
                        kb.op("act", lambda e: e.activation(out=p_t[:, :, 0:n], in_=ps_t[:, :, 0:n], func=AF.Exp, scale=HD ** -0.5),
                              reads=[bps], writes=[bp])
                        if rel is not None:
                            kb.op("dve", lambda e: e.tensor_tensor(out=p_t[:, :, 0:n], in0=p_t[:, :, 0:n],
                                                                    in1=masks[:, rel, 0:n].unsqueeze(1).broadcast_to([128, 2, n]), op=ALU.mult),
                                  reads=[bp, b_mask], writes=[bp])
                        pend.append((ki, p_t, bp, kblk))
                        if len(pend) > LAG:
                            pk, pp_t, pbp, pkblk = pend.pop(0)
                            for hh in range(2):
                                kb.mm1(psO[0:65, hh, 0:n], V1[:, pkblk, :], pp_t[:, hh, 0:n], start=(pk == 0), stop=(pk == nk - 1),
                                       reads=[b_V1, pbp], writes=[b_psO[hh]])
                    while pend:
                        pk, pp_t, pbp, pkblk = pend.pop(0)
                        for hh in range(2):
                            kb.mm1(psO[0:65, hh, 0:n], V1[:, pkblk, :], pp_t[:, hh, 0:n], start=(pk == 0), stop=(pk == nk - 1),
                                   reads=[b_V1, pbp], writes=[b_psO[hh]])
                    psBt, b_psB = psS[ei % 2], b_psS[ei % 2]
                    ei += 1
                    for hh in range(2):
                        if which == "a":
                            kb.op("dve", lambda e: e.tensor_scalar(out=den[64:65, 0:n], in0=psO[64:65, hh, 0:n],
                                                                   scalar1=sk[64:65, 2 * j + hh:2 * j + hh + 1], scalar2=None, op0=ALU.add),
                                  reads=[b_psO[hh], b_sk], writes=[b_den])
                            kb.op("dve", lambda e: e.reciprocal(out=den[64:65, 0:n], in_=den[64:65, 0:n]), reads=[b_den], writes=[b_den])
                        else:
                            kb.op("dve", lambda e: e.reciprocal(out=den[64:65, 0:n], in_=psO[64:65, hh, 0:n]), reads=[b_psO[hh]], writes=[b_den])
                        kb.mm(psBt[0:64, hh, 0:n], [(ones_f[64:65, 0:64], den[64:65, 0:n])], reads=[b_ones, b_den], writes=[b_psB])
                        kb.op("act", lambda e: e.copy(out=o_t[:, hh, 0:n], in_=psO[0:64, hh, 0:n]), reads=[b_psO[hh]], writes=[bo])
                        kb.op("dve", lambda e: e.tensor_tensor(out=o_t[:, hh, 0:n], in0=o_t[:, hh, 0:n], in1=psBt[0:64, hh, 0:n], op=ALU.mult),
                              reads=[bo, b_psB], writes=[bo])
                    r0 = rb + 128 * j
                    kb.dma("sp", self.mixT[r0:r0 + 128, t0:t0 + n].rearrange("(h d) t -> d h t", h=2), o_t[:, :, 0:n], bo, self.b_mixT)
            kb.end_phase()
    def dft_alloc(self, es, pfx):
        kb = self.kb
        Dd = {}
        Dd["psA"] = [self.ps(es, pfx + "psA%d" % i, [128, 1024]) for i in range(2)]
        Dd["psBr"] = [self.ps(es, pfx + "psBr%d" % i, [128, 512]) for i in range(2)]
        Dd["psBi"] = [self.ps(es, pfx + "psBi%d" % i, [128, 512]) for i in range(2)]
        Dd["t"] = [self.sb(es, pfx + "t%d" % i, [128, 512], F32) for i in range(8)]
        Dd["Br"] = [self.sb(es, pfx + "Br%d" % i, [128, 512], BF16) for i in range(2)]
        Dd["Bi"] = [self.sb(es, pfx + "Bi%d" % i, [128, 512], BF16) for i in range(2)]
        for k in ("psA", "psBr", "psBi", "Br", "Bi"):
            Dd["b_" + k] = [kb.buf(pfx + k + str(i)) for i in range(2)]
        Dd["b_t"] = [kb.buf(pfx + "t%d" % i) for i in range(8)]
        Dd["cnt"] = 0
        return Dd

    def dft_tabs(self, es, pfx, N1, inv):
        kb = self.kb
        key = "%d_%s" % (N1, "i" if inv else "f")
        KA, MA, FA = (128, N1, 128) if inv else (N1, 128, N1)
        Tt = {"N1": N1, "MA": MA, "FA": FA}
        Tt["tabA"] = self.sb(es, pfx + "tabA" + key, [KA, 2, 2 * FA], BF16)
        Tt["tw"] = self.sb(es, pfx + "tw" + key, [MA, 2, FA], F32)
        Tt["tabB"] = self.sb(es, pfx + "tabB" + key, [MA, 4, MA], BF16)
        Tt["b"] = kb.buf(pfx + "tabs" + key, dma=True)
        kb.dma("sp", Tt["tabA"][:], self.c_dft["tabA_" + key], None, Tt["b"])
        kb.dma("sp", Tt["tw"][:], self.c_dft["tw_" + key], None, Tt["b"])
        kb.dma("sp", Tt["tabB"][:], self.c_dft["tabB_" + key], None, Tt["b"])
        return Tt

    def dft2(self, Dd, Tt, Xre, Xim, bX, KA, want_im, P_out, out_cb, CB=64):
        kb = self.kb
        MA, FA = Tt["MA"], Tt["FA"]
        G = min(512 // FA, CB)
        W = G * FA
        tabA, tw, tabB, bT = Tt["tabA"], Tt["tw"], Tt["tabB"], Tt["b"]
        bXs = list(bX) if isinstance(bX, (list, tuple)) else [bX]
        for g0 in range(0, CB, G):
            i = Dd["cnt"] % 2
            Dd["cnt"] += 1
            psA, bpsA = Dd["psA"][i], Dd["b_psA"][i]
            Br, bBr = Dd["Br"][i], Dd["b_Br"][i]
            Bi, bBi = Dd["Bi"][i], Dd["b_Bi"][i]
            for ci in range(G):
                c = g0 + ci
                pairs = [(Xre[0:KA, c, 0:MA], tabA[0:KA, 0, :])]
                if Xim is not None:
                    pairs.append((Xim[0:KA, c, 0:MA], tabA[0:KA, 1, :]))
                kb.mm(psA[0:MA, ci * 2 * FA:(ci + 1) * 2 * FA], pairs, reads=bXs + [bT], writes=[bpsA])
            Av = psA[0:MA, 0:2 * W].rearrange("p (g s k) -> p g s k", g=G, s=2)
            Are, Aim = Av[:, :, 0, :], Av[:, :, 1, :]
            cb_ = tw[:, 0, :].unsqueeze(1).broadcast_to([MA, G, FA])
            sb_ = tw[:, 1, :].unsqueeze(1).broadcast_to([MA, G, FA])
            tset = Dd["t"][4 * i:4 * i + 4]
            bt = Dd["b_t"][4 * i:4 * i + 4]
            t = [x[0:MA, 0:W].rearrange("p (g k) -> p g k", g=G) for x in tset]
            kb.op("dve", lambda e: e.tensor_tensor(out=t[0], in0=Are, in1=cb_, op=ALU.mult), reads=[bpsA, bT], writes=[bt[0]])
            kb.op("dve", lambda e: e.tensor_tensor(out=t[1], in0=Aim, in1=sb_, op=ALU.mult), reads=[bpsA, bT], writes=[bt[1]])
            kb.op("pool", lambda e: e.tensor_tensor(out=Br[0:MA, 0:W], in0=tset[0][0:MA, 0:W], in1=tset[1][0:MA, 0:W], op=ALU.add),
                  reads=[bt[0], bt[1]], writes=[bBr])
            kb.op("dve", lambda e: e.tensor_tensor(out=t[2], in0=Aim, in1=cb_, op=ALU.mult), reads=[bpsA, bT], writes=[bt[2]])
            kb.op("dve", lambda e: e.tensor_tensor(out=t[3], in0=Are, in1=sb_, op=ALU.mult), reads=[bpsA, bT], writes=[bt[3]])
            kb.op("pool", lambda e: e.tensor_tensor(out=Bi[0:MA, 0:W], in0=tset[2][0:MA, 0:W], in1=tset[3][0:MA, 0:W], op=ALU.subtract),
                  reads=[bt[2], bt[3]], writes=[bBi])
            psr, bpsr = Dd["psBr"][i], Dd["b_psBr"][i]
            kb.mm(psr[0:P_out, 0:W], [(tabB[0:MA, 0, 0:P_out], Br[0:MA, 0:W]), (tabB[0:MA, 1, 0:P_out], Bi[0:MA, 0:W])],
                  reads=[bBr, bBi, bT], writes=[bpsr])
            psi, bpsi = None, None
            if want_im:
                psi, bpsi = Dd["psBi"][i], Dd["b_psBi"][i]
                kb.mm(psi[0:P_out, 0:W], [(tabB[0:MA, 2, 0:P_out], Br[0:MA, 0:W]), (tabB[0:MA, 3, 0:P_out], Bi[0:MA, 0:W])],
                      reads=[bBr, bBi, bT], writes=[bpsi])
            out_cb(g0, G, W, psr, bpsr, psi, bpsi)

    def phase_hyena_filter(self, l):
        kb = self.kb
        L = self.L
        NT = 2 * L + 512 + 1
        with ExitStack() as es:
            w1 = self.sb(es, "hf_w1", [HY_EMB, HY_HID], F32)
            w2 = self.sb(es, "hf_w2", [HY_HID, HY_HID], F32)
            w3 = self.sb(es, "hf_w3", [HY_HID, 512], F32)
            vec = self.sb(es, "hf_vec", [HY_HID, 6], F32)
            b_w = kb.buf("hf_w", dma=True)
            kb.dma("sp", w1[:], self.w["hy_f_w1"][l], None, b_w)
            kb.dma("sp", w2[:], self.w["hy_f_w2"][l], None, b_w)
            kb.dma("sp", w3[:], self.w["hy_f_w3"][l], None, b_w)
            for i, nm in enumerate(("hy_f_b1", "hy_f_freq", "hy_f_b2")):
                kb.dma("sp", vec[:, i:i + 1], self.w[nm][l:l + 1, :].rearrange("o h -> h o"), None, b_w, allow_slow_non_contiguous=True)
            kb.op("dve", lambda e: e.tensor_tensor(out=vec[:, 3:4], in0=vec[:, 0:1], in1=vec[:, 1:2], op=ALU.mult), reads=[b_w], writes=[b_w])
            kb.op("dve", lambda e: e.tensor_tensor(out=vec[:, 4:5], in0=vec[:, 2:3], in1=vec[:, 1:2], op=ALU.mult), reads=[b_w], writes=[b_w])
            zt = [self.sb(es, "hf_z%d" % i, [HY_EMB, 512], F32) for i in range(2)]
            b_zt = [kb.buf("hf_z%d" % i, dma=True) for i in range(2)]
            dec = [self.sb(es, "hf_dec%d" % i, [128, 2, 512], F32) for i in range(2)]
            b_dec = [kb.buf("hf_dec%d" % i, dma=True) for i in range(2)]
            ps1s = [self.ps(es, "hf_ps1%d" % i, [128, 512]) for i in range(2)]
            ps3 = [self.ps(es, "hf_ps3%d" % i, [128, 512]) for i in range(2)]
            b_ps1s, b_ps3 = [kb.buf("ps1%d" % i) for i in range(2)], [kb.buf("ps3%d" % i) for i in range(2)]
            hs = [self.sb(es, "hf_h%d" % i, [HY_HID, 512], F32) for i in range(2)]
            ms = [self.sb(es, "hf_m%d" % i, [HY_HID, 512], F32) for i in range(2)]
            b_hs, b_ms = [kb.buf("h%d" % i) for i in range(2)], [kb.buf("m%d" % i) for i in range(2)]
            kf = [self.sb(es, "hf_kf%d" % i, [128, 2, 512], F32) for i in range(2)]
            b_kf = [kb.buf("kf%d" % i) for i in range(2)]
            kfb = [self.sb(es, "hf_kfb%d" % i, [128, 2, 512], BF16) for i in range(2)]
            b_kfb = [kb.buf("kfb%d" % i) for i in range(2)]
            hb0 = self.sb(es, "hf_hb0", [128, 2, 1], F32)
            b_hb0 = kb.buf("hb0")
            tl = [(2 * L + 512, 1, True, False)]
            for c0 in range(0, L, 512):
                tl.append((c0, 512, False, c0 == 0))
            for c0 in range(L, 2 * L, 512):
                tl.append((c0, 512, True, False))
            tl.append((2 * L, 256, False, True))
            tl.append((2 * L + 256, 256, True, False))
            kfv = self.kfT.rearrange("(c p) n -> p c n", p=128)
            decv = self.c_hdec.rearrange("(c p) n -> p c n", p=128)
            for it, (c0, n, back, addh) in enumerate(tl):
                z_t, bz = zt[it % 2], b_zt[it % 2]
                d_t, bd = dec[it % 2], b_dec[it % 2]
                ps1, b_ps1 = ps1s[it % 2], b_ps1s[it % 2]
                h, b_h = hs[it % 2], b_hs[it % 2]
                m, b_m = ms[it % 2], b_ms[it % 2]
                kw = dict(allow_slow_non_contiguous=True) if n == 1 else {}
                kb.dma("sp", z_t[:, 0:n], self.c_hz[:, c0:c0 + n], None, bz, **kw)
                if n > 1:
                    kb.dma("sp", d_t[:, :, 0:n], decv[:, :, c0:c0 + n], None, bd, **kw)
                src, bsrc, K = z_t, bz, HY_EMB
                for (wt, bcol) in ((w1, 3), (w2, 4)):
                    kb.mm(ps1[0:HY_HID, 0:n], [(wt[0:K, :], src[0:K, 0:n])], reads=[b_w, bsrc], writes=[b_ps1])
                    kb.op("act", lambda e: e.activation(out=h[:, 0:n], in_=ps1[0:HY_HID, 0:n], func=AF.Identity,
                                                        scale=vec[:, 1:2], bias=vec[:, bcol:bcol + 1]),
                          reads=[b_ps1, b_w], writes=[b_h])
                    for rep in range(1):
                        for (cmp_, thr, add) in ((ALU.is_gt, math.pi, -2 * math.pi), (ALU.is_lt, -math.pi, 2 * math.pi)):
                            kb.op("dve", lambda e: e.tensor_scalar(out=m[:, 0:n], in0=h[:, 0:n], scalar1=thr, scalar2=add, op0=cmp_, op1=ALU.mult),
                                  reads=[b_h], writes=[b_m])
                            kb.op("dve", lambda e: e.tensor_tensor(out=h[:, 0:n], in0=h[:, 0:n], in1=m[:, 0:n], op=ALU.add),
                                  reads=[b_h, b_m], writes=[b_h])
                    kb.op("act", lambda e: e.activation(out=h[:, 0:n], in_=h[:, 0:n], func=AF.Sin), reads=[b_h], writes=[b_h])
                    src, bsrc, K = h, b_h, HY_HID
                kf_t, bkf = kf[it % 2], b_kf[it % 2]
                kfb_t, bkfb = kfb[it % 2], b_kfb[it % 2]
                for ch in range(2):
                    p3, bp3 = ps3[ch], b_ps3[ch]
                    wc0 = (256 if back else 0) + ch * 128
                    kb.mm(p3[:, 0:n], [(w3[:, wc0:wc0 + 128], h[:, 0:n])], reads=[b_w, b_h], writes=[bp3])
                    if n == 1:
                        kb.op("dve", lambda e: e.tensor_copy(out=hb0[:, ch, :], in_=p3[:, 0:1]), reads=[bp3], writes=[b_hb0])
                    else:
                        kb.op("dve", lambda e: e.tensor_tensor(out=kf_t[:, ch, 0:n], in0=p3[:, 0:n], in1=d_t[:, ch, 0:n], op=ALU.mult),
                              reads=[bp3, bd], writes=[bkf])
                if n == 1:
                    continue
                if addh:
                    kb.op("dve", lambda e: e.tensor_tensor(out=kf_t[:, :, 0:1], in0=kf_t[:, :, 0:1], in1=hb0[:], op=ALU.add),
                          reads=[bkf, b_hb0], writes=[bkf])
                kb.op("pool", lambda e: e.tensor_copy(out=kfb_t[:, :, 0:n], in_=kf_t[:, :, 0:n]), reads=[bkf], writes=[bkfb])
                kb.dma("sp", kfv[:, :, c0:c0 + n], kfb_t[:, :, 0:n], bkfb, self.b_kfT)
            kb.end_phase()

    def phase_zinit(self):
        kb = self.kb
        with ExitStack() as es:
            z = self.sb(es, "zi_z", [128, 6, 1], BF16)
            b_z = kb.buf("z")
            kb.op("dve", lambda e: e.memset(z[:], 0.0), writes=[b_z])
            zv = self.zhT.rearrange("(c p) t -> p c t", p=128)
            for col in (0, self.L + 1, self.L + 2, self.T + 3):
                kb.dma("sp", zv[:, :, col:col + 1], z[:], b_z, self.b_zhT, allow_slow_non_contiguous=True)
            kb.end_phase()

    def phase_hyena_prep(self, l):
        kb = self.kb
        with ExitStack() as es:
            cw = self.sb(es, "hp_cw", [128, 6, 4], F32)
            b_cw = kb.buf("cw", dma=True)
            for tp in range(3):
                kb.dma("sp", cw[:, :, tp:tp + 1], self.w["hy_conv_w"][l, tp:tp + 1, :].rearrange("o (c p) -> p c o", p=128), None, b_cw,
                       allow_slow_non_contiguous=True)
            kb.dma("sp", cw[:, :, 3:4], self.w["hy_conv_b"][l:l + 1, :].rearrange("o (c p) -> p c o", p=128), None, b_cw,
                   allow_slow_non_contiguous=True)
            zt = [self.sb(es, "hp_z%d" % i, [128, 6, 514], BF16) for i in range(2)]
            b_zt = [kb.buf("z%d" % i, dma=True) for i in range(2)]
            u = [self.sb(es, "hp_u%d" % i, [128, 6, 512], F32) for i in range(2)]
            b_u = [kb.buf("u%d" % i) for i in range(2)]
            o = [self.sb(es, "hp_o%d" % i, [128, 4, 512], BF16) for i in range(2)]
            b_o = [kb.buf("o%d" % i) for i in range(2)]
            zv = self.zhT.rearrange("(c p) t -> p c t", p=128)
            for it, (t0, n, isc) in enumerate(self.tiles()):
                z_t, bz = zt[it % 2], b_zt[it % 2]
                u_t, bu = u[it % 2], b_u[it % 2]
                o_t, bo = o[it % 2], b_o[it % 2]
                c0 = self.zcol(t0, isc)
                kb.dma("sp", z_t[:, :, 0:n + 2], zv[:, :, c0 - 1:c0 + n + 1], self.b_zhT, bz)
                for c in range(6):
                    kb.op("act", lambda e: e.activation(out=u_t[:, c, 0:n], in_=z_t[:, c, 1:n + 1], func=AF.Identity,
                                                        scale=cw[:, c, 1:2], bias=cw[:, c, 3:4]), reads=[bz, b_cw], writes=[bu])
                    kb.op("dve", lambda e: e.scalar_tensor_tensor(out=u_t[:, c, 0:n], in0=z_t[:, c, 0:n], scalar=cw[:, c, 0:1],
                                                                  in1=u_t[:, c, 0:n], op0=ALU.mult, op1=ALU.add),
                          reads=[bz, b_cw, bu], writes=[bu])
                    kb.op("dve", lambda e: e.scalar_tensor_tensor(out=u_t[:, c, 0:n], in0=z_t[:, c, 2:n + 2], scalar=cw[:, c, 2:3],
                                                                  in1=u_t[:, c, 0:n], op0=ALU.mult, op1=ALU.add),
                          reads=[bz, b_cw, bu], writes=[bu])
                kb.op("pool", lambda e: e.tensor_tensor(out=o_t[:, 0:2, 0:n], in0=u_t[:, 4:6, 0:n], in1=u_t[:, 2:4, 0:n], op=ALU.mult),
                      reads=[bu], writes=[bo])
                kb.op("pool", lambda e: e.tensor_copy(out=o_t[:, 2:4, 0:n], in_=u_t[:, 0:2, 0:n]), reads=[bu], writes=[bo])
                kb.dma("sp", self.vxT.rearrange("(c p) t -> p c t", p=128)[:, :, t0:t0 + n], o_t[:, 0:2, 0:n], bo, self.b_vxT)
                kb.dma("sp", self.x0T.rearrange("(c p) t -> p c t", p=128)[:, :, t0:t0 + n], o_t[:, 2:4, 0:n], bo, self.b_x0T)
            kb.end_phase()

    def phase_hyena_conv(self, l):
        kb = self.kb
        L = self.L
        with ExitStack() as es:
            Dd = self.dft_alloc(es, "hc_")
            skip = self.sb(es, "hc_skip", [128, 256], F32)
            b_skip = kb.buf("skip", dma=True)
            kb.dma("sp", skip[:], self.w["hy_skip"][l:l + 1, :].broadcast_to([128, 256]), None, b_skip)
            X = self.sb(es, "hc_X", [128, 64, 128], BF16)
            X0 = self.sb(es, "hc_X0", [128, 64, 128], BF16)
            Yr = self.sb(es, "hc_Yr", [128, 64, 128], BF16)
            Yi = self.sb(es, "hc_Yi", [128, 64, 128], BF16)
            O = self.sb(es, "hc_O", [128, 64, 128], F32)
            b_X, b_X0 = kb.buf("X", dma=True), kb.buf("X0", dma=True)
            b_Yr, b_Yi, b_O = kb.buf("Yr"), kb.buf("Yi"), kb.buf("O")
            kh = [self.sb(es, "hc_kh%d" % i, [128, 2, 512], F32) for i in range(2)]
            b_kh = [kb.buf("kh%d" % i, dma=True) for i in range(2)]
            tt = [self.sb(es, "hc_tt%d" % i, [128, 512], F32) for i in range(8)]
            b_tt = [kb.buf("tt%d" % i) for i in range(8)]
            ksp = self.sb(es, "hc_ksp", [128, 2, 512], F32)
            b_ksp = kb.buf("ksp")
            cnt = [0]
            for (tok0, Ls, kcol, Kh, bKh) in ((0, L, 0, self.Khat, self.b_Khat), (L, CTX, 2 * L, self.Khatc, self.b_Khatc)):
                N1 = 2 * Ls // 128
                P1 = Ls // 128
                Tf = self.dft_tabs(es, "hc%d_" % tok0, N1, False)
                Ti = self.dft_tabs(es, "hc%d_" % tok0, N1, True)
                G = min(512 // N1, 64)
                W = G * N1
                for c0 in range(0, 256, 64):
                    for h0 in (0, 32):
                        kb.dma("sp", X[0:N1, h0:h0 + 32, :], self.kfT[c0 + h0:c0 + h0 + 32, kcol:kcol + 2 * Ls].rearrange("c (a b) -> a c b", b=128), self.b_kfT, b_X)

                    def cb_k(g0, G_, W_, psr, bpsr, psi, bpsi, c0=c0):
                        kb.op("act", lambda e: e.copy(out=ksp[:, 0, 0:W_], in_=psr[:, 0:W_]), reads=[bpsr], writes=[b_ksp])
                        kb.op("dve", lambda e: e.tensor_copy(out=ksp[:, 1, 0:W_], in_=psi[:, 0:W_]), reads=[bpsi], writes=[b_ksp])
                        col = (c0 + g0) * N1
                        kb.dma("sp", Kh[:, :, col:col + W_].rearrange("s p n -> p s n"), ksp[:, :, 0:W_], b_ksp, bKh)
                    self.dft2(Dd, Tf, X, None, b_X, N1, True, 128, cb_k)
                for c0 in range(0, 256, 64):
                    for h0 in (0, 32):
                        kb.dma("sp", X[0:P1, h0:h0 + 32, :], self.vxT[c0 + h0:c0 + h0 + 32, tok0:tok0 + Ls].rearrange("c (a b) -> a c b", b=128), self.b_vxT, b_X)
                        kb.dma("sp", X0[0:P1, h0:h0 + 32, :], self.x0T[c0 + h0:c0 + h0 + 32, tok0:tok0 + Ls].rearrange("c (a b) -> a c b", b=128), self.b_x0T, b_X0)

                    def cb_f(g0, G_, W_, psr, bpsr, psi, bpsi, c0=c0):
                        i = cnt[0] % 2
                        cnt[0] += 1
                        k_t, bk = kh[i], b_kh[i]
                        col = (c0 + g0) * N1
                        kb.dma("sp", k_t[:, :, 0:W_], Kh[:, :, col:col + W_].rearrange("s p n -> p s n"), bKh, bk)
                        yr = Yr[:, g0:g0 + G_, 0:N1]
                        yi = Yi[:, g0:g0 + G_, 0:N1]
                        tv = [x[:, 0:W_] for x in tt[4 * i:4 * i + 4]]
                        btt = b_tt[4 * i:4 * i + 4]
                        kb.op("dve", lambda e: e.tensor_tensor(out=tv[0], in0=psr[:, 0:W_], in1=k_t[:, 0, 0:W_], op=ALU.mult), reads=[bpsr, bk], writes=[btt[0]])
                        kb.op("dve", lambda e: e.tensor_tensor(out=tv[1], in0=psi[:, 0:W_], in1=k_t[:, 1, 0:W_], op=ALU.mult), reads=[bpsi, bk], writes=[btt[1]])
                        kb.op("pool", lambda e: e.tensor_tensor(out=yr, in0=tv[0].rearrange("p (g k) -> p g k", g=G_),
                                                                in1=tv[1].rearrange("p (g k) -> p g k", g=G_), op=ALU.subtract),
                              reads=[btt[0], btt[1]], writes=[b_Yr])
                        kb.op("dve", lambda e: e.tensor_tensor(out=tv[2], in0=psr[:, 0:W_], in1=k_t[:, 1, 0:W_], op=ALU.mult), reads=[bpsr, bk], writes=[btt[2]])
                        kb.op("dve", lambda e: e.tensor_tensor(out=tv[3], in0=psi[:, 0:W_], in1=k_t[:, 0, 0:W_], op=ALU.mult), reads=[bpsi, bk], writes=[btt[3]])
                        kb.op("pool", lambda e: e.tensor_tensor(out=yi, in0=tv[2].rearrange("p (g k) -> p g k", g=G_),
                                                                in1=tv[3].rearrange("p (g k) -> p g k", g=G_), op=ALU.add),
                              reads=[btt[2], btt[3]], writes=[b_Yi])
                    self.dft2(Dd, Tf, X, None, b_X, P1, True, 128, cb_f)

                    def cb_i(g0, G_, W_, psr, bpsr, psi, bpsi, c0=c0):
                        ov = O[0:P1, g0:g0 + G_, :]
                        sk = skip[0:P1, c0 + g0:c0 + g0 + G_].unsqueeze(2).broadcast_to([P1, G_, 128])
                        kb.op("pool", lambda e: e.tensor_tensor(out=ov, in0=X[0:P1, g0:g0 + G_, :], in1=sk, op=ALU.mult),
                              reads=[b_X, b_skip], writes=[b_O])
                        kb.op("dve", lambda e: e.scalar_tensor_tensor(out=ov, in0=psr[0:P1, 0:W_].rearrange("p (g k) -> p g k", g=G_),
                                                                      scalar=1.0 / (2 * Ls), in1=ov, op0=ALU.mult, op1=ALU.add),
                              reads=[bpsr, b_O], writes=[b_O])
                        kb.op("dve", lambda e: e.tensor_tensor(out=ov, in0=ov, in1=X0[0:P1, g0:g0 + G_, :], op=ALU.mult),
                              reads=[b_O, b_X0], writes=[b_O])
                    self.dft2(Dd, Ti, Yr, Yi, [b_Yr, b_Yi], 128, False, P1, cb_i)
                    for h0 in (0, 32):
                        kb.dma("sp", self.mixT[512 + c0 + h0:512 + c0 + h0 + 32, tok0:tok0 + Ls].rearrange("c (a b) -> a c b", b=128), O[0:P1, h0:h0 + 32, :], b_O, self.b_mixT)
            kb.end_phase()

    def phase_fnet(self, l):
        kb = self.kb
        L = self.L
        with ExitStack() as es:
            Dd = self.dft_alloc(es, "fn_")
            Xr = self.sb(es, "fn_Xr", [128, 64, 128], BF16)
            Xi = self.sb(es, "fn_Xi", [128, 64, 128], BF16)
            b_Xr, b_Xi = kb.buf("Xr", dma=True), kb.buf("Xi", dma=True)
            fo = [self.sb(es, "fn_fo%d" % i, [128, 512], BF16) for i in range(2)]
            b_fo = [kb.buf("fo%d" % i) for i in range(2)]
            cnt = [0]
            for (tok0, Ls) in ((0, L), (L, CTX)):
                N1 = Ls // 128
                Tf = self.dft_tabs(es, "fn%d_" % tok0, N1, False)
                scale = 1.0 / math.sqrt(64.0 * Ls)
                for c0 in range(0, 256, 64):
                    for h0 in (0, 32):
                        kb.dma("sp", Xr[0:N1, h0:h0 + 32, :], self.faT[c0 + h0:c0 + h0 + 32, tok0:tok0 + Ls].rearrange("c (a b) -> a c b", b=128), self.b_faT, b_Xr)
                        kb.dma("sp", Xi[0:N1, h0:h0 + 32, :], self.fbT[c0 + h0:c0 + h0 + 32, tok0:tok0 + Ls].rearrange("c (a b) -> a c b", b=128), self.b_fbT, b_Xi)

                    def cb(g0, G_, W_, psr, bpsr, psi, bpsi, c0=c0, N1=N1, tok0=tok0, Ls=Ls, scale=scale):
                        i = cnt[0] % 2
                        cnt[0] += 1
                        f_t, bf_ = fo[i], b_fo[i]
                        kb.op("act", lambda e: e.activation(out=f_t[:, 0:W_], in_=psr[:, 0:W_], func=AF.Copy, scale=scale), reads=[bpsr], writes=[bf_])
                        kb.dma("sp", self.fT[c0 + g0:c0 + g0 + G_, tok0:tok0 + Ls].rearrange("c (a b) -> a c b", b=N1),
                               f_t[:, 0:W_].rearrange("p (g k) -> p g k", g=G_), bf_, self.b_fT)
                    self.dft2(Dd, Tf, Xr, Xi, [b_Xr, b_Xi], N1, False, 128, cb)
            kb.end_phase()
        with ExitStack() as es:
            wf = self.sb(es, "fl_w", [128, 2, 256], BF16)
            b_wf = kb.buf("wf", dma=True)
            kb.dma("pool", wf[:], self.w["fnet_w"][l].rearrange("(k p) n -> p k n", p=128), None, b_wf)
            bias = self.sb(es, "fl_b", [128, 2], F32)
            b_bias = kb.buf("bias", dma=True)
            kb.dma("sp", bias[:], self.w["fnet_b"][l:l + 1, :].rearrange("o (c p) -> p (o c)", p=128), None, b_bias, allow_slow_non_contiguous=True)
            ft = [self.sb(es, "fl_f%d" % i, [128, 2, 512], BF16) for i in range(2)]
            b_ft = [kb.buf("f%d" % i, dma=True) for i in range(2)]
            ot = [self.sb(es, "fl_o%d" % i, [128, 2, 512], F32) for i in range(2)]
            b_ot = [kb.buf("o%d" % i) for i in range(2)]
            pst = [self.ps(es, "fl_ps%d" % i, [128, 512]) for i in range(2)]
            b_ps = [kb.buf("ps%d" % i) for i in range(2)]
            for it, (t0, n, isc) in enumerate(self.tiles()):
                f_t, bf_ = ft[it % 2], b_ft[it % 2]
                o_t, bo = ot[it % 2], b_ot[it % 2]
                kb.dma("sp", f_t[:, :, 0:n], self.fT.rearrange("(c p) t -> p c t", p=128)[:, :, t0:t0 + n], self.b_fT, bf_)
                for oc in range(2):
                    p_t, bp = pst[oc], b_ps[oc]
                    kb.mm(p_t[:, 0:n], [(wf[:, k, oc * 128:(oc + 1) * 128], f_t[:, k, 0:n]) for k in range(2)], reads=[b_wf, bf_], writes=[bp])
                    kb.op("act", lambda e: e.activation(out=o_t[:, oc, 0:n], in_=p_t[:, 0:n], func=AF.Identity, bias=bias[:, oc:oc + 1]),
                          reads=[bp, b_bias], writes=[bo])
                kb.dma("sp", self.mixT.rearrange("(c p) t -> p c t", p=128)[:, 6:8, t0:t0 + n], o_t[:, :, 0:n], bo, self.b_mixT)
            kb.end_phase()
    def load_vec8(self, es, name, src_row, b):
        t = self.sb(es, name, [128, NCH], F32)
        self.kb.dma("sp", t[:], src_row.rearrange("o (c p) -> p (o c)", p=128), None, b, allow_slow_non_contiguous=True)
        return t

    def phase_merge(self, l, xin, b_xin, xout, b_xout):
        kb = self.kb
        with ExitStack() as es:
            C = self.load_consts_common(es, "p5_")
            S = self.ln_alloc(es, "p5_", 512)
            ones_g = self.sb(es, "p5_onesg", [128, 128], BF16)
            kb.op("dve", lambda e: e.memset(ones_g[:], 1.0 / 256.0), writes=[C["b"]])
            wo = self.sb(es, "p5_wo", [128, NCH, D], BF16)
            b_wo = kb.buf("wo", dma=True)
            for k in range(NCH):
                kb.dma("pool", wo[:, k, :], self.w["w_out"][l, k * 128:(k + 1) * 128, :], None, b_wo)
            b_v = kb.buf("vecs", dma=True)
            gout = self.load_vec8(es, "p5_gout", self.w["out_norm_g"][l:l + 1, :], b_v)
            bout = self.load_vec8(es, "p5_bout", self.w["b_out"][l:l + 1, :], b_v)
            lng = self.load_vec8(es, "p5_lng", self.w["ln1_g"][l:l + 1, :], b_v)
            lnb = self.load_vec8(es, "p5_lnb", self.w["ln1_b"][l:l + 1, :], b_v)
            gb = self.sb(es, "p5_gb", [128, NCH, 2], F32)
            for j in range(2):
                kb.op("dve", lambda e: e.tensor_tensor(out=gb[:, :, j], in0=bout[:], in1=self.modT[:, l, 16:24, j], op=ALU.mult),
                      reads=[b_v, self.b_modT], writes=[b_v])
            mt = [self.sb(es, "p5_m%d" % i, [128, NCH, 512], F32) for i in range(2)]
            b_mt = [kb.buf("m%d" % i, dma=True) for i in range(2)]
            xt = [self.sb(es, "p5_x%d" % i, [128, NCH, 512], F32) for i in range(2)]
            b_xt = [kb.buf("x%d" % i, dma=True) for i in range(2)]
            sqms = [self.sb(es, "p5_sqm%d" % i, [128, NCH, 512], BF16) for i in range(2)]
            b_sqms = [kb.buf("sqm%d" % i) for i in range(2)]
            psg = self.ps(es, "p5_psg", [128, 4, 512])
            b_psg = kb.buf("psg")
            rg = self.sb(es, "p5_rg", [128, 4, 512], F32)
            b_rg = kb.buf("rg")
            mbs = [self.sb(es, "p5_mb%d" % i, [128, NCH, 512], BF16) for i in range(2)]
            b_mbs = [kb.buf("mb%d" % i) for i in range(2)]
            psy = [self.ps(es, "p5_psy%d" % i, [128, 512]) for i in range(2)]
            b_psy = [kb.buf("psy%d" % i) for i in range(2)]
            rs = [self.sb(es, "p5_r%d" % i, [128, NCH, 512], F32) for i in range(2)]
            b_rs = [kb.buf("r%d" % i) for i in range(2)]
            eps_c = C["eps"]
            tiles = self.tiles()

            def front(it, stage):
                t0, n, isc = tiles[it]
                j = 1 if isc else 0
                m_t, bm = mt[it % 2], b_mt[it % 2]
                x_t, bx = xt[it % 2], b_xt[it % 2]
                sqm, b_sqm = sqms[it % 2], b_sqms[it % 2]
                mb, b_mb = mbs[it % 2], b_mbs[it % 2]
                r, b_r = rs[it % 2], b_rs[it % 2]
                if stage == 0:
                    kb.dma("sp", m_t[:, :, 0:n], self.mixT.rearrange("(c p) t -> p c t", p=128)[:, :, t0:t0 + n], self.b_mixT, bm)
                    kb.dma("sp", x_t[:, :, 0:n], xin.rearrange("(c p) t -> p c t", p=128)[:, :, t0:t0 + n], b_xin, bx)
                elif stage == 1:
                    kb.op("act", lambda e: e.activation(out=sqm[:, :, 0:n], in_=m_t[:, :, 0:n], func=AF.Square), reads=[bm], writes=[b_sqm])
                    for g in range(4):
                        kb.mm(psg[:, g, 0:n], [(ones_g[:], sqm[:, 2 * g, 0:n]), (ones_g[:], sqm[:, 2 * g + 1, 0:n])], reads=[b_sqm, C["b"]], writes=[b_psg])
                    kb.op("act", lambda e: e.activation(out=rg[:, :, 0:n], in_=psg[:, :, 0:n], func=AF.Ln, bias=eps_c[:, 0:1]), reads=[b_psg, C["b"]], writes=[b_rg])
                    kb.op("act", lambda e: e.activation(out=rg[:, :, 0:n], in_=rg[:, :, 0:n], func=AF.Exp, scale=-0.5), reads=[b_rg], writes=[b_rg])
                elif stage == 2:
                    for g in range(4):
                        kb.op("dve", lambda e: e.tensor_tensor(out=m_t[:, 2 * g:2 * g + 2, 0:n], in0=m_t[:, 2 * g:2 * g + 2, 0:n],
                                                               in1=rg[:, g, 0:n].unsqueeze(1).broadcast_to([128, 2, n]), op=ALU.mult),
                              reads=[bm, b_rg], writes=[bm])
                    for k in range(NCH):
                        kb.op("act" if k % 2 == 0 else "pool",
                              (lambda e: e.activation(out=mb[:, k, 0:n], in_=m_t[:, k, 0:n], func=AF.Copy, scale=gout[:, k:k + 1])) if k % 2 == 0 else
                              (lambda e: e.tensor_scalar(out=mb[:, k, 0:n], in0=m_t[:, k, 0:n], scalar1=gout[:, k:k + 1], scalar2=None, op0=ALU.mult)),
                              reads=[bm, b_v], writes=[b_mb])
                else:
                    for oc in range(NCH):
                        p_t, bp = psy[oc % 2], b_psy[oc % 2]
                        kb.mm(p_t[:, 0:n], [(wo[:, k, oc * 128:(oc + 1) * 128], mb[:, k, 0:n]) for k in range(NCH)], reads=[b_wo, b_mb], writes=[bp])
                        kb.op("act", lambda e: e.activation(out=r[:, oc, 0:n], in_=p_t[:, 0:n], func=AF.Identity,
                                                            scale=self.modT[:, l, 16 + oc, j:j + 1], bias=gb[:, oc, j:j + 1]),
                              reads=[bp, self.b_modT, b_v], writes=[b_r])
                        kb.op("dve", lambda e: e.scalar_tensor_tensor(out=r[:, oc, 0:n], in0=x_t[:, oc, 0:n], scalar=float(self.alpha),
                                                                      in1=r[:, oc, 0:n], op0=ALU.mult, op1=ALU.add),
                              reads=[bx, b_r], writes=[b_r])

            def back(it, stage):
                t0, n, isc = tiles[it]
                r, b_r = rs[it % 2], b_rs[it % 2]
                if stage < 2:
                    self.ln_stage(stage, r, b_r, n, S, C)
                else:
                    for k in range(NCH):
                        kb.op("act", lambda e: e.activation(out=r[:, k, 0:n], in_=S["xc"][:, k, 0:n], func=AF.Identity,
                                                            scale=lng[:, k:k + 1], bias=lnb[:, k:k + 1]),
                              reads=[S["b_xc"], b_v], writes=[b_r])
                    kb.dma("sp", xout.rearrange("(c p) t -> p c t", p=128)[:, :, t0:t0 + n], r[:, :, 0:n], b_r, b_xout)

            nt = len(tiles)
            front(0, 0)
            for it in range(nt + 1):
                if it + 1 < nt:
                    front(it + 1, 0)
                for st in range(3):
                    if it < nt:
                        front(it, st + 1)
                    if it >= 1:
                        back(it - 1, st)
            kb.end_phase()

    def ffn_tiles(self):
        out = []
        for (s0, s1, isc) in ((0, self.L, False), (self.L, self.T, True)):
            t = s0
            while t < s1:
                nv = min(510, s1 - t)
                out.append((t, nv, s0, s1, isc))
                t += nv
        return out

    def phase_wprep(self):
        kb = self.kb
        with ExitStack() as es:
            st = [self.sb(es, "wp_s%d" % i, [128, NCH, 256], BF16) for i in range(3)]
            b_st = [kb.buf("s%d" % i, dma=True) for i in range(3)]
            it = 0
            for l in range(self.depth):
                wv = self.w["ffn_w_up"][l].rearrange("(k p) n -> p k n", p=128)
                for f in range(22):
                    sz = 128 if f < 21 else 64
                    s_t, bs = st[it % 3], b_st[it % 3]
                    it += 1
                    kb.dma("pool", s_t[:, :, 0:sz], wv[:, :, f * 128:f * 128 + sz], None, bs)
                    kb.dma("pool", s_t[:, :, 128:128 + sz], wv[:, :, DFF + f * 128:DFF + f * 128 + sz], None, bs)
                    kb.dma("sp", self.wup[l, f], s_t[:], bs, self.b_wup)
            kb.end_phase()

    def ln_stage(self, stage, X, bX, n, S, C):
        kb = self.kb
        xc, sq, rstd, psm, psv = S["xc"], S["sq"], S["rstd"], S["psm"], S["psv"]
        if stage == 0:
            kb.mm(psm[:, 0:n], [(C["ones_f"][:], X[:, k, 0:n]) for k in range(NCH)], reads=[bX, C["b"]], writes=[S["b_psm"]])
            kb.op("dve", lambda e: e.tensor_tensor(out=xc[:, :, 0:n], in0=X[:, :, 0:n],
                                                   in1=psm[:, 0:n].unsqueeze(1).broadcast_to([128, NCH, n]), op=ALU.subtract),
                  reads=[bX, S["b_psm"]], writes=[S["b_xc"]])
            kb.op("act", lambda e: e.activation(out=sq[:, :, 0:n], in_=xc[:, :, 0:n], func=AF.Square), reads=[S["b_xc"]], writes=[S["b_sq"]])
        elif stage == 1:
            kb.mm(psv[:, 0:n], [(C["ones_b"][:], sq[:, k, 0:n]) for k in range(NCH)], reads=[S["b_sq"], C["b"]], writes=[S["b_psv"]])
            kb.op("act", lambda e: e.activation(out=rstd[:, 0:n], in_=psv[:, 0:n], func=AF.Ln, bias=C["eps"][:, 0:1]),
                  reads=[S["b_psv"], C["b"]], writes=[S["b_rstd"]])
            kb.op("act", lambda e: e.activation(out=rstd[:, 0:n], in_=rstd[:, 0:n], func=AF.Exp, scale=-0.5), reads=[S["b_rstd"]], writes=[S["b_rstd"]])
            kb.op("dve", lambda e: e.tensor_tensor(out=xc[:, :, 0:n], in0=xc[:, :, 0:n],
                                                   in1=rstd[:, 0:n].unsqueeze(1).broadcast_to([128, NCH, n]), op=ALU.mult),
                  reads=[S["b_xc"], S["b_rstd"]], writes=[S["b_xc"]])

    def phase_ffn(self, l, xin, b_xin, xout, b_xout):
        kb = self.kb
        with ExitStack() as es:
            C = self.load_consts_common(es, "p6_")
            S = self.ln_alloc(es, "p6_", 512)
            wd = self.sb(es, "p6_wd", [128, 22, D], BF16)
            b_wd = kb.buf("wd", dma=True)
            for f in range(22):
                sz = 128 if f < 21 else 64
                kb.dma("pool", wd[0:sz, f, :], self.w["ffn_w_down"][l, f * 128:f * 128 + sz, :], None, b_wd)
            b_v = kb.buf("vecs", dma=True)
            bdn = self.load_vec8(es, "p6_bdn", self.w["ffn_b_down"][l:l + 1, :], b_v)
            lng = self.load_vec8(es, "p6_lng", self.w["ln2_g"][l:l + 1, :], b_v)
            lnb = self.load_vec8(es, "p6_lnb", self.w["ln2_b"][l:l + 1, :], b_v)
            hv = self.sb(es, "p6_hv", [128, 44, 8], F32)
            kb.op("dve", lambda e: e.memset(hv[:], 0.0), writes=[b_v])
            srcs = [self.w["ffn_b_up"][l:l + 1, :], self.w["ffn_conv_w"][l, 0:1, :], self.w["ffn_conv_w"][l, 1:2, :],
                    self.w["ffn_conv_w"][l, 2:3, :], self.w["ffn_conv_b"][l:l + 1, :]]
            for vi, src in enumerate(srcs):
                for half in range(2):
                    base = half * DFF
                    kb.dma("sp", hv[:, 22 * half:22 * half + 21, vi:vi + 1],
                           src[:, base:base + 21 * 128].rearrange("o (c p) -> p c o", p=128), None, b_v, allow_slow_non_contiguous=True)
                    kb.dma("sp", hv[0:64, 22 * half + 21:22 * half + 22, vi:vi + 1],
                           src[:, base + 21 * 128:base + DFF].rearrange("o (c p) -> p c o", p=64), None, b_v, allow_slow_non_contiguous=True)
            kb.op("dve", lambda e: e.tensor_tensor(out=hv[:, :, 5:6], in0=hv[:, :, 1:2], in1=hv[:, :, 2:3], op=ALU.add), reads=[b_v], writes=[b_v])
            kb.op("dve", lambda e: e.tensor_tensor(out=hv[:, :, 5:6], in0=hv[:, :, 5:6], in1=hv[:, :, 3:4], op=ALU.add), reads=[b_v], writes=[b_v])
            kb.op("dve", lambda e: e.tensor_tensor(out=hv[:, :, 5:6], in0=hv[:, :, 5:6], in1=hv[:, :, 0:1], op=ALU.mult), reads=[b_v], writes=[b_v])
            kb.op("dve", lambda e: e.tensor_tensor(out=hv[:, :, 5:6], in0=hv[:, :, 5:6], in1=hv[:, :, 4:5], op=ALU.add), reads=[b_v], writes=[b_v])
            kb.op("dve", lambda e: e.scalar_tensor_tensor(out=hv[:, :, 6:7], in0=hv[:, :, 1:2], scalar=-1.0, in1=hv[:, :, 0:1], op0=ALU.mult, op1=ALU.mult),
                  reads=[b_v], writes=[b_v])
            kb.op("dve", lambda e: e.scalar_tensor_tensor(out=hv[:, :, 7:8], in0=hv[:, :, 3:4], scalar=-1.0, in1=hv[:, :, 0:1], op0=ALU.mult, op1=ALU.mult),
                  reads=[b_v], writes=[b_v])
            gb = self.sb(es, "p6_gb", [128, NCH, 2], F32)
            for j in range(2):
                kb.op("dve", lambda e: e.tensor_tensor(out=gb[:, :, j], in0=bdn[:], in1=self.modT[:, l, 40:48, j], op=ALU.mult),
                      reads=[b_v, self.b_modT], writes=[b_v])
            xt = [self.sb(es, "p6_x%d" % i, [128, NCH, 512], F32) for i in range(2)]
            b_xt = [kb.buf("x%d" % i, dma=True) for i in range(2)]
            hTs = [self.sb(es, "p6_hT%d" % i, [128, NCH, 512], BF16) for i in range(2)]
            b_hTs = [kb.buf("hT%d" % i) for i in range(2)]
            wu = [self.sb(es, "p6_wu%d" % i, [128, NCH, 256], BF16) for i in range(3)]
            b_wu = [kb.buf("wu%d" % i, dma=True) for i in range(3)]
            psu = [self.ps(es, "p6_psu%d" % i, [128, 2, 512]) for i in range(3)]
            b_psu = [kb.buf("psu%d" % i) for i in range(3)]
            cc = [self.sb(es, "p6_c%d" % i, [128, 2, 512], F32) for i in range(3)]
            b_cc = [kb.buf("c%d" % i) for i in range(3)]
            sl = [self.sb(es, "p6_sl%d" % i, [128, 512], F32) for i in range(2)]
            b_sl = [kb.buf("sl%d" % i) for i in range(2)]
            pT = self.sb(es, "p6_pT", [128, 22, 512], BF16)
            b_pT = kb.buf("pT")
            psy = [S["psm"], S["psv"]]
            b_psy = [S["b_psm"], S["b_psv"]]
            r = self.sb(es, "p6_r", [128, NCH, 512], F32)
            b_r = kb.buf("r")
            xv = xin.rearrange("(c p) t -> p c t", p=128)
            tiles = self.ffn_tiles()

            def load_x(it):
                t0, nv, s0, s1, isc = tiles[it]
                x_t, bx = xt[it % 2], b_xt[it % 2]
                lo, hi = max(s0, t0 - 1), min(s1, t0 + nv + 1)
                c_lo = lo - (t0 - 1)
                kb.op("pool", lambda e: e.memset(x_t[:, :, 0:nv + 2], 0.0), writes=[bx])
                kb.dma("sp", x_t[:, :, c_lo:c_lo + hi - lo], xv[:, :, lo:hi], b_xin, bx)

            def prologue(it, stage):
                t0, nv, s0, s1, isc = tiles[it]
                j = 1 if isc else 0
                x_t, bx = xt[it % 2], b_xt[it % 2]
                hT, b_hT = hTs[it % 2], b_hTs[it % 2]
                ncol = nv + 2
                if stage < 2:
                    self.ln_stage(stage, x_t, bx, ncol, S, C)
                    return
                for k in range(NCH):
                    kb.op("act", lambda e: e.activation(out=hT[:, k, 0:ncol], in_=S["xc"][:, k, 0:ncol], func=AF.Identity,
                                                        scale=self.modT[:, l, 32 + k, j:j + 1], bias=self.modT[:, l, 24 + k, j:j + 1]),
                          reads=[S["b_xc"], self.b_modT], writes=[b_hT])
                if t0 == s0:
                    kb.op("pool", lambda e: e.memset(hT[:, :, 0:1], 0.0), writes=[b_hT])
                if t0 + nv == s1:
                    kb.op("pool", lambda e: e.memset(hT[:, :, nv + 1:nv + 2], 0.0), writes=[b_hT])

            def epilogue(ep, stage):
                et0, env = ep
                self.ln_stage(stage, r, b_r, env, S, C)
                if stage == 1:
                    for k in range(NCH):
                        kb.op("act", lambda e: e.activation(out=r[:, k, 0:env], in_=S["xc"][:, k, 0:env], func=AF.Identity,
                                                            scale=lng[:, k:k + 1], bias=lnb[:, k:k + 1]),
                              reads=[S["b_xc"], b_v], writes=[b_r])
                    kb.dma("sp", xout.rearrange("(c p) t -> p c t", p=128)[:, :, et0:et0 + env], r[:, :, 0:env], b_r, b_xout)

            load_x(0)
            for st in range(3):
                prologue(0, st)
            fi = 0
            epi = None
            for it, (t0, nv, s0, s1, isc) in enumerate(tiles):
                j = 1 if isc else 0
                x_t, bx = xt[it % 2], b_xt[it % 2]
                hT, b_hT = hTs[it % 2], b_hTs[it % 2]
                ncol = nv + 2
                has_next = it + 1 < len(tiles)
                if has_next:
                    load_x(it + 1)
                pend = None

                def stage2(pp):
                    pf, psz_, pc_t, pbc = pp
                    s_t, bs = sl[pf % 2], b_sl[pf % 2]
                    kb.op("act", lambda e: e.activation(out=s_t[0:psz_, 0:nv], in_=pc_t[0:psz_, 0, 0:nv], func=AF.Silu), reads=[pbc], writes=[bs])
                    kb.op("pool", lambda e: e.tensor_tensor(out=pT[0:psz_, pf, 0:nv], in0=s_t[0:psz_, 0:nv], in1=pc_t[0:psz_, 1, 0:nv], op=ALU.mult),
                          reads=[bs, pbc], writes=[b_pT])

                for f in range(22):
                    sz = 128 if f < 21 else 64
                    w_t, bw = wu[fi % 3], b_wu[fi % 3]
                    p_t, bp = psu[fi % 3], b_psu[fi % 3]
                    c_t, bc = cc[fi % 3], b_cc[fi % 3]
                    fi += 1
                    kb.dma("sp", w_t[:], self.wup[l, f], self.b_wup, bw)
                    for ag in range(2):
                        kb.mm(p_t[0:sz, ag, 0:ncol], [(w_t[:, k, ag * 128:ag * 128 + sz], hT[:, k, 0:ncol]) for k in range(NCH)],
                              reads=[bw, b_hT], writes=[bp])
                    for ag in range(2):
                        hc = 22 * ag + f
                        kb.op("act", lambda e: e.activation(out=c_t[0:sz, ag, 0:nv], in_=p_t[0:sz, ag, 1:nv + 1], func=AF.Identity,
                                                            scale=hv[0:sz, hc, 2:3], bias=hv[0:sz, hc, 5:6]), reads=[bp, b_v], writes=[bc])
                    for ag in range(2):
                        hc = 22 * ag + f
                        kb.op("dve", lambda e: e.scalar_tensor_tensor(out=c_t[0:sz, ag, 0:nv], in0=p_t[0:sz, ag, 0:nv], scalar=hv[0:sz, hc, 1:2],
                                                                      in1=c_t[0:sz, ag, 0:nv], op0=ALU.mult, op1=ALU.add),
                              reads=[bp, b_v, bc], writes=[bc])
                        kb.op("dve", lambda e: e.scalar_tensor_tensor(out=c_t[0:sz, ag, 0:nv], in0=p_t[0:sz, ag, 2:nv + 2], scalar=hv[0:sz, hc, 3:4],
                                                                      in1=c_t[0:sz, ag, 0:nv], op0=ALU.mult, op1=ALU.add),
                              reads=[bp, b_v, bc], writes=[bc])
                        if t0 == s0:
                            kb.op("pool", lambda e: e.tensor_scalar(out=c_t[0:sz, ag, 0:1], in0=c_t[0:sz, ag, 0:1], scalar1=hv[0:sz, hc, 6:7],
                                                                    scalar2=None, op0=ALU.add), reads=[bc, b_v], writes=[bc])
                        if t0 + nv == s1:
                            kb.op("pool", lambda e: e.tensor_scalar(out=c_t[0:sz, ag, nv - 1:nv], in0=c_t[0:sz, ag, nv - 1:nv], scalar1=hv[0:sz, hc, 7:8],
                                                                    scalar2=None, op0=ALU.add), reads=[bc, b_v], writes=[bc])
                    if pend is not None:
                        stage2(pend)
                    pend = (f, sz, c_t, bc)
                    if has_next and f in (5, 11, 16):
                        prologue(it + 1, (5, 11, 16).index(f))
                    if epi is not None and f in (1, 3):
                        epilogue(epi, 0 if f == 1 else 1)
                        if f == 3:
                            epi = None
                stage2(pend)
                for oc in range(NCH):
                    p_t, bp = psy[oc % 2], b_psy[oc % 2]
                    kb.mm(p_t[:, 0:nv], [(wd[0:(128 if f < 21 else 64), f, oc * 128:(oc + 1) * 128], pT[0:(128 if f < 21 else 64), f, 0:nv])
                                         for f in range(22)], reads=[b_wd, b_pT], writes=[bp])
                    kb.op("act", lambda e: e.activation(out=r[:, oc, 0:nv], in_=p_t[:, 0:nv], func=AF.Identity,
                                                        scale=self.modT[:, l, 40 + oc, j:j + 1], bias=gb[:, oc, j:j + 1]),
                          reads=[bp, self.b_modT, b_v], writes=[b_r])
                    kb.op("dve", lambda e: e.scalar_tensor_tensor(out=r[:, oc, 0:nv], in0=x_t[:, oc, 1:nv + 1], scalar=float(self.alpha),
                                                                  in1=r[:, oc, 0:nv], op0=ALU.mult, op1=ALU.add),
                          reads=[bx, b_r], writes=[b_r])
                epi = (t0, nv)
            if epi is not None:
                epilogue(epi, 0)
                epilogue(epi, 1)
            kb.end_phase()
    def build(self, stop_after=None):
        nc = self.nc
        L, T, depth = self.L, self.T, self.depth
        nc.allow_low_precision("bf16 matmul operands, fp32 accumulation")
        self.x_in = self.din("x", [L, D])
        self.ctx_in = self.din("ctx", [CTX, D])
        self.c_in = self.din("c", [1, D])
        self.cctx_in = self.din("c_ctx", [1, D])
        self.w = {}
        for name, shape in WEIGHT_SHAPES.items():
            self.w[name] = self.din(name, [depth] + list(shape))
        self.c_ident_f = self.din("c_ident_f", [128, 128])
        self.c_ident_b = self.din("c_ident_b", [128, 128], BF16)
        self.c_bdcs = self.din("c_bdcs", [128, 2, 128], BF16)
        self.c_rope = self.din("c_rope", [T, 2, 32])
        self.c_wmask = self.din("c_wmask", [128, 6, 512], BF16)
        NT = 2 * L + 512 + 1
        self.c_hz = self.din("c_hz", [HY_EMB, NT])
        self.c_hdec = self.din("c_hdec", [256, NT])
        self.c_dft = {}
        for (N1, inv) in dft_plans(L):
            key = "%d_%s" % (N1, "i" if inv else "f")
            KA, MA, FA = (128, N1, 128) if inv else (N1, 128, N1)
            self.c_dft["tabA_" + key] = self.din("c_tabA_" + key, [KA, 2, 2 * FA], BF16)
            self.c_dft["tw_" + key] = self.din("c_tw_" + key, [MA, 2, FA])
            self.c_dft["tabB_" + key] = self.din("c_tabB_" + key, [MA, 4, MA], BF16)
        self.out = self.nc.dram_tensor("out", [L, D], F32, kind="ExternalOutput").ap()
        self.xA = self.dscr("xA", [D, T])
        self.xB = self.dscr("xB", [D, T])
        self.modv = self.dscr("modv", [depth, 2, 6 * D])
        self.QKa = self.dscr("QKa", [384, T], BF16)
        self.QKb = self.dscr("QKb", [384, T], BF16)
        self.Va = self.dscr("Va", [T, 128], BF16)
        self.Vb = self.dscr("Vb", [T, 128], BF16)
        self.zhT = self.dscr("zhT", [768, T + 4], BF16)
        self.faT = self.dscr("faT", [256, T], BF16)
        self.fbT = self.dscr("fbT", [256, T], BF16)
        self.mixT = self.dscr("mixT", [D, T])
        self.kfT = self.dscr("kfT", [256, 2 * L + 512], BF16)
        self.vxT = self.dscr("vxT", [256, T], BF16)
        self.x0T = self.dscr("x0T", [256, T], BF16)
        self.fT = self.dscr("fT", [256, T], BF16)
        self.wup = self.dscr("wup", [depth, 22, 128, NCH, 256], BF16)
        self.alpha = 8.0 ** 0.25
        self.Khat = self.dscr("Khat", [2, 128, 256 * (2 * L // 128)])
        self.Khatc = self.dscr("Khatc", [2, 128, 256 * 4])
        with ExitStack() as es:
            self.kb = kb = KB(nc, es)
            for nm in ("xA", "xB", "out", "modv", "QKa", "QKb", "Va", "Vb", "zhT", "faT", "fbT", "mixT",
                       "kfT", "vxT", "x0T", "fT", "Khat", "Khatc", "wup"):
                setattr(self, "b_" + nm, kb.buf(nm, dma=True, persistent=True))
            self.modT = self.sb(es, "modT", [128, depth, 48, 2], F32)
            self.b_modT = kb.buf("modT", dma=True, persistent=True)
            phases = []
            phases.append(("zinit", self.phase_zinit))
            phases.append(("tin", lambda: self.phase_tin(self.x_in, self.ctx_in, self.xA, self.b_xA)))
            phases.append(("mod", self.phase_mod))
            phases.append(("wprep", self.phase_wprep))
            for l in range(depth):
                phases.append(("proj%d" % l, lambda l=l: self.phase_proj(l, self.xA, self.b_xA)))
                phases.append(("attna%d" % l, lambda l=l: self.phase_attn(l, "a")))
                phases.append(("attnb%d" % l, lambda l=l: self.phase_attn(l, "b")))
                phases.append(("hyf%d" % l, lambda l=l: self.phase_hyena_filter(l)))
                phases.append(("hyp%d" % l, lambda l=l: self.phase_hyena_prep(l)))
                phases.append(("hyc%d" % l, lambda l=l: self.phase_hyena_conv(l)))
                phases.append(("fnet%d" % l, lambda l=l: self.phase_fnet(l)))
                phases.append(("merge%d" % l, lambda l=l: self.phase_merge(l, self.xA, self.b_xA, self.xB, self.b_xB)))
                phases.append(("ffn%d" % l, lambda l=l: self.phase_ffn(l, self.xB, self.b_xB, self.xA, self.b_xA)))
            phases.append(("tout", lambda: self.phase_tout(self.xA, self.b_xA, self.out, self.b_out)))
            for name, fn in phases:
                fn()
                if stop_after is not None and name == stop_after:
                    break
            kb.barrier()
        return nc


WEIGHT_SHAPES = {
    "w_ada": (D, 6 * D), "b_ada": (6 * D,), "w_in": (D, DPROJ), "sink_a": (4,), "q_norm_g": (HD,), "k_norm_g": (HD,),
    "hy_conv_w": (3, 768), "hy_conv_b": (768,), "hy_f_w1": (HY_EMB, HY_HID), "hy_f_b1": (HY_HID,), "hy_f_freq": (HY_HID,),
    "hy_f_w2": (HY_HID, HY_HID), "hy_f_b2": (HY_HID,), "hy_f_w3": (HY_HID, 512), "hy_skip": (256,),
    "fnet_w": (256, 256), "fnet_b": (256,), "out_norm_g": (D,), "w_out": (D, D), "b_out": (D,),
    "ln1_g": (D,), "ln1_b": (D,), "ffn_w_up": (D, 2 * DFF), "ffn_b_up": (2 * DFF,), "ffn_conv_w": (3, 2 * DFF),
    "ffn_conv_b": (2 * DFF,), "ffn_w_down": (DFF, D), "ffn_b_down": (D,), "ln2_g": (D,), "ln2_b": (D,),
}


def dft_plans(L):
    s = set()
    for N1 in (2 * L // 128, 2 * CTX // 128):
        s.add((N1, False))
        s.add((N1, True))
    for N1 in (L // 128, CTX // 128):
        s.add((N1, False))
    return sorted(s)


def dft_tables(N1, inv):
    bf = ml_dtypes.bfloat16
    N = N1 * 128
    if not inv:
        n1 = np.arange(N1)[:, None]
        k1 = np.arange(N1)[None, :]
        a = 2 * np.pi * ((n1 * k1) % N1) / N1
        tabA = np.stack([np.concatenate([np.cos(a), -np.sin(a)], 1), np.concatenate([np.sin(a), np.cos(a)], 1)], 1)
        n2 = np.arange(128)[:, None]
        t = 2 * np.pi * ((n2 * k1) % N) / N
        tw = np.stack([np.cos(t), np.sin(t)], 1)
        k2 = np.arange(128)[None, :]
        b = 2 * np.pi * ((n2 * k2) % 128) / 128
        tabB = np.stack([np.cos(b), np.sin(b), -np.sin(b), np.cos(b)], 1)
    else:
        k2 = np.arange(128)[:, None]
        nl = np.arange(128)[None, :]
        a = 2 * np.pi * ((k2 * nl) % 128) / 128
        tabA = np.stack([np.concatenate([np.cos(a), np.sin(a)], 1), np.concatenate([-np.sin(a), np.cos(a)], 1)], 1)
        k1 = np.arange(N1)[:, None]
        t = 2 * np.pi * ((k1 * nl) % N) / N
        tw = np.stack([np.cos(t), -np.sin(t)], 1)
        nh = np.arange(N1)[None, :]
        b = 2 * np.pi * ((k1 * nh) % N1) / N1
        tabB = np.stack([np.cos(b), -np.sin(b), np.sin(b), np.cos(b)], 1)
    return tabA.astype(bf), tw.astype(np.float32), tabB.astype(bf)


def hyena_tables(L):
    def feats(Ls, pos):
        t = pos.astype(np.float64)
        t_norm = t / max(Ls - 1, 1)
        bands = np.linspace(1e-4, HY_BANDS - 1, HY_BANDS)
        ang = 2.0 * np.pi * t[:, None] * bands[None, :] / Ls
        z = np.concatenate([t_norm[:, None], np.cos(ang), -np.sin(ang)], axis=-1)
        min_decay = math.log(1e-2) / 1.5
        max_decay = math.log(1e-2) / 0.3
        deltas = np.abs(np.linspace(min_decay, max_decay, 256))
        dec = np.exp(-t_norm[:, None] * deltas[None, :])
        return z, dec
    zs, ds = [], []
    for Ls in (L, CTX):
        z, d = feats(Ls, np.arange(Ls))
        zs.append(z); ds.append(d)
        pos = Ls - np.arange(Ls)
        pos[0] = 0
        z, d = feats(Ls, pos)
        d[0] = 0.0
        zs.append(z); ds.append(d)
    z, d = feats(L, np.arange(1))
    zs.append(z); ds.append(d)
    order = [0, 1, 2, 3, 4]
    Z = np.concatenate([zs[i] for i in order], 0).T
    Dd = np.concatenate([ds[i] for i in order], 0).T
    return np.ascontiguousarray(Z.astype(np.float32)), np.ascontiguousarray(Dd.astype(np.float32))


def host_consts(L):
    T = L + CTX
    bf = ml_dtypes.bfloat16
    c = {}
    c["c_ident_f"] = np.eye(128, dtype=np.float32)
    c["c_ident_b"] = np.eye(128).astype(bf)
    rows = L // GRID_W
    row = np.broadcast_to(np.arange(rows)[:, None], (rows, GRID_W)).reshape(-1).astype(np.float32)
    col = np.broadcast_to(np.arange(GRID_W)[None, :], (rows, GRID_W)).reshape(-1).astype(np.float32)
    half = HD // 2
    inv = (np.float32(ROPE_THETA) ** (-np.arange(0, half, 2, dtype=np.float32) / np.float32(half))).astype(np.float32)
    ang = np.concatenate([row[:, None] * inv, col[:, None] * inv], axis=-1).astype(np.float32)
    rope = np.zeros((T, 2, 32), np.float32)
    rope[:L, 0] = np.cos(ang)
    rope[:L, 1] = np.sin(ang)
    rope[L:, 0] = 1.0
    c["c_rope"] = rope
    jj = np.arange(64)
    a64 = 2.0 * np.pi * np.outer(jj, jj) / 64.0
    bd = np.zeros((128, 2, 128), np.float64)
    for g in range(2):
        bd[g * 64:(g + 1) * 64, 0, g * 64:(g + 1) * 64] = np.cos(a64)
        bd[g * 64:(g + 1) * 64, 1, g * 64:(g + 1) * 64] = -np.sin(a64)
    c["c_bdcs"] = bd.astype(bf)
    pp = np.arange(128)[:, None, None]
    rr = (np.arange(6) - 1)[None, :, None]
    ff = np.arange(512)[None, None, :]
    c["c_wmask"] = (np.abs(rr * 128 + pp - ff) <= 128).astype(np.float32).astype(bf)
    c["c_hz"], c["c_hdec"] = hyena_tables(L)
    for (N1, inv) in dft_plans(L):
        key = "%d_%s" % (N1, "i" if inv else "f")
        c["c_tabA_" + key], c["c_tw_" + key], c["c_tabB_" + key] = dft_tables(N1, inv)
    return c


def run(inputs, L=8192, depth=4, dbg=(), n_cores=8, trace=False, stop_after=None):
    prog = Prog(L, depth, dbg)
    nc = prog.build(stop_after=stop_after)
    consts = host_consts(L)
    in_maps = []
    for i in range(n_cores):
        m = dict(consts)
        m["x"] = np.ascontiguousarray(inputs["x"][i])
        m["ctx"] = np.ascontiguousarray(inputs["ctx"][i])
        m["c"] = np.ascontiguousarray(inputs["c"][i:i + 1])
        m["c_ctx"] = np.ascontiguousarray(inputs["c_ctx"]).reshape(1, D)
        for name in WEIGHT_SHAPES:
            m[name] = np.ascontiguousarray(inputs[name][:depth])
        in_maps.append(m)
    res = run_bass_kernel_spmd(nc, in_maps, core_ids=list(range(n_cores)), trace=trace)
    return res, prog


def kernel(**inputs):
    res, prog = run(inputs)
    return np.stack([r["out"] for r in res.results], axis=0)
```
